# Optimizing a Trainium2 kernel written in Bass

```python
import jax
import jax.numpy as jnp
from jax import lax
import numpy as np

D_MODEL = 1024
BATCH = 8
SEQ = 2048
DEPTH = 1
DEC_BATCH = 128
DEC_SEQ = 8
PAST_LEN = 16384
PAGE_SIZE = 128

N_MEM = 256
CONV_WIDTH = 3
D_CONV = D_MODEL
RWKV_HEAD = 64
RWKV_HEADS = D_MODEL // RWKV_HEAD
D_RWKV = RWKV_HEADS * RWKV_HEAD
DECAY_LORA = 64
ICL_LORA = 64
MEM_HEADS = 4
MEM_HEAD_DIM = D_MODEL // MEM_HEADS
D_MEM = MEM_HEADS * MEM_HEAD_DIM
N_BRANCH = 3
D_SHIFT = 3 * D_RWKV + DECAY_LORA + ICL_LORA
D_IN = 4 * D_CONV + D_SHIFT + D_RWKV + 2 * D_MEM + N_BRANCH * D_MODEL
NORM_EPS = 1e-6
GN_EPS = RWKV_HEAD * 1e-5

kernel_name = 'hybrid_shortconv_rwkv7_memxattn_step'


def _in_split_points():
    sizes = [D_CONV, D_CONV, D_CONV, D_CONV,
             D_SHIFT, D_RWKV,
             D_MEM, D_MEM,
             D_MODEL, D_MODEL, D_MODEL]
    pts = []
    acc = 0
    for s in sizes[:-1]:
        acc += s
        pts.append(acc)
    return pts


def _rmsnorm(x, g):
    xf = x.astype(jnp.float32)
    y = xf * lax.rsqrt(jnp.mean(xf * xf, axis=-1, keepdims=True) + NORM_EPS)
    return (y * g.astype(jnp.float32)).astype(x.dtype)


def _short_conv(u, conv_w, conv_b, conv_state):
    t = u.shape[1]
    ext = jnp.concatenate([conv_state.astype(u.dtype), u], axis=1)
    y = conv_b
    for j in range(CONV_WIDTH):
        y = y + ext[:, j:j + t] * conv_w[j]
    return y, ext[:, t:]


def _wkv_step(S, inp):
    r, w, k, v, kk, a = inp
    sa = jnp.einsum('bhvk,bhk->bhv', S, kk)
    S = (S * w[:, :, None, :] - sa[..., None] * (kk * a)[:, :, None, :]
         + v[..., None] * k[:, :, None, :])
    o = jnp.einsum('bhvk,bhk->bhv', S, r)
    return S, o


def _rwkv7(p, prev_row, wkv_state, mu, w0, w_up, a0, a_up, k_k, k_a, r_k, ln_g, ln_b):
    b, t, _ = p.shape
    pf = p.astype(jnp.float32)
    prev = jnp.concatenate([prev_row[:, None].astype(jnp.float32), pf[:, :-1]], axis=1)
    ps = pf + mu.astype(jnp.float32) * (prev - pf)
    r, k, v, wd, ad = jnp.split(ps, [D_RWKV, 2 * D_RWKV, 3 * D_RWKV, 3 * D_RWKV + DECAY_LORA], axis=-1)
    w_log = -jax.nn.softplus(-(w0 + jnp.tanh(wd) @ w_up)) - 0.5
    decay = jnp.exp(-jnp.exp(w_log))
    a = jax.nn.sigmoid(a0 + ad @ a_up)
    heads = lambda z: z.reshape(b, t, RWKV_HEADS, RWKV_HEAD)
    r, k, v, decay, a = heads(r), heads(k), heads(v), heads(decay), heads(a)
    kk = k * k_k.reshape(RWKV_HEADS, RWKV_HEAD)
    kk = kk * lax.rsqrt(jnp.sum(kk * kk, axis=-1, keepdims=True) + 1e-12)
    k = k * (1.0 + (a - 1.0) * k_a.reshape(RWKV_HEADS, RWKV_HEAD))
    seq = tuple(jnp.moveaxis(z, 1, 0) for z in (r, decay, k, v, kk, a))
    s_final, o = lax.scan(_wkv_step, wkv_state.astype(jnp.float32), seq)
    o = jnp.moveaxis(o, 0, 1)
    mean = jnp.mean(o, axis=-1, keepdims=True)
    var = jnp.mean(jnp.square(o - mean), axis=-1, keepdims=True)
    o = ((o - mean) * lax.rsqrt(var + GN_EPS) * ln_g.reshape(RWKV_HEADS, RWKV_HEAD)
         + ln_b.reshape(RWKV_HEADS, RWKV_HEAD))
    o = o + jnp.sum(r * k * r_k, axis=-1, keepdims=True) * v
    return o.reshape(b, t, D_RWKV), p[:, -1], s_final


def _mem_kv(mem, norm_g, w_kv):
    b, m, _ = mem.shape
    kv = _rmsnorm(mem, norm_g) @ w_kv
    mk, mv = jnp.split(kv, 2, axis=-1)
    return (mk.reshape(b, m, MEM_HEADS, MEM_HEAD_DIM), mv.reshape(b, m, MEM_HEADS, MEM_HEAD_DIM))


def _mem_attention(q, mem_k, mem_v):
    b, t, _ = q.shape
    qh = q.reshape(b, t, MEM_HEADS, MEM_HEAD_DIM).astype(jnp.float32)
    s = jnp.einsum('bthd,bmhd->bhtm', qh, mem_k.astype(jnp.float32)) * (MEM_HEAD_DIM ** -0.5)
    pr = jax.nn.softmax(s, axis=-1)
    o = jnp.einsum('bhtm,bmhd->bthd', pr, mem_v.astype(jnp.float32))
    return o.reshape(b, t, D_MEM)


def _layer(x, conv_state, shift_state, wkv_state, mem_k, mem_v,
           norm_g, w_in, conv_w, conv_b, mu, w0, w_up, a0, a_up, k_k, k_a, r_k,
           ln_g, ln_b, w_branch, w_out):
    xn = _rmsnorm(x, norm_g)
    proj = xn @ w_in
    (h, b_gate, c_gate, g_conv, p_shift, g_rwkv, q_mem, g_mem,
     m_conv, m_rwkv, m_mem) = jnp.split(proj, _in_split_points(), axis=-1)
    y_conv, conv_new = _short_conv(c_gate * h, conv_w, conv_b, conv_state)
    y_conv = b_gate * y_conv * jax.nn.silu(g_conv)
    y_rwkv, shift_new, wkv_new = _rwkv7(p_shift, shift_state, wkv_state, mu, w0, w_up, a0, a_up,
                                        k_k, k_a, r_k, ln_g, ln_b)
    y_rwkv = y_rwkv.astype(x.dtype) * jax.nn.silu(g_rwkv)
    y_mem = _mem_attention(q_mem, mem_k, mem_v).astype(x.dtype) * jax.nn.silu(g_mem)
    merged = (jax.nn.sigmoid(m_conv) * (y_conv @ w_branch[0])
              + jax.nn.sigmoid(m_rwkv) * (y_rwkv @ w_branch[1])
              + jax.nn.sigmoid(m_mem) * (y_mem @ w_branch[2]))
    return x + merged @ w_out, conv_new, shift_new, wkv_new


def setup_inputs(seed: int = 0) -> dict:
    key = jax.random.key(seed)
    ks = jax.random.split(key, 32)
    nrm = lambda k, shape, s: jax.random.normal(k, shape, jnp.float32) * s
    H, N = RWKV_HEADS, RWKV_HEAD
    return {
        'x_prompt': nrm(ks[0], (BATCH, SEQ, D_MODEL), 1.0),
        'x_sample': nrm(ks[1], (DEC_BATCH, DEC_SEQ, D_MODEL), 1.0),
        'mem_prompt': nrm(ks[2], (BATCH, N_MEM, D_MODEL), 1.0),
        'cache_mem_k': nrm(ks[3], (DEPTH, DEC_BATCH, N_MEM, MEM_HEADS, MEM_HEAD_DIM), 1.0),
        'cache_mem_v': nrm(ks[4], (DEPTH, DEC_BATCH, N_MEM, MEM_HEADS, MEM_HEAD_DIM), 1.0),
        'state_conv': nrm(ks[5], (DEPTH, DEC_BATCH, CONV_WIDTH - 1, D_CONV), 1.0),
        'state_shift': nrm(ks[6], (DEPTH, DEC_BATCH, D_SHIFT), 1.0),
        'state_wkv': nrm(ks[7], (DEPTH, DEC_BATCH, H, N, N), 0.3),
        'norm_in': 1.0 + nrm(ks[8], (DEPTH, D_MODEL), 0.02),
        'w_in': nrm(ks[9], (DEPTH, D_MODEL, D_IN), D_MODEL ** -0.5),
        'conv_w': nrm(ks[10], (DEPTH, CONV_WIDTH, D_CONV), CONV_WIDTH ** -0.5),
        'conv_b': nrm(ks[11], (DEPTH, D_CONV), 0.02),
        'shift_mu': jax.random.uniform(ks[12], (DEPTH, D_SHIFT), jnp.float32),
        'decay_w0': jax.random.uniform(ks[13], (DEPTH, D_RWKV), jnp.float32, -4.0, 1.0),
        'decay_up': nrm(ks[14], (DEPTH, DECAY_LORA, D_RWKV), 0.1),
        'icl_a0': nrm(ks[15], (DEPTH, D_RWKV), 0.1),
        'icl_up': nrm(ks[16], (DEPTH, ICL_LORA, D_RWKV), 0.5 * ICL_LORA ** -0.5),
        'k_k': 0.85 + nrm(ks[17], (DEPTH, D_RWKV), 0.05),
        'k_a': 1.0 + nrm(ks[18], (DEPTH, D_RWKV), 0.05),
        'r_k': nrm(ks[19], (DEPTH, H, N), 0.1),
        'ln_x_g': 1.0 + nrm(ks[20], (DEPTH, D_RWKV), 0.02),
        'ln_x_b': nrm(ks[21], (DEPTH, D_RWKV), 0.02),
        'norm_mem': 1.0 + nrm(ks[22], (DEPTH, D_MODEL), 0.02),
        'w_mem_kv': nrm(ks[23], (DEPTH, D_MODEL, 2 * D_MEM), D_MODEL ** -0.5),
        'w_branch': nrm(ks[24], (DEPTH, N_BRANCH, D_CONV, D_MODEL), D_CONV ** -0.5),
        'w_out': nrm(ks[25], (DEPTH, D_MODEL, D_MODEL), D_MODEL ** -0.5),
        'norm_final': 1.0 + nrm(ks[26], (D_MODEL,), 0.02),
    }


def reference(x_prompt, x_sample, mem_prompt, cache_mem_k, cache_mem_v, state_conv, state_shift,
              state_wkv, norm_in, w_in, conv_w, conv_b, shift_mu, decay_w0, decay_up, icl_a0,
              icl_up, k_k, k_a, r_k, ln_x_g, ln_x_b, norm_mem, w_mem_kv, w_branch, w_out,
              norm_final):
    bp = x_prompt.shape[0]
    hp, hs = x_prompt, x_sample
    mk_p, mv_p, conv_p, shift_p, wkv_p = [], [], [], [], []
    conv_s, shift_s, wkv_s = [], [], []
    for l in range(DEPTH):
        lw = (norm_in[l], w_in[l], conv_w[l], conv_b[l], shift_mu[l], decay_w0[l], decay_up[l],
              icl_a0[l], icl_up[l], k_k[l], k_a[l], r_k[l], ln_x_g[l], ln_x_b[l],
              w_branch[l], w_out[l])
        mk, mv = _mem_kv(mem_prompt, norm_mem[l], w_mem_kv[l])
        zc = jnp.zeros((bp, CONV_WIDTH - 1, D_CONV), x_prompt.dtype)
        zs = jnp.zeros((bp, D_SHIFT), x_prompt.dtype)
        zw = jnp.zeros((bp, RWKV_HEADS, RWKV_HEAD, RWKV_HEAD), jnp.float32)
        hp, cp, sp, wp = _layer(hp, zc, zs, zw, mk, mv, *lw)
        hs, cs, ss, ws = _layer(hs, state_conv[l], state_shift[l], state_wkv[l],
                                cache_mem_k[l], cache_mem_v[l], *lw)
        mk_p.append(mk); mv_p.append(mv)
        conv_p.append(cp); shift_p.append(sp); wkv_p.append(wp)
        conv_s.append(cs); shift_s.append(ss); wkv_s.append(ws)
    y_prompt = _rmsnorm(hp, norm_final)
    y_sample = _rmsnorm(hs, norm_final)
    return (y_prompt, y_sample, jnp.stack(mk_p), jnp.stack(mv_p), jnp.stack(conv_p),
            jnp.stack(shift_p), jnp.stack(wkv_p), jnp.stack(conv_s), jnp.stack(shift_s),
            jnp.stack(wkv_s))
```

```python
import os
import numpy as np
from contextlib import ExitStack
import concourse.bass as bass
import concourse.mybir as mybir
from concourse.bass_utils import run_bass_kernel_spmd

F32 = mybir.dt.float32
BF16 = mybir.dt.bfloat16
AF = mybir.ActivationFunctionType
ALU = mybir.AluOpType
AX = mybir.AxisListType

NCORES = 8
D = 1024
T = 2048
NTOK = 2176
NB = 16
NT = 8
TBLK = [(0, 512), (512, 512), (1024, 512), (1536, 512), (2048, 128)]
EPS = 1e-6
GN_EPS = 64 * 1e-5

O_H, O_B, O_C, O_GC, O_R, O_K, O_V, O_WD, O_AD, O_GR, O_Q, O_GM, O_MC, O_MR, O_MM = (
    0, 1024, 2048, 3072, 4096, 5120, 6144, 7168, 7232, 7296, 8320, 9344, 10368, 11392, 12416)
G_CONV = 0
G_RWKV = 8
G_LORA = 16
G_MEM = 17
G_GATE = 21
G_BR = 27
G_OUT = 33
G_KV = 35
NGROUPS = 39

V_NORM_IN, V_CW0, V_CW1, V_CW2, V_CB, V_MU, V_W0, V_A0, V_KK, V_KA, V_RK, V_LNG, V_LNB, V_NMEM = (
    0, 8, 16, 24, 32, 40, 65, 73, 81, 89, 97, 105, 113, 121)
NV = 129


class _Op:
    __slots__ = ("eng", "fn", "deps", "raw", "dma", "stream", "signal", "sig_idx", "idx", "force", "cost", "lat", "tset")


class Prog:
    def __init__(self, nc):
        self.nc = nc
        self.ops = []
        self.lastw = {}
        self.readers = {}
        self.engs = {"pe": nc.tensor, "act": nc.scalar, "dve": nc.vector, "pool": nc.gpsimd, "sp": nc.sync}

    DEFC = {"pe": 0.12, "act": 0.45, "dve": 0.40, "pool": 0.60, "sp": 0.10}

    def add(self, eng, fn, r=(), w=(), dma=False, stream=None, force=False, cost=None, lat=None):
        op = _Op()
        op.force = force
        if cost is None and not dma and hasattr(fn, "free"):
            a, b = {"pe": (0.07, 0.00042), "act": (0.22, 0.00085), "dve": (0.12, 0.0011), "pool": (0.15, 0.0020), "sp": (0.1, 0.0)}[eng]
            cost = a + b * fn.free
        op.cost = cost if cost is not None else ((1.8 if eng == "pool" else 0.15) if dma else self.DEFC[eng])
        op.lat = lat if lat is not None else (4.0 if dma else 0.0)
        op.tset = getattr(fn, "tset", 0)
        op.eng, op.fn, op.dma, op.stream = eng, fn, dma, stream
        op.signal = False
        op.idx = len(self.ops)
        deps, raw = set(), set()
        psr = [k for k in r if k.startswith("ps")]
        for k in psr:
            if k in self.lastw:
                raw.add(self.lastw[k])
        r = [k for k in r if not k.startswith("ps")]
        w = list(w) + psr
        for k in r:
            if k in self.lastw:
                deps.add(self.lastw[k]); raw.add(self.lastw[k])
        for k in w:
            if k in self.lastw:
                deps.add(self.lastw[k]); raw.add(self.lastw[k])
            for x in self.readers.get(k, ()):
                deps.add(x)
        deps.discard(op.idx)
        op.deps, op.raw = deps, raw
        for k in r:
            self.readers.setdefault(k, []).append(op.idx)
        for k in w:
            self.lastw[k] = op.idx
            self.readers[k] = []
        self.ops.append(op)
        return op.idx

    def act(self, fn, r=(), w=()): return self.add("act", fn, r, w)
    def dve(self, fn, r=(), w=()): return self.add("dve", fn, r, w)
    def pool(self, fn, r=(), w=()): return self.add("pool", fn, r, w)
    def pe(self, fn, r=(), w=(), force=False): return self.add("pe", fn, r, w, force=force)

    def dma(self, eng, out, in_, r=(), w=(), stream=None, lat=None):
        assert stream is not None
        return self.add(eng, lambda e: e.dma_start(out=out, in_=in_), r, w, dma=True, stream=stream, lat=lat)

    def schedule(self, window=800):
        import heapq
        ops = self.ops; n = len(ops)
        if os.environ.get("KNOSCHED"):
            return list(range(n))
        succ = [[] for _ in ops]; indeg = [0] * n
        for op in ops:
            indeg[op.idx] = len(op.deps)
            for d in op.deps:
                succ[d].append(op.idx)
        start = [0.0] * n; finish = [0.0] * n; rtime = [0.0] * n
        blev = [0.0] * n
        for i in range(n - 1, -1, -1):
            op = ops[i]
            m = 0.0
            for sidx in succ[i]:
                v = blev[sidx] + (0.55 if ops[sidx].eng != op.eng else 0.12)
                if v > m:
                    m = v
            blev[i] = op.cost + op.lat + m
        PRI = os.environ.get("KPRI", "cp")
        ready = {e: [] for e in self.engs}
        efree = {e: 0.0 for e in self.engs}
        done = [False] * n
        for op in ops:
            if indeg[op.idx] == 0:
                ready[op.eng].append(op.idx)
        order = []; base = 0
        act_set = [0]
        TSW = 1.3
        while len(order) < n:
            while base < n and done[base]:
                base += 1
            lim = base + window
            best = None
            for e, lst in ready.items():
                cand = None; cand_key = None
                for i in lst:
                    if i >= lim:
                        continue
                    st = max(efree[e], rtime[i])
                    if e == "act" and ops[i].tset and ops[i].tset != act_set[0]:
                        st = max(efree[e] + TSW, rtime[i])
                    pr = i if PRI == "idx" else -blev[i]
                    key = (st, pr) if st > efree[e] else (efree[e], pr)
                    if cand is None or key < cand_key:
                        cand, cand_key = i, key
                if cand is not None and (best is None or cand_key < best[0]):
                    best = (cand_key, cand, e)
            assert best is not None, "scheduler stuck"
            (st, _pr), i, e = best
            op = ops[i]
            ready[e].remove(i)
            if e == "act" and op.tset:
                if op.tset != act_set[0]:
                    self.n_tsw = getattr(self, "n_tsw", 0) + 1
                act_set[0] = op.tset
            start[i] = st; efree[e] = st + op.cost; finish[i] = st + op.cost + op.lat
            done[i] = True; order.append(i)
            for sidx in succ[i]:
                sop = ops[sidx]
                if sop.eng == op.eng and not op.dma and not sop.dma and op.eng == "pe" and not sop.force:
                    t = start[i] + 0.01
                elif sop.eng == op.eng and not op.dma:
                    t = finish[i] + 0.12
                else:
                    t = finish[i] + 0.55
                if t > rtime[sidx]:
                    rtime[sidx] = t
                indeg[sidx] -= 1
                if indeg[sidx] == 0:
                    ready[sop.eng].append(sidx)
        self.est_us = max(finish)
        if os.environ.get("KDUMP"):
            a, b = [int(x) for x in os.environ["KDUMP"].split(":")]
            for i in range(a, b):
                op = ops[i]
                crit = max(op.deps, key=lambda d: finish[d]) if op.deps else -1
                print(f"   op {i} {op.eng:4s} start {start[i]:8.2f} fin {finish[i]:8.2f} rtime {rtime[i]:8.2f} critdep {crit} ({ops[crit].eng if crit >= 0 else ''} fin {finish[crit] if crit >= 0 else 0:8.2f}) dma={op.dma}")
        if os.environ.get("KSTAT"):
            tot = {e: 0.0 for e in self.engs}
            for op in ops:
                tot[op.eng] += op.cost
            print("model engine busy us:", {k: round(v) for k, v in tot.items()})
            for name, idx in getattr(self, "marks", []):
                if idx < n:
                    print(f"  mark {name:24s} op {idx:6d} start {start[idx]:9.1f}")
        return order

    def emit(self, es):
        nc = self.nc
        ops = self.ops
        need = [[] for _ in ops]
        for op in ops:
            for d in sorted(op.deps):
                dep = ops[d]
                if dep.dma:
                    need[op.idx].append(d)
                elif op.dma or dep.eng != op.eng:
                    dep.signal = True
                    need[op.idx].append(d)
                elif op.eng != "pe" or op.force:
                    dep.signal = True
                    need[op.idx].append(d)
        sems = {}
        for e in ("pe", "act", "dve", "pool", "sp"):
            sems[e] = es.enter_context(nc.semaphore("s_" + e))
        streams = {}
        for op in ops:
            if op.dma and op.stream not in streams:
                streams[op.stream] = es.enter_context(nc.semaphore("d_" + op.stream))
        cnt = {k: 0 for k in sems}
        scnt = {k: 0 for k in streams}
        waited = {e: {} for e in self.engs}
        order = self.schedule()
        for oi in order:
            op = ops[oi]
            wl = {}
            for d in need[op.idx]:
                dep = ops[d]
                if dep.dma:
                    key, val = ("d", dep.stream), 16 * dep.sig_idx
                else:
                    key, val = ("e", dep.eng), dep.sig_idx
                if wl.get(key, 0) < val:
                    wl[key] = val
            eng = self.engs[op.eng]
            wd = waited[op.eng]
            pend = []
            for key, val in wl.items():
                if wd.get(key, 0) >= val:
                    continue
                wd[key] = val
                sem = streams[key[1]] if key[0] == "d" else sems[key[1]]
                pend.append((sem, val))
            attach = None
            if pend and not op.dma and not os.environ.get("KNOATTACH"):
                attach = pend.pop()
            for sem, val in pend:
                eng.wait_ge(sem, val)
            ins = op.fn(eng)
            if attach is not None:
                ins._wait_ge(attach[0], attach[1])
            self.n_wait = getattr(self, "n_wait", 0) + len(pend)
            self.n_att = getattr(self, "n_att", 0) + (1 if attach else 0)
            if op.dma:
                scnt[op.stream] += 1
                op.sig_idx = scnt[op.stream]
                ins.then_inc(streams[op.stream], 16)
            elif op.signal:
                cnt[op.eng] += 1
                op.sig_idx = cnt[op.eng]
                ins.then_inc(sems[op.eng], 1)
            else:
                op.sig_idx = cnt[op.eng] + 1
        return len(ops)


def MM(out, lhsT, rhs, start, stop):
    return _tag(lambda e: e.matmul(out, lhsT, rhs, start=start, stop=stop), out)

def TR(out, in_, ident):
    return _tag(lambda e: e.transpose(out, in_, ident), out)

def _fs(ap):
    n = 1
    for d in ap.shape[1:]:
        n *= int(d)
    return n

def _tag(fn, out):
    fn.free = _fs(out)
    return fn

def ACTF(out, in_, func, bias=None, scale=None, accum=None):
    kw = {}
    if bias is not None: kw["bias"] = bias
    if scale is not None: kw["scale"] = scale
    if accum is not None: kw["accum_out"] = accum
    fn = _tag(lambda e: e.activation(out=out, in_=in_, func=func, **kw), out)
    fn.tset = {AF.Sigmoid: 1, AF.Silu: 2, AF.Exp: 3, AF.Ln: 3}.get(func, 0)
    return fn

def TS(out, in0, s1, s2, op0, op1=None):
    if op1 is None:
        return _tag(lambda e: e.tensor_scalar(out=out, in0=in0, scalar1=s1, scalar2=None, op0=op0), out)
    return _tag(lambda e: e.tensor_scalar(out=out, in0=in0, scalar1=s1, scalar2=s2, op0=op0, op1=op1), out)

def TT(out, in0, in1, op):
    return _tag(lambda e: e.tensor_tensor(out=out, in0=in0, in1=in1, op=op), out)

def STT(out, in0, scalar, in1, op0, op1):
    fn = _tag(lambda e: e.scalar_tensor_tensor(out=out, in0=in0, scalar=scalar, in1=in1, op0=op0, op1=op1), out)
    fn.free *= 1.6
    return fn

def CP(out, in_):
    return _tag(lambda e: e.tensor_copy(out=out, in_=in_), out)

def RCP(out, in_):
    return _tag(lambda e: e.reciprocal(out=out, in_=in_), out)

def MSET(out, v):
    return lambda e: e.memset(out, v)

def CPA(out, in_):
    return _tag(lambda e: e.copy(out=out, in_=in_), out)

def MULA(out, in_, m):
    return _tag(lambda e: e.mul(out=out, in_=in_, mul=m), out)


def _tile_w(w, cols):
    out = np.zeros((128, 8, 512), np.float32)
    sub = w[:, cols]
    out[:, :, :sub.shape[1]] = sub.reshape(8, 128, -1).transpose(1, 0, 2)
    return out


def _build_wall(w_in, w_branch, w_out, w_mem_kv):
    ar = np.arange
    groups = []
    for j in range(8):
        groups.append(_tile_w(w_in, np.concatenate([o + j * 128 + ar(128) for o in (O_H, O_B, O_C, O_GC)])))
    for j in range(8):
        groups.append(_tile_w(w_in, np.concatenate([o + j * 128 + ar(128) for o in (O_R, O_K, O_V, O_GR)])))
    groups.append(_tile_w(w_in, O_WD + ar(128)))
    for h in range(4):
        groups.append(_tile_w(w_in, np.concatenate([O_Q + h * 256 + ar(256), O_GM + h * 256 + ar(256)])))
    for o in (O_MC, O_MR, O_MM):
        for hf in range(2):
            groups.append(_tile_w(w_in, o + hf * 512 + ar(512)))
    for i in range(3):
        for hf in range(2):
            groups.append(_tile_w(w_branch[i], hf * 512 + ar(512)))
    for hf in range(2):
        groups.append(_tile_w(w_out, hf * 512 + ar(512)))
    for q in range(4):
        groups.append(_tile_w(w_mem_kv, q * 512 + ar(512)))
    assert len(groups) == NGROUPS
    return np.ascontiguousarray(np.stack(groups))


def _fm(v):
    return np.ascontiguousarray(v.reshape(-1, 128).T)


def _host_inputs(inp):
    f = lambda a: np.ascontiguousarray(np.asarray(a, dtype=np.float32))
    x_prompt, x_sample, mem_prompt = f(inp["x_prompt"]), f(inp["x_sample"]), f(inp["mem_prompt"])
    ck, cv = f(inp["cache_mem_k"])[0], f(inp["cache_mem_v"])[0]
    sconv, sshift, swkv = f(inp["state_conv"])[0], f(inp["state_shift"])[0], f(inp["state_wkv"])[0]
    wall = _build_wall(f(inp["w_in"])[0], f(inp["w_branch"])[0], f(inp["w_out"])[0], f(inp["w_mem_kv"])[0])
    vecs = np.zeros((128, NV), np.float32)
    cw = f(inp["conv_w"])[0]
    for off, v in ((V_NORM_IN, f(inp["norm_in"])[0]), (V_CW0, cw[0]), (V_CW1, cw[1]), (V_CW2, cw[2]),
                   (V_CB, f(inp["conv_b"])[0]), (V_MU, f(inp["shift_mu"])[0]), (V_W0, f(inp["decay_w0"])[0]),
                   (V_A0, f(inp["icl_a0"])[0]), (V_KK, f(inp["k_k"])[0]), (V_KA, f(inp["k_a"])[0]),
                   (V_RK, f(inp["r_k"])[0].reshape(-1)), (V_LNG, f(inp["ln_x_g"])[0]), (V_LNB, f(inp["ln_x_b"])[0]),
                   (V_NMEM, f(inp["norm_mem"])[0])):
        m = _fm(v)
        vecs[:, off:off + m.shape[1]] = m
    gfin = np.ascontiguousarray(np.broadcast_to(f(inp["norm_final"])[None, :], (128, D)))
    lora = np.ascontiguousarray(np.concatenate([f(inp["decay_up"])[0], f(inp["icl_up"])[0]], axis=0))
    consts = _make_consts()
    maps = []
    for c in range(NCORES):
        xall = np.concatenate([x_prompt[c].reshape(16, 128, D),
                               x_sample[NB * c:NB * (c + 1)].reshape(1, 128, D)], axis=0)
        maps.append({
            "xall": np.ascontiguousarray(xall),
            "mem": np.ascontiguousarray(mem_prompt[c].reshape(2, 128, D)),
            "ck": np.ascontiguousarray(ck[NB * c:NB * (c + 1)].reshape(NB, 256, D)),
            "cv": np.ascontiguousarray(cv[NB * c:NB * (c + 1)].reshape(NB, 256, D)),
            "sconv": np.ascontiguousarray(sconv[NB * c:NB * (c + 1)].reshape(NB * 2, D)),
            "sshift": np.ascontiguousarray(sshift[NB * c:NB * (c + 1)]),
            "swkv": np.ascontiguousarray(swkv[NB * c:NB * (c + 1)]),
            "wall": wall, "vecs": vecs, "gfin": gfin, "lora": lora, "consts": consts,
        })
    return maps


C_IDENT, C_TRIL_S, C_TRIL_I, C_BD_S, C_BD_I, C_BLK, C_RESET, C_LT, C_BD_LT = 0, 128, 256, 384, 512, 640, 768, 896, 1024
NCONST = 1152


def _make_consts():
    c = np.zeros((128, NCONST), np.float32)
    i = np.arange(128)
    c[:, C_IDENT:C_IDENT + 128] = np.eye(128)
    c[:, C_TRIL_S:C_TRIL_S + 128] = (i[:, None] < i[None, :])
    c[:, C_TRIL_I:C_TRIL_I + 128] = (i[:, None] <= i[None, :])
    same = (i[:, None] // NT) == (i[None, :] // NT)
    c[:, C_BD_S:C_BD_S + 128] = (i[:, None] < i[None, :]) & same
    c[:, C_BD_I:C_BD_I + 128] = (i[:, None] <= i[None, :]) & same
    c[:, C_BLK:C_BLK + 128] = (i[:, None] // 64) == (i[None, :] // 64)
    c[:, C_RESET:C_RESET + 16] = (i[:, None] // NT) == np.arange(16)[None, :]
    c[:, C_LT:C_LT + 128] = (i[None, :] < i[:, None])
    c[:, C_BD_LT:C_BD_LT + 128] = (i[None, :] < i[:, None]) & same
    return c


NTP = 1152
PASS_TILES = [list(range(8)) + [16], list(range(8, 16))]
PASS_TBLK = [[(0, 512), (512, 512), (1024, 128)], [(0, 512), (512, 512)]]


def build_nc(stages=("rwkv", "conv", "mem")):
    nc = bass.Bass("TRN2", target_bir_lowering=False)
    din = lambda name, shape: nc.dram_tensor(name, list(shape), F32, kind="ExternalInput").ap()
    dout = lambda name, shape: nc.dram_tensor(name, list(shape), F32, kind="ExternalOutput").ap()
    xall_d = din("xall", (17, 128, D)); mem_d = din("mem", (2, 128, D))
    ck_d = din("ck", (NB, 256, D)); cv_d = din("cv", (NB, 256, D))
    sconv_d = din("sconv", (NB * 2, D)); sshift_d = din("sshift", (NB, 3200)); swkv_d = din("swkv", (NB, 16, 64, 64))
    wall_d = din("wall", (NGROUPS, 128, 8, 512)); vecs_d = din("vecs", (128, NV)); gfin_d = din("gfin", (128, D))
    lora_d = din("lora", (128, 1024)); consts_d = din("consts", (128, NCONST))
    y_d = dout("y", (17, 128, D)); mk_d = dout("mk", (256, D)); mv_d = dout("mv", (256, D))
    convp_d = dout("convp", (2, D)); shiftp_d = dout("shiftp", (1, 3200)); wkvp_d = dout("wkvp", (16, 64, 64))
    convs_d = dout("convs", (NB * 2, D)); shifts_d = dout("shifts", (NB, 3200)); wkvs_d = dout("wkvs", (NB, 16, 64, 64))

    with ExitStack() as es:
        sb = lambda name, shape, dt=F32: es.enter_context(nc.sbuf_tensor("s_" + name, list(shape), dt))
        P = Prog(nc)
        vecs = sb("vecs", (128, NV)); consts = sb("consts", (128, NCONST)); gfin = sb("gfin", (128, D))
        identb = sb("identb", (128, 128), BF16); onesb = sb("onesb", (128, 128), BF16)
        xt = sb("xt", (128, D)); xnb = sb("xnb", (128, D), BF16)
        xt2 = sb("xt2", (128, D)); xnb2 = sb("xnb2", (128, D), BF16)
        ssb = sb("ssb", (128, 16))
        xnT = sb("xnT", (128, 8, NTP), BF16); yT = sb("yT", (128, 8, NTP), BF16)
        acc = sb("acc", (128, 8, NTP))
        NS = 4
        ring = [sb(f"ring{s}", (128, 8, 512), BF16) for s in range(NS)]
        NTMP = 12
        tmps = [sb(f"tmp{i}", (128, 512)) for i in range(NTMP)]
        memnT = sb("memnT", (128, 8, 256), BF16); KT = sb("KT", (128, 8, 256), BF16); Vp = sb("Vp", (128, 2, D), BF16)
        U = sb("U", (128, 1026)); us = sb("us", (128, NB, 10)); ucarry = sb("ucarry", (128, 8, 2)); uls = sb("uls", (128, 8, NB, 2))
        scT = sb("scT", (128, 8, NB * 2))
        qTs = sb("qTs", (128, 8, 128), BF16); sgms = sb("sgms", (128, 8, 128))
        Kb = sb("Kb", (128, 2, D), BF16); Vb = sb("Vb", (128, 2, D), BF16); KTb = sb("KTb", (128, 8, 256), BF16)
        ps = [es.enter_context(nc.psum_tensor(f"ps{i}", [128, 512], F32)) for i in range(8)]
        psk = [f"ps{i}" for i in range(8)]
        ident = consts[:, C_IDENT:C_IDENT + 128]
        psT = ps[7][:, :].bitcast(BF16)

        tctr = [0]
        tmp_excl = set()
        def tmp():
            i = tctr[0] % NTMP; tctr[0] += 1
            while i in tmp_excl:
                i = tctr[0] % NTMP; tctr[0] += 1
            return tmps[i], f"tmp{i}"

        vcol = lambda off, j: vecs[:, off + j:off + j + 1]

        P.dma("sp", vecs[:], vecs_d[:, :], w=["vecs"], stream="c_vecs")
        P.dma("sp", consts[:], consts_d[:, :], w=["consts"], stream="c_consts")
        P.dma("sp", gfin[:], gfin_d[:, :], w=["gfin"], stream="c_gfin")
        P.dve(CP(identb[:], ident), r=["consts"], w=["identb"])
        P.dve(MSET(onesb[:], 1.0), w=["onesb"])
        P.dve(MSET(ucarry[:], 0.0), w=["ucarry"])

        sched = [G_KV + q for q in range(4)]
        for p_ in range(2):
            if "rwkv" in stages:
                sched += [G_LORA] + [G_RWKV + j for j in range(8)] + [G_GATE + 2, G_BR + 2, G_GATE + 3, G_BR + 3]
            if "conv" in stages:
                sched += [G_CONV + j for j in range(8)] + [G_GATE + 0, G_BR + 0, G_GATE + 1, G_BR + 1]
            if "mem" in stages:
                sched += [G_MEM + h for h in range(4)] + [G_GATE + 4, G_BR + 4, G_GATE + 5, G_BR + 5]
            if 'noout' not in stages:
                sched += [G_OUT, G_OUT + 1]
        wst = {"i": 0, "loaded": 0}
        def w_prefetch(upto):
            while wst["loaded"] < min(upto, len(sched)):
                k = wst["loaded"]; s = k % NS
                P.dma("pool", ring[s][:], wall_d[sched[k]], w=[f"ring{s}"], stream=f"ring{s}", lat=14.0)
                wst["loaded"] += 1
        def w_acquire(g):
            if not hasattr(P, "marks"):
                P.marks = []
            P.marks.append((f"grp{g}", len(P.ops)))
            i = wst["i"]
            if os.environ.get('RCUT'):
                sched[i] = g
            assert sched[i] == g, (i, sched[i], g)
            w_prefetch(i + NS - 1)
            wst["i"] += 1
            return i % NS

        def mm8(bank, bkey, slot, c0, rhs_of_kc, rkeys):
            for kc in range(8):
                P.add("pe", MM(bank, ring[slot][:, kc, c0:c0 + 128], rhs_of_kc(kc), kc == 0, kc == 7),
                      r=[f"ring{slot}"] + rkeys, w=[bkey], cost=0.06 + 0.00041 * int(bank.shape[-1]))

        nctr = [0]
        def norm_T(src, voff, dst3, dkeys):
            q = nctr[0] % 2; nctr[0] += 1
            xt_, kx = (xt, "xt") if q == 0 else (xt2, "xt2")
            xn_, kn = (xnb, "xnb") if q == 0 else (xnb2, "xnb2")
            pT_ = ps[7 - q][:, :].bitcast(BF16); kp = psk[7 - q]
            o = 8 * q
            P.dma("sp", xt_[:], src, w=[kx], stream=kx)
            P.act(ACTF(xn_[:], xt_[:], AF.Square, accum=ssb[:, o:o + 1]), r=[kx], w=[kn, f"ss{o}"])
            P.act(ACTF(ssb[:, o + 1:o + 2], ssb[:, o:o + 1], AF.Ln, bias=EPS, scale=1.0 / D), r=[f"ss{o}"], w=[f"ss{o + 1}"])
            P.act(ACTF(ssb[:, o + 2:o + 3], ssb[:, o + 1:o + 2], AF.Exp, scale=-0.5), r=[f"ss{o + 1}"], w=[f"ss{o + 2}"])
            P.dve(TS(xn_[:], xt_[:], ssb[:, o + 2:o + 3], None, ALU.mult), r=[kx, f"ss{o + 2}"], w=[kn])
            for c in range(8):
                P.pe(TR(pT_[:, c * 128:(c + 1) * 128], xn_[:, c * 128:(c + 1) * 128], identb[:]),
                     r=[kn, "identb"], w=[kp])
            for c in range(8):
                if c % 2 == 0:
                    P.dve(TS(dst3[:, c, :], pT_[:, c * 128:(c + 1) * 128], vcol(voff, c), None, ALU.mult),
                          r=[kp, "vecs"], w=dkeys)
                else:
                    P.act(MULA(dst3[:, c, :], pT_[:, c * 128:(c + 1) * 128], vcol(voff, c)), r=[kp, "vecs"], w=dkeys)

        LVL = int(os.environ.get('KCUT', '99'))
        for i in range(2 if LVL >= 1 else 0):
            norm_T(mem_d[i], V_NMEM, memnT[:, :, i * 128:(i + 1) * 128], ["memnT"])

        t0, k0 = tmp()
        P.dma("sp", t0[0:32, :], sconv_d[:, 0:512], w=[k0], stream="c_sc0")
        t1, k1 = tmp()
        P.dma("sp", t1[0:32, :], sconv_d[:, 512:1024], w=[k1], stream="c_sc1")
        for j in range(8 if LVL >= 2 else 0):
            src_t = t0 if j < 4 else t1
            P.pe(TR(ps[6][:, j * 32:(j + 1) * 32], src_t[0:32, (j % 4) * 128:(j % 4 + 1) * 128], ident[0:32, 0:32]),
                 r=[k0, k1, "consts"], w=[psk[6]])
        if LVL >= 2:
            P.act(CPA(scT[:].rearrange("p j s -> p (j s)"), ps[6][:, 0:256]), r=[psk[6]], w=["scT"])

        for q in range(4 if LVL >= 3 else 0):
            s = w_acquire(G_KV + q)
            KSUB = int(os.environ.get('KSUB', '9'))
            if KSUB < 1:
                P.pe(MM(ps[0][:, 0:128], ring[s][:, 0, 0:128], memnT[:, 0, 0:128], True, True), r=[f'ring{s}', 'memnT'], w=[psk[0]])
                continue
            if q < 2:
                for ct in range(4):
                    bank, bk = ps[ct % 2], psk[ct % 2]
                    mm8(bank[:, 0:256], bk, s, ct * 128, lambda kc: memnT[:, kc, :], ["memnT"])
                    P.act(CPA(KT[:, q * 4 + ct, :], bank[:, 0:256]), r=[bk], w=["KT"])
            for mt in range(2 if KSUB >= 2 else 0):
                bank, bk = ps[2 + mt], psk[2 + mt]
                for kc in range(8):
                    P.pe(MM(bank[:, :], memnT[:, kc, mt * 128:(mt + 1) * 128], ring[s][:, kc, :], kc == 0, kc == 7),
                         r=[f"ring{s}", "memnT"], w=[bk])
                t, tk = tmp()
                P.dve(CP(t[:], bank[:, :]), r=[bk], w=[tk])
                if q >= 2:
                    P.act(CPA(Vp[:, mt, (q - 2) * 512:(q - 1) * 512], bank[:, :]), r=[bk], w=["Vp"])
                dst = (mk_d if q < 2 else mv_d)[mt * 128:(mt + 1) * 128, (q % 2) * 512:(q % 2 + 1) * 512]
                P.dma("sp", dst, t[:], r=[tk], w=[f"o_mkv{q}{mt}"], stream="o_" + tk)

        def branch(i, first, tblk):
            for hf in range(2):
                sg = w_acquire(G_GATE + 2 * i + hf)
                sw = w_acquire(G_BR + 2 * i + hf)
                cnt = 0
                for ct in range(4):
                    j = hf * 4 + ct
                    for tb, (c0, n) in enumerate(tblk):
                        par = cnt % 4; cnt += 1
                        bA, kA = ps[par * 2], psk[par * 2]
                        bB, kB = ps[par * 2 + 1], psk[par * 2 + 1]
                        mm8(bA[:, 0:n], kA, sg, ct * 128, lambda kc: xnT[:, kc, c0:c0 + n], [f"xnT{tb}"])
                        mm8(bB[:, 0:n], kB, sw, ct * 128, lambda kc: yT[:, kc, c0:c0 + n], [f"yT{tb}"])
                        t, tk = tmp()
                        P.act(ACTF(t[:, 0:n], bA[:, 0:n], AF.Sigmoid), r=[kA], w=[tk])
                        ak = f"acc{j}_{tb}"
                        if first:
                            P.dve(TT(acc[:, j, c0:c0 + n], bB[:, 0:n], t[:, 0:n], ALU.mult), r=[kB, tk], w=[ak])
                        else:
                            t2, tk2 = tmp()
                            P.dve(TT(t2[:, 0:n], bB[:, 0:n], t[:, 0:n], ALU.mult), r=[kB, tk], w=[tk2])
                            P.pool(TT(acc[:, j, c0:c0 + n], acc[:, j, c0:c0 + n], t2[:, 0:n], ALU.add), r=[ak, tk2], w=[ak])

        def conv_phase(p_, tblk):
            for j in range(8):
                s = w_acquire(G_CONV + j)
                P.pool(CP(U[:, 0:2], ucarry[:, j, :]), r=["ucarry"], w=["U"])
                if p_ == 0:
                    P.pool(CP(us[:, :, 0:2], scT[:, j, :].rearrange("p (b s) -> p b s", s=2)), r=["scT"], w=["us"])
                for tb, (c0, n) in enumerate(tblk):
                    o = (tb % 2) * 4
                    bh, bB, bC, bg = ps[o:o + 4]; kh, kB, kC, kg = psk[o:o + 4]
                    for ct, (bank, bk) in enumerate(((bh, kh), (bB, kB), (bC, kC), (bg, kg))):
                        mm8(bank[:, 0:n], bk, s, ct * 128, lambda kc: xnT[:, kc, c0:c0 + n], [f"xnT{tb}"])
                    th, tkh = tmp()
                    P.act(CPA(th[:, 0:n], bh[:, 0:n]), r=[kh], w=[tkh])
                    if n == 512:
                        ucur, um1, um2 = U[:, 2 + c0:2 + c0 + n], U[:, 1 + c0:1 + c0 + n], U[:, c0:c0 + n]
                        v3 = lambda a: a
                        ukey = "U"
                    else:
                        ucur, um1, um2 = us[:, :, 2:10], us[:, :, 1:9], us[:, :, 0:8]
                        v3 = lambda a: a.rearrange("p (b t) -> p b t", t=NT)
                        ukey = "us"
                    P.dve(TT(ucur, v3(bC[:, 0:n]), v3(th[:, 0:n]), ALU.mult), r=[kC, tkh], w=[ukey])
                    a1, ka1 = tmp()
                    P.dve(TS(v3(a1[:, 0:n]), ucur, vcol(V_CW2, j), vcol(V_CB, j), ALU.mult, ALU.add), r=[ukey, "vecs"], w=[ka1])
                    a2, ka2 = tmp()
                    P.dve(STT(v3(a2[:, 0:n]), um1, vcol(V_CW1, j), v3(a1[:, 0:n]), ALU.mult, ALU.add), r=[ukey, ka1, "vecs"], w=[ka2])
                    a3, ka3 = tmp()
                    P.dve(STT(v3(a3[:, 0:n]), um2, vcol(V_CW0, j), v3(a2[:, 0:n]), ALU.mult, ALU.add), r=[ukey, ka2, "vecs"], w=[ka3])
                    sg, ksg = tmp()
                    P.act(ACTF(sg[:, 0:n], bg[:, 0:n], AF.Silu), r=[kg], w=[ksg])
                    a4, ka4 = tmp()
                    P.dve(TT(a4[:, 0:n], bB[:, 0:n], a3[:, 0:n], ALU.mult), r=[kB, ka3], w=[ka4])
                    P.pool(TT(yT[:, j, c0:c0 + n], a4[:, 0:n], sg[:, 0:n], ALU.mult), r=[ka4, ksg], w=[f"yT{tb}"])
                P.pool(CP(ucarry[:, j, :], U[:, 1024:1026]), r=["U"], w=["ucarry"])
                if p_ == 0:
                    P.pool(CP(uls[:, j, :, :], us[:, :, 8:10]), r=["us"], w=["uls"])
            for hf in range(2):
                for jj in range(4):
                    j = hf * 4 + jj
                    if p_ == 1:
                        P.pe(TR(ps[0][0:2, jj * 128:(jj + 1) * 128], ucarry[:, j, :], ident), r=["ucarry", "consts"], w=[psk[0]])
                    else:
                        P.pe(TR(ps[1][0:32, jj * 128:(jj + 1) * 128], uls[:, j, :, :].rearrange("p b s -> p (b s)"), ident),
                             r=["uls", "consts"], w=[psk[1]])
                t, tk = tmp()
                if p_ == 1:
                    P.dve(CP(t[0:2, :], ps[0][0:2, :]), r=[psk[0]], w=[tk])
                    P.dma("sp", convp_d[:, hf * 512:(hf + 1) * 512], t[0:2, :], r=[tk], w=[f"o_convp{hf}"], stream="o_" + tk)
                else:
                    P.dve(CP(t[0:32, :], ps[1][0:32, :]), r=[psk[1]], w=[tk])
                    P.dma("sp", convs_d[:, hf * 512:(hf + 1) * 512], t[0:32, :], r=[tk], w=[f"o_convs{hf}"], stream="o_" + tk)

        def mem_phase(p_, tblk):
            for h in range(4):
                s = w_acquire(G_MEM + h)
                for tb, (c0, n) in enumerate(tblk):
                    bq = [ps[0], ps[1]]; bg = [ps[2], ps[3]]
                    for ct in range(4):
                        mm8(ps[ct][:, 0:n], psk[ct], s, ct * 128, lambda kc: xnT[:, kc, c0:c0 + n], [f"xnT{tb}"])
                    if n == 128:
                        for dc in range(2):
                            P.act(CPA(qTs[:, 2 * h + dc, :], bq[dc][:, 0:128]), r=[psk[dc]], w=["qTs"])
                            P.act(ACTF(sgms[:, 2 * h + dc, :], bg[dc][:, 0:128], AF.Silu), r=[psk[2 + dc]], w=["sgms"])
                        continue
                    qt, kq = tmp()
                    qT = qt[:, :].bitcast(BF16)
                    sg0, ks0 = tmp(); sg1, ks1 = tmp()
                    for dc in range(2):
                        P.act(CPA(qT[:, dc * 512:(dc + 1) * 512], bq[dc][:, :]), r=[psk[dc]], w=[kq])
                    P.act(ACTF(sg0[:, :], bg[0][:, :], AF.Silu), r=[psk[2]], w=[ks0])
                    P.act(ACTF(sg1[:, :], bg[1][:, :], AF.Silu), r=[psk[3]], w=[ks1])
                    et, ke = tmp()
                    E = et[:, :].bitcast(BF16)
                    for mt in range(2):
                        bS, kS = ps[4 + mt], psk[4 + mt]
                        for dc in range(2):
                            P.pe(MM(bS[:, :], KT[:, 2 * h + dc, mt * 128:(mt + 1) * 128], qT[:, dc * 512:(dc + 1) * 512],
                                    dc == 0, dc == 1), r=["KT", kq], w=[kS])
                        P.act(ACTF(E[:, mt * 512:(mt + 1) * 512], bS[:, :], AF.Exp, scale=1.0 / 16.0), r=[kS], w=[ke])
                    for mt in range(2):
                        P.pe(MM(ps[6][:, :], onesb[:], E[:, mt * 512:(mt + 1) * 512], mt == 0, mt == 1),
                             r=["onesb", ke], w=[psk[6]])
                    rd, krd = tmp()
                    P.dve(RCP(rd[:, :], ps[6][:, :]), r=[psk[6]], w=[krd])
                    for dvc in range(2):
                        for mt in range(2):
                            P.pe(MM(ps[7][:, :], Vp[:, mt, h * 256 + dvc * 128:h * 256 + (dvc + 1) * 128],
                                    E[:, mt * 512:(mt + 1) * 512], mt == 0, mt == 1), r=["Vp", ke], w=[psk[7]])
                        o1, ko1 = tmp()
                        P.dve(TT(o1[:, :], ps[7][:, :], rd[:, :], ALU.mult), r=[psk[7], krd], w=[ko1])
                        sgd, ksd = (sg0, ks0) if dvc == 0 else (sg1, ks1)
                        P.pool(TT(yT[:, 2 * h + dvc, c0:c0 + n], o1[:, :], sgd[:, :], ALU.mult), r=[ko1, ksd], w=[f"yT{tb}"])
            if p_ != 0:
                return
            tmp_excl.update({4, 5, 6, 7})
            Kset = [[(Kb[:, 0, :], "Kb0"), (Kb[:, 1, :], "Kb1")],
                    [(tmps[4][:, :].bitcast(BF16), "tmp4"), (tmps[5][:, :].bitcast(BF16), "tmp5")]]
            Vset = [[(Vb[:, 0, :], "Vb0"), (Vb[:, 1, :], "Vb1")],
                    [(tmps[6][:, :].bitcast(BF16), "tmp6"), (tmps[7][:, :].bitcast(BF16), "tmp7")]]
            for b in range(NB):
                Kc, Vc = Kset[b % 2], Vset[b % 2]
                for mt in range(2):
                    P.dma("pool", Kc[mt][0], ck_d[b, mt * 128:(mt + 1) * 128, :], w=[Kc[mt][1]], stream="k_" + Kc[mt][1], lat=8.0)
                    P.dma("pool", Vc[mt][0], cv_d[b, mt * 128:(mt + 1) * 128, :], w=[Vc[mt][1]], stream="k_" + Vc[mt][1], lat=8.0)
                for mt in range(2):
                    for dch in range(8):
                        P.pe(TR(psT[:, dch * 128:(dch + 1) * 128], Kc[mt][0][:, dch * 128:(dch + 1) * 128], identb[:]),
                             r=[Kc[mt][1], "identb"], w=[psk[7]])
                    P.act(CPA(KTb[:, :, mt * 128:(mt + 1) * 128], psT.rearrange("p (c t) -> p c t", c=8)), r=[psk[7]], w=["KTb"])
                bS, kS = ps[4], psk[4]
                for h in range(4):
                    for mt in range(2):
                        for dc in range(2):
                            P.pe(MM(bS[:, (h * 2 + mt) * 8:(h * 2 + mt + 1) * 8], KTb[:, 2 * h + dc, mt * 128:(mt + 1) * 128],
                                    qTs[:, 2 * h + dc, b * 8:(b + 1) * 8], dc == 0, dc == 1), r=["KTb", "qTs"], w=[kS])
                et, ke = tmp()
                E = et[:, 0:32].bitcast(BF16)
                P.act(ACTF(E, bS[:, 0:64], AF.Exp, scale=1.0 / 16.0), r=[kS], w=[ke])
                E4 = E.rearrange("p (h m t) -> p h m t", h=4, m=2)
                for mt in range(2):
                    P.pe(MM(ps[5][:, 0:32], onesb[:], E4[:, :, mt, :], mt == 0, mt == 1), r=["onesb", ke], w=[psk[5]])
                rd, krd = tmp()
                P.dve(RCP(rd[:, 0:32], ps[5][:, 0:32]), r=[psk[5]], w=[krd])
                for h in range(4):
                    for dvc in range(2):
                        for mt in range(2):
                            P.pe(MM(ps[6][:, (h * 2 + dvc) * 8:(h * 2 + dvc + 1) * 8],
                                    Vc[mt][0][:, h * 256 + dvc * 128:h * 256 + (dvc + 1) * 128],
                                    E[:, (h * 2 + mt) * 8:(h * 2 + mt + 1) * 8], mt == 0, mt == 1), r=[Vc[mt][1], ke], w=[psk[6]])
                o1, ko1 = tmp()
                rd4 = rd[:, 0:32].rearrange("p (h t) -> p h t", h=4).unsqueeze(2).to_broadcast([128, 4, 2, 8])
                P.dve(TT(o1[:, 0:64].rearrange("p (h d t) -> p h d t", h=4, d=2),
                         ps[6][:, 0:64].rearrange("p (h d t) -> p h d t", h=4, d=2), rd4, ALU.mult), r=[psk[6], krd], w=[ko1])
                P.pool(TT(yT[:, :, 1024 + b * 8:1024 + (b + 1) * 8], o1[:, 0:64].rearrange("p (c t) -> p c t", c=8),
                          sgms[:, :, b * 8:(b + 1) * 8], ALU.mult), r=[ko1, "sgms"], w=["yT2"])
            tmp_excl.clear()

        def out_phase(p_, tblk):
            s0 = w_acquire(G_OUT); s1 = w_acquire(G_OUT + 1)
            for j in range(8):
                for tb, (c0, n) in enumerate(tblk):
                    eng = P.pool if (j + tb) % 2 else P.dve
                    eng(CP(yT[:, j, c0:c0 + n], acc[:, j, c0:c0 + n]), r=[f"acc{j}_{tb}"], w=[f"yT{tb}"])
            for li, gi in enumerate(PASS_TILES[p_]):
                tb = li // 4
                q = li % 2
                xt_, kx = (xt, "xt") if q == 0 else (xt2, "xt2")
                xn_, kn = (xnb, "xnb") if q == 0 else (xnb2, "xnb2")
                o = 8 * q
                P.dma("sp", xt_[:], xall_d[gi], w=[kx], stream=kx)
                hh = (tmp(), tmp())
                for hf, s in enumerate((s0, s1)):
                    bank, bk = ps[(li % 2) * 2 + hf], psk[(li % 2) * 2 + hf]
                    for kc in range(8):
                        P.pe(MM(bank[:, :], yT[:, kc, li * 128:(li + 1) * 128], ring[s][:, kc, :], kc == 0, kc == 7),
                             r=[f"yT{tb}", f"ring{s}"], w=[bk])
                    P.dve(TT(hh[hf][0][:, :], bank[:, :], xt_[:, hf * 512:(hf + 1) * 512], ALU.add), r=[bk, kx], w=[hh[hf][1]])
                    P.act(ACTF(xn_[:, hf * 512:(hf + 1) * 512], hh[hf][0][:, :], AF.Square, accum=ssb[:, o + 4 + hf:o + 5 + hf]),
                          r=[hh[hf][1]], w=[kn, f"ss{o + 4 + hf}"])
                P.dve(TT(ssb[:, o + 6:o + 7], ssb[:, o + 4:o + 5], ssb[:, o + 5:o + 6], ALU.add), r=[f"ss{o + 4}", f"ss{o + 5}"], w=[f"ss{o + 6}"])
                P.act(ACTF(ssb[:, o + 7:o + 8], ssb[:, o + 6:o + 7], AF.Ln, bias=EPS, scale=1.0 / D), r=[f"ss{o + 6}"], w=[f"ss{o + 7}"])
                P.act(ACTF(ssb[:, o + 3:o + 4], ssb[:, o + 7:o + 8], AF.Exp, scale=-0.5), r=[f"ss{o + 7}"], w=[f"ss{o + 3}"])
                for hf in range(2):
                    P.dve(STT(hh[hf][0][:, :], hh[hf][0][:, :], ssb[:, o + 3:o + 4], gfin[:, hf * 512:(hf + 1) * 512], ALU.mult, ALU.mult),
                          r=[hh[hf][1], f"ss{o + 3}", "gfin"], w=[hh[hf][1]])
                    P.dma("sp", y_d[gi][:, hf * 512:(hf + 1) * 512], hh[hf][0][:, :], r=[hh[hf][1]], w=[f"o_y{gi}_{hf}"],
                          stream="o_" + hh[hf][1])

        C0 = float(np.exp(-0.5))
        accf = acc[:].rearrange("p j t -> p (j t)")
        roff = [0]
        def ralloc(n32):
            a = accf[:, roff[0]:roff[0] + n32]; roff[0] += n32
            assert roff[0] <= 8 * NTP
            return a
        shb = [ralloc(514).rearrange("p (a t) -> p a t", a=2) for _ in range(4)]
        shs = [ralloc(144).rearrange("p (b t) -> p b t", t=9) for _ in range(4)]
        wdad = ralloc(576).bitcast(BF16)
        AR = ralloc(512).bitcast(BF16).rearrange("p (a t) -> p a t", a=2)
        BTt = ralloc(256).bitcast(BF16); KTt = ralloc(256).bitcast(BF16); VTt = ralloc(256).bitcast(BF16)
        Ms = ralloc(1024).rearrange("p (b v) -> p b v", v=64)
        Msb = ralloc(512).bitcast(BF16).rearrange("p (b v) -> p b v", v=64)
        Sin = ralloc(2048).rearrange("p (b c) -> p b c", c=128)
        UTs = ralloc(128)
        RKEYS = ["shb0_0", "shb1_0", "shb2_0", "shb3_0", "shb0_1", "shb1_1", "shb2_1", "shb3_1", "AR0", "AR1", "BTt0", "BTt1", "KTt0", "KTt1", "VTt0", "VTt1", "pc0", "pc1", "shs0", "shs1", "shs2", "shs3", "wdad", "AR", "BTt", "KTt", "VTt",
                 "Ms", "Msb", "Sin", "UTs"]
        loraW = sb("loraW", (128, D), BF16); blkb = sb("blkb", (128, 128), BF16)
        Mst = sb("Mst", (128, 8, 64)); M0b = sb("M0b", (128, 64), BF16)
        pcarry = sb("pcarry", (128, 25)); pls = sb("pls", (128, 25, NB)); shT = sb("shT", (128, 25, NB))
        pc = sb("pc", (128, 16)); maskP = sb("maskP", (128, 512)); maskS = sb("maskS", (128, 128))
        dummy = sb("dummy", (128, 4))
        big16 = sb("big16", (128, 2688 + 1024), BF16); Utok = sb("Utok", (128, 128), BF16)
        Mtmp = sb("Mtmp", (128, 64))
        Ubm = Kb[:].rearrange("p a n -> p (a n)").rearrange("p (b c) -> p b c", c=128)
        Mmask = consts[:, C_RESET:C_RESET + 16]

        hv = sb("hv", (128, 32))

        def rwkv_setup():
            for q_, off_ in enumerate((V_W0, V_A0, V_LNG, V_LNB)):
                P.dve(TS(hv[:, 8 * q_:8 * q_ + 8], vecs[:, off_:off_ + 8], 0.5, None, ALU.mult), r=["vecs"], w=["hv"])
            P.dma("pool", loraW[:], lora_d[:, :], w=["loraW"], stream="c_lora")
            P.dve(CP(blkb[:], consts[:, C_BLK:C_BLK + 128]), r=["consts"], w=["blkb"])
            P.dve(MSET(Mst[:], 0.0), w=["Mst"])
            P.dve(MSET(pcarry[:], 0.0), w=["pcarry"])
            P.dve(MSET(maskP[:], 1.0), w=["maskP"])
            P.dve(MSET(maskP[:].rearrange("p (c t) -> p c t", t=128)[:, :, 0:1], 0.0), w=["maskP"])
            P.dve(MSET(maskS[:], 1.0), w=["maskS"])
            P.dve(MSET(maskS[:].rearrange("p (c t) -> p c t", t=NT)[:, :, 0:1], 0.0), w=["maskS"])
            for half, (a, b_) in enumerate(((0, 13), (13, 25))):
                t, tk = tmp(); t2, tk2 = tmp(); t3, tk3 = tmp(); t4, tk4 = tmp()
                tl = [(t, tk), (t2, tk2), (t3, tk3), (t4, tk4)]
                for q in range(4):
                    lo = a * 128 + q * 512; hi = min(lo + 512, b_ * 128)
                    if hi > lo:
                        P.dma("sp", tl[q][0][0:NB, 0:hi - lo], sshift_d[:, lo:hi], w=[tl[q][1]], stream="l_" + tl[q][1])
                for ti in range(a, b_):
                    q, o = divmod((ti - a) * 128, 512)
                    P.pe(TR(ps[5][:, (ti - a) * 16:(ti - a + 1) * 16], tl[q][0][0:NB, o:o + 128], ident[0:NB, 0:NB]),
                         r=[tl[q][1], "consts"], w=[psk[5]])
                P.act(CPA(shT[:, a:b_, :].rearrange("p c b -> p (c b)"), ps[5][:, 0:(b_ - a) * 16]), r=[psk[5]], w=["shT"])

        def shift_lerp(bank, bkey, n, bi, tile_idx, out, okey, first, last, par):
            mu = vcol(V_MU, tile_idx)
            d, dk = tmps[11][:, par * 256:(par + 1) * 256], f"tmp11_{par}"
            if n == 256:
                b_, k_ = shb[bi][:, par, :], f"shb{bi}_{par}"
                o_, ok_ = shb[bi][:, 1 - par, :], f"shb{bi}_{1 - par}"
                if first:
                    P.pool(CP(b_[:, 0:1], pcarry[:, tile_idx:tile_idx + 1]), r=["pcarry"], w=[k_])
                else:
                    P.pool(CP(b_[:, 0:1], o_[:, 256:257]), r=[ok_], w=[k_])
                P.act(CPA(b_[:, 1:257], bank[:, 0:256]), r=[bkey], w=[k_])
                P.dve(TT(d[:, :], b_[:, 0:256], b_[:, 1:257], ALU.subtract), r=[k_], w=[dk])
                P.dve(STT(out[:, 0:256], d[:, :], mu, b_[:, 1:257], ALU.mult, ALU.add), r=[dk, k_, "vecs"], w=[okey])
                if last:
                    P.pool(CP(pcarry[:, tile_idx:tile_idx + 1], b_[:, 256:257]), r=[k_], w=["pcarry"])
            else:
                b_, k_ = shs[bi], f"shs{bi}"
                P.pool(CP(b_[:, :, 0:1], shT[:, tile_idx, :].unsqueeze(2)), r=["shT"], w=[k_])
                P.act(CPA(b_[:, :, 1:9], bank[:, 0:128].rearrange("p (b t) -> p b t", t=NT)), r=[bkey], w=[k_])
                d3 = d[:, 0:128].rearrange("p (b t) -> p b t", t=NT)
                P.dve(TT(d3, b_[:, :, 0:8], b_[:, :, 1:9], ALU.subtract), r=[k_], w=[dk])
                P.dve(STT(out[:, 0:128].rearrange("p (b t) -> p b t", t=NT), d3, mu, b_[:, :, 1:9], ALU.mult, ALU.add),
                      r=[dk, k_, "vecs"], w=[okey])
                P.pool(CP(pls[:, tile_idx, :].unsqueeze(2), b_[:, :, 8:9]), r=[k_], w=["pls"])

        TM = lambda i: (tmps[i], f"tmp{i}")
        TMp = lambda i, par: (tmps[i][:, par * 256:(par + 1) * 256], f"tmp{i}_{par}")
        TKEYS = [f"tmp{i}" for i in range(NTMP)] + [f"tmp{i}_{p}" for i in range(NTMP) for p in range(2)]
        pc2 = sb("pc2", (128, 4))
        psT0 = ps[0][:, :].bitcast(BF16)
        Utok2 = sb("Utok2", (128, 128), BF16)
        Vbm = Vb[:].rearrange("p a n -> p (a n)").rearrange("p (b c) -> p b c", c=128)
        Mtmp2 = sgms[:, 0:4, :].rearrange("p a (c v) -> p (a c) v", v=64)

        def rblocks(tblk):
            out = []
            for (c0, n) in tblk:
                if n == 512:
                    out += [(c0, 256), (c0 + 256, 256)]
                else:
                    out.append((c0, n))
            return out

        def lora_part(p_, tblk):
            s = w_acquire(G_LORA)
            rb = rblocks(tblk)
            npb = sum(1 for (_, n) in rb if n == 256)
            for bi_, (c0, n) in enumerate(rb):
                par = bi_ % 2; tb = c0 // 512
                bank, bk = ps[4 * par], psk[4 * par]
                mm8(bank[:, 0:n], bk, s, 0, lambda kc: xnT[:, kc, c0:c0 + n], [f"xnT{tb}"])
                o, ok = TMp(0, par)
                shift_lerp(bank, bk, n, 3, 24, o, ok, bi_ == 0, n == 256 and bi_ == npb - 1, par)
                P.act(ACTF(wdad[0:64, c0:c0 + n], o[0:64, 0:n], AF.Tanh), r=[ok], w=["wdad"])
                P.act(CPA(wdad[64:128, c0:c0 + n], o[64:128, 0:n]), r=[ok], w=["wdad"])

        def prep(j, tb, c0, n, s, first, last, par):
            P.marks.append((f"prep{j}_{tb}", len(P.ops)))
            B = 4 * par
            ARp = AR[:, :, par * 256:(par + 1) * 256]; kAR = f"AR{par}"
            BTp = BTt[:, par * 256:(par + 1) * 256]; kBT = f"BTt{par}"
            KTp = KTt[:, par * 256:(par + 1) * 256]; kKT = f"KTt{par}"
            VTp = VTt[:, par * 256:(par + 1) * 256]; kVT = f"VTt{par}"
            N = slice(0, n)
            for ct in range(4):
                mm8(ps[B + ct][:, 0:n], psk[B + ct], s, ct * 128, lambda kc: xnT[:, kc, c0:c0 + n], [f"xnT{tb}"])
            (rs, krs), (ks, kks), (vs, kvs) = TMp(0, par), TMp(1, par), TMp(2, par)
            (t3, k3), (t4, k4), (t5, k5), (t6, k6), (t7, k7), (t8, k8), (t9, k9) = [TMp(i, par) for i in range(3, 10)]
            P.act(ACTF(t9[:, N], ps[B + 3][:, N], AF.Tanh, scale=0.5), r=[psk[B + 3]], w=[k9])
            P.dve(STT(t9[:, N], t9[:, N], 1.0, ps[B + 3][:, N], ALU.add, ALU.mult), r=[k9, psk[B + 3]], w=[k9])
            shift_lerp(ps[B + 0], psk[B + 0], n, 0, j, rs, krs, first, last, par)
            shift_lerp(ps[B + 1], psk[B + 1], n, 1, 8 + j, ks, kks, first, last, par)
            shift_lerp(ps[B + 2], psk[B + 2], n, 2, 16 + j, vs, kvs, first, last, par)
            P.pe(MM(ps[B + 0][:, 0:n], loraW[0:64, j * 128:(j + 1) * 128], wdad[0:64, c0:c0 + n], True, True),
                 r=["loraW", "wdad"], w=[psk[B + 0]])
            P.pe(MM(ps[B + 1][:, 0:n], loraW[64:128, j * 128:(j + 1) * 128], wdad[64:128, c0:c0 + n], True, True),
                 r=["loraW", "wdad"], w=[psk[B + 1]])
            P.pool(CP(VTp[:, N], vs[:, N]), r=[kvs], w=[kVT])
            P.act(ACTF(t3[:, N], ps[B + 0][:, N], AF.Tanh, bias=hv[:, j:j + 1], scale=0.5), r=[psk[B + 0], "hv"], w=[k3])
            P.act(ACTF(t4[:, N], ps[B + 1][:, N], AF.Tanh, bias=hv[:, 8 + j:9 + j], scale=0.5), r=[psk[B + 1], "hv"], w=[k4])
            P.pool(TS(t3[:, N], t3[:, N], 0.5, 0.5, ALU.mult, ALU.add), r=[k3], w=[k3])
            P.pool(TS(t4[:, N], t4[:, N], 0.5, 0.5, ALU.mult, ALU.add), r=[k4], w=[k4])
            mask, mkey = (maskP, "maskP") if n == 256 else (maskS, "maskS")
            P.dve(lambda e: e.tensor_tensor_scan(out=t5[:, N], data0=mask[:, N], data1=t3[:, N], initial=0.0,
                                                 op0=ALU.mult, op1=ALU.add), r=[mkey, k3], w=[k5])
            P.dve(TT(t3[:, N], t5[:, N], t3[:, N], ALU.subtract), r=[k5, k3], w=[k3])
            P.act(ACTF(t6[:, N], t5[:, N], AF.Exp, scale=-C0), r=[k5], w=[k6])
            P.act(ACTF(t5[:, N], t5[:, N], AF.Exp, scale=C0), r=[k5], w=[k5])
            P.act(ACTF(t3[:, N], t3[:, N], AF.Exp, scale=-C0), r=[k3], w=[k3])
            if n == 256:
                P.pool(CP(pc2[:, 2 * par:2 * par + 2].unsqueeze(2), t6[:, N].rearrange("p (c t) -> p c t", t=128)[:, :, 127:128]), r=[k6], w=[f"pc{par}"])
            else:
                P.pool(CP(pc[:, 0:16].unsqueeze(2), t6[:, N].rearrange("p (c t) -> p c t", t=NT)[:, :, NT - 1:NT]), r=[k6], w=["pc"])
            P.pool(TT(ARp[:, 1, N], rs[:, N], t6[:, N], ALU.mult), r=[krs, k6], w=[kAR])
            P.act(MULA(t6[:, N], ks[:, N], vcol(V_KK, j)), r=[kks, "vecs"], w=[k6])
            t7b = t7[:, 0:128].bitcast(BF16)
            P.dve(TT(t7b[:, N], t6[:, N], t6[:, N], ALU.mult), r=[k6], w=[k7])
            P.pe(MM(ps[B + 2][:, N], blkb[:], t7b[:, N], True, True), r=["blkb", k7], w=[psk[B + 2]])
            P.act(ACTF(t7[:, N], ps[B + 2][:, N], AF.Ln, bias=1e-12), r=[psk[B + 2]], w=[k7])
            P.act(ACTF(t7[:, N], t7[:, N], AF.Exp, scale=-0.5), r=[k7], w=[k7])
            P.dve(TT(t6[:, N], t6[:, N], t7[:, N], ALU.mult), r=[k6, k7], w=[k6])
            P.dve(STT(ARp[:, 0, N], t6[:, N], -1.0, t3[:, N], ALU.mult, ALU.mult), r=[k6, k3], w=[kAR])
            P.pool(TT(t3[:, N], t6[:, N], t4[:, N], ALU.mult), r=[k6, k4], w=[k3])
            P.pool(TT(BTp[:, N], t3[:, N], t5[:, N], ALU.mult), r=[k3, k5], w=[kBT])
            P.dve(TS(t7[:, N], t4[:, N], -1.0, vcol(V_KA, j), ALU.add, ALU.mult), r=[k4, "vecs"], w=[k7])
            P.dve(STT(t7[:, N], t7[:, N], 1.0, ks[:, N], ALU.add, ALU.mult), r=[k7, kks], w=[k7])
            P.pool(TT(KTp[:, N], t7[:, N], t5[:, N], ALU.mult), r=[k7, k5], w=[kKT])
            t3b = t3[:, 0:128].bitcast(BF16)
            P.dve(STT(t3b[:, N], rs[:, N], vcol(V_RK, j), t7[:, N], ALU.mult, ALU.mult), r=[krs, k7, "vecs", k3], w=[k3])
            P.pe(MM(ps[B + 3][:, N], blkb[:], t3b[:, N], True, True), r=["blkb", k3], w=[psk[B + 3]])
            P.dve(STT(t8[:, N], ps[B + 3][:, N], 0.5, vs[:, N], ALU.mult, ALU.mult), r=[psk[B + 3], kvs], w=[k8])

        def post(j, tb, c0, n, par):
            P.marks.append((f"post{j}_{tb}", len(P.ops)))
            B = 4 * par
            N = slice(0, n)
            (ot, ko), (t8, k8), (t9, k9) = TMp(10, par), TMp(8, par), TMp(9, par)
            (a0, ka0), (a1, ka1), (a2, ka2) = TMp(0, par), TMp(1, par), TMp(2, par)
            ob = a0[:, 0:128].bitcast(BF16); o2b = a1[:, 0:128].bitcast(BF16)
            P.pool(CP(ob[:, N], ot[:, N]), r=[ko], w=[ka0])
            P.dve(TT(o2b[:, N], ot[:, N], ot[:, N], ALU.mult), r=[ko], w=[ka1])
            P.pe(MM(ps[B + 2][:, N], blkb[:], ob[:, N], True, True), r=["blkb", ka0], w=[psk[B + 2]])
            P.pe(MM(ps[B + 3][:, N], blkb[:], o2b[:, N], True, True), r=["blkb", ka1], w=[psk[B + 3]])
            P.act(MULA(a2[:, N], ps[B + 2][:, N], 1.0 / 64), r=[psk[B + 2]], w=[ka2])
            P.dve(TT(a0[:, N], a2[:, N], a2[:, N], ALU.mult), r=[ka2], w=[ka0])
            P.dve(STT(a1[:, N], ps[B + 3][:, N], 1.0 / 64, a0[:, N], ALU.mult, ALU.subtract), r=[psk[B + 3], ka0], w=[ka1])
            P.act(ACTF(a1[:, N], a1[:, N], AF.Ln, bias=GN_EPS), r=[ka1], w=[ka1])
            P.act(ACTF(a1[:, N], a1[:, N], AF.Exp, scale=-0.5), r=[ka1], w=[ka1])
            P.dve(TT(a0[:, N], ot[:, N], a2[:, N], ALU.subtract), r=[ko, ka2], w=[ka0])
            P.dve(TT(a0[:, N], a0[:, N], a1[:, N], ALU.mult), r=[ka0, ka1], w=[ka0])
            P.act(ACTF(a0[:, N], a0[:, N], AF.Identity, bias=hv[:, 24 + j:25 + j], scale=hv[:, 16 + j:17 + j]), r=[ka0, "hv"], w=[ka0])
            P.pool(TT(a0[:, N], a0[:, N], t8[:, N], ALU.add), r=[ka0, k8], w=[ka0])
            P.pool(TT(yT[:, j, c0:c0 + n], a0[:, N], t9[:, N], ALU.mult), r=[ka0, k9], w=[f"yT{tb}"])

        DONORS = [(memnT[:].rearrange("p a b -> p (a b)"), 2048), (KTb[:].rearrange("p a b -> p (a b)"), 2048),
                  (qTs[:].rearrange("p a b -> p (a b)"), 1024), (U[:, 0:1024].bitcast(BF16), 2048),
                  (xt[:, :].bitcast(BF16), 2048), (xnb[:, :], 1024), (big16[:, :], 2688 + 1024)]
        DONOR_KEYS = ["memnT", "KTb", "qTs", "U", "xt", "xnb", "Kb", "Vb", "Kb0", "Kb1", "Vb0", "Vb1", "sgms"]
        dst_ = {"d": 0, "o": 0}
        def dalloc(n):
            while DONORS[dst_["d"]][1] - dst_["o"] < n:
                dst_["d"] += 1; dst_["o"] = 0
            a = DONORS[dst_["d"]][0][:, dst_["o"]:dst_["o"] + n]; dst_["o"] += n
            return a
        NU = 8
        uSB = [dalloc(256) for _ in range(NU)]; uSK = [dalloc(256) for _ in range(NU)]
        uXY = [dalloc(256).rearrange("p (a t) -> p a t", a=2) for _ in range(NU)]
        uZ = [dalloc(128) for _ in range(NU)]; uLV = [dalloc(64) for _ in range(NU)]
        cTok = [dalloc(512).rearrange("p (q t) -> p q t", q=4) for _ in range(4)]
        cWT = [dalloc(128) for _ in range(4)]; cU0 = [dalloc(256).bitcast(F32) for _ in range(4)]
        UKEYS = ([f"uSB{u}" for u in range(NU)] + [f"uSK{u}" for u in range(NU)] + [f"uXY{u}" for u in range(NU)] +
                 [f"uZ{u}" for u in range(NU)] + [f"uLV{u}" for u in range(NU)] + [f"cTok{c}" for c in range(4)] +
                 [f"cWT{c}" for c in range(4)] + [f"cU0{c}" for c in range(4)])

        def scan_block(j, nch, sample, ot, ko, par):
            NL = 2 if sample else 6
            mSI = consts[:, C_BD_S:C_BD_S + 256] if sample else consts[:, C_TRIL_S:C_TRIL_S + 256]
            mLT = consts[:, C_BD_LT:C_BD_LT + 128] if sample else consts[:, C_LT:C_LT + 128]
            units = [(c, hh) for c in range(nch) for hh in range(2)]
            B = 4 * par
            UB = 4 * par
            CB = 2 * par
            AR_ = AR[:, :, par * 256:(par + 1) * 256]; kAR = f"AR{par}"
            BT_ = BTt[:, par * 256:(par + 1) * 256]; kBT = f"BTt{par}"
            KT_ = KTt[:, par * 256:(par + 1) * 256]; kKT = f"KTt{par}"
            VT_ = VTt[:, par * 256:(par + 1) * 256]; kVT = f"VTt{par}"
            CS = lambda c: slice(c * 128, (c + 1) * 128)
            SL = lambda hh: slice(64 * hh, 64 * hh + 64)
            for c in range(nch):
                pT = ps[B + c][:, :].bitcast(BF16)
                for q, (src, skey) in enumerate(((AR_[:, 0, CS(c)], kAR), (BT_[:, CS(c)], kBT), (KT_[:, CS(c)], kKT), (VT_[:, CS(c)], kVT))):
                    P.pe(TR(pT[:, q * 128:(q + 1) * 128], src, identb[:]), r=[skey, "identb"], w=[psk[B + c]])
            for c in range(nch):
                pT = ps[B + c][:, :].bitcast(BF16)
                P.act(CPA(cTok[CB + c][:].rearrange("p q t -> p (q t)"), pT[:, 0:512]), r=[psk[B + c]], w=[f"cTok{CB + c}"])
            P.marks.append((f"S2_{j}", len(P.ops)))
            for u, (c, hh) in enumerate(units):
                sl = SL(hh)
                P.pe(MM(ps[B + u][:, 0:256], BT_[sl, CS(c)], AR_[sl, :, CS(c)], True, True), r=[kBT, kAR], w=[psk[B + u]])
                P.pe(MM(ps[B + u][:, 256:512], KT_[sl, CS(c)], AR_[sl, :, CS(c)], True, True), r=[kKT, kAR], w=[psk[B + u]])
            for u, (c, hh) in enumerate(units):
                P.dve(TT(uSB[UB + u][:, :], ps[B + u][:, 0:256], mSI, ALU.mult), r=[psk[B + u], "consts"], w=[f"uSB{UB + u}"])
                P.dve(TT(uSK[UB + u][:, :], ps[B + u][:, 256:512], mSI, ALU.mult), r=[psk[B + u], "consts"], w=[f"uSK{UB + u}"])
            for u, (c, hh) in enumerate(units):
                sl = SL(hh)
                P.pe(MM(ps[B + u][:, 0:128], AR_[sl, 0, CS(c)], BT_[sl, CS(c)], True, True), r=[kBT, kAR], w=[psk[B + u]])
                P.pe(MM(ps[B + u][:, 384:448], uSK[UB + u][:, 0:128], cTok[CB + c][:, 3, sl], True, True), r=[f"uSK{UB + u}", f"cTok{CB + c}"], w=[psk[B + u]])
            for u, (c, hh) in enumerate(units):
                P.dve(TT(uXY[UB + u][:, 1, :], ps[B + u][:, 0:128], mLT, ALU.mult), r=[psk[B + u], "consts"], w=[f"uXY{UB + u}"])
                P.act(CPA(uLV[UB + u][:, :], ps[B + u][:, 384:448]), r=[psk[B + u]], w=[f"uLV{UB + u}"])
                P.pool(CP(uXY[UB + u][:, 0, :], uSB[UB + u][:, 0:128]), r=[f"uSB{UB + u}"], w=[f"uXY{UB + u}"])
                P.pool(TT(uZ[UB + u][:, :], uSB[UB + u][:, 0:128], identb[:], ALU.add), r=[f"uSB{UB + u}", "identb"], w=[f"uZ{UB + u}"])
            P.marks.append((f"S3_{j}", len(P.ops)))
            for n_ in range(1, NL + 1):
                for u in range(len(units)):
                    P.pe(MM(ps[B + u][:, 0:128], uXY[UB + u][:, 1, :], uXY[UB + u][:, 0, :], True, True), r=[f"uXY{UB + u}"], w=[psk[B + u]])
                    P.pe(MM(ps[B + u][:, 128:256], uXY[UB + u][:, 0, :], uXY[UB + u][:, 1, :], True, True), r=[f"uXY{UB + u}"], w=[psk[B + u]])
                for u in range(len(units)):
                    P.act(CPA(uXY[UB + u][:].rearrange("p a t -> p (a t)"), ps[B + u][:, 0:256]), r=[psk[B + u]], w=[f"uXY{UB + u}"])
                for u in range(len(units)):
                    P.pe(MM(ps[B + u][:, 256:384], uXY[UB + u][:, 1, :], uZ[UB + u][:, :], True, True), r=[f"uXY{UB + u}", f"uZ{UB + u}"], w=[psk[B + u]])
                for u in range(len(units)):
                    P.dve(TT(uZ[UB + u][:, :], ps[B + u][:, 256:384], uZ[UB + u][:, :], ALU.add), r=[psk[B + u], f"uZ{UB + u}"], w=[f"uZ{UB + u}"])
            P.marks.append((f"S4_{j}", len(P.ops)))
            for u, (c, hh) in enumerate(units):
                sl = SL(hh)
                P.pe(MM(ps[B + 2 * c][sl, 0:128], cTok[CB + c][:, 0, sl], uZ[UB + u][:, :], True, True), r=[f"cTok{CB + c}", f"uZ{UB + u}"], w=[psk[B + 2 * c]])
                if not sample:
                    P.pe(MM(ps[B + 2 * c + 1][:, 64 * hh:64 * hh + 64], uZ[UB + u][:, :], uLV[UB + u][:, :], True, True), r=[f"uZ{UB + u}", f"uLV{UB + u}"], w=[psk[B + 2 * c + 1]])
                else:
                    P.pe(MM(ps[B + 2 * c + 1][sl, 0:128], uLV[UB + u][:, :], uZ[UB + u][:, :], True, True), r=[f"uZ{UB + u}", f"uLV{UB + u}"], w=[psk[B + 2 * c + 1]])
            for c in range(nch):
                P.act(CPA(cWT[CB + c][:, :], ps[B + 2 * c][:, 0:128]), r=[psk[B + 2 * c]], w=[f"cWT{CB + c}"])
                P.act(CPA(cU0[CB + c][:, :], ps[B + 2 * c + 1][:, 0:128]), r=[psk[B + 2 * c + 1]], w=[f"cU0{CB + c}"])
            P.marks.append((f"S5_{j}", len(P.ops)))
            for c in range(nch):
                tk, tkk = cTok[CB + c], f"cTok{CB + c}"
                bU, kU = ps[B + 0], psk[B + 0]
                if not sample:
                    for hh in range(2):
                        P.pe(MM(bU[:, 64 * hh:64 * hh + 64], cWT[CB + c][SL(hh), :], M0b[SL(hh), :], True, True), r=[f"cWT{CB + c}", "M0b"], w=[kU], force=(hh == 1))
                    P.dve(TT(Utok[:, :], bU[:, 0:128], cU0[CB + c][:, :], ALU.add), r=[kU, f"cU0{CB + c}"], w=["Utok"])
                    ut, utk = Utok, "Utok"
                else:
                    for hh in range(2):
                        for b in range(NB):
                            P.pe(MM(bU[SL(hh), 8 * b:8 * b + 8], Msb[SL(hh), b, :], cWT[CB + c][SL(hh), 8 * b:8 * b + 8], True, True),
                                 r=["Msb", f"cWT{CB + c}"], w=[kU], force=(hh == 1 and b == 0))
                    P.dve(TT(Utok[:, :], bU[:, 0:128], cU0[CB + c][:, :], ALU.add), r=[kU, f"cU0{CB + c}"], w=["Utok"])
                    pT = ps[B + 3][:, :].bitcast(BF16)
                    P.pe(TR(pT[:, 0:128], Utok[:, :], identb[:]), r=["Utok", "identb"], w=[psk[B + 3]])
                    P.act(CPA(Utok2[:, :], pT[:, 0:128]), r=[psk[B + 3]], w=["Utok2"])
                    ut, utk = Utok2, "Utok2"
                bO, kO = ps[B + 1], psk[B + 1]
                for hh in range(2):
                    u = UB + 2 * c + hh; sl = SL(hh)
                    out = bO[sl, 0:128]
                    if not sample:
                        P.pe(MM(out, M0b[sl, :], AR_[sl, 1, CS(c)], True, False), r=["M0b", kAR], w=[kO], force=(hh == 1))
                    else:
                        for b in range(NB):
                            P.pe(MM(bO[sl, 8 * b:8 * b + 8], Msb[sl, b, :], AR_[sl, 1, 8 * b:8 * b + 8], b == 0, False),
                                 r=["Msb", kAR], w=[kO], force=(hh == 1 and b == 0))
                    P.pe(MM(out, tk[:, 3, sl], uSK[u][:, 128:256], False, False), r=[tkk, f"uSK{u}"], w=[kO])
                    P.pe(MM(out, ut[:, sl], uSB[u][:, 128:256], False, True), r=[utk, f"uSB{u}"], w=[kO])
                if not sample:
                    bM, kM = ps[B + 2], psk[B + 2]
                    for hh in range(2):
                        sl = SL(hh)
                        out = bM[sl, 128:192]
                        P.pe(MM(out, tk[:, 2, sl], tk[:, 3, sl], True, False), r=[tkk], w=[kM])
                        P.pe(MM(out, tk[:, 1, sl], ut[:, sl], False, True), r=[tkk, utk], w=[kM])
                    P.dve(TT(Mtmp[:, :], bM[:, 128:192], Mst[:, j, :], ALU.add), r=[kM, "Mst"], w=["Mtmp"])
                    P.dve(TS(M0b[:, :], Mtmp[:, :], pc2[:, 2 * par + c:2 * par + c + 1], None, ALU.mult), r=["Mtmp", f"pc{par}"], w=["M0b"])
                    P.pool(TS(Mst[:, j, :], Mtmp[:, :], pc2[:, 2 * par + c:2 * par + c + 1], None, ALU.mult), r=["Mtmp", f"pc{par}"], w=["Mst"])
                else:
                    mm3 = Mmask.unsqueeze(2).to_broadcast([128, NB, 128])
                    P.dve(TT(Ubm[:], ut[:, :].unsqueeze(1).to_broadcast([128, NB, 128]), mm3, ALU.mult), r=[utk, "consts"], w=["Kb"])
                    P.pool(TT(Vbm[:], tk[:, 3, :].unsqueeze(1).to_broadcast([128, NB, 128]), mm3, ALU.mult), r=[tkk, "consts"], w=["Vb"])
                    for half in range(2):
                        bank, bk = ps[6 + half], psk[6 + half]
                        bs = slice(8 * half, 8 * half + 8)
                        for hh in range(2):
                            sl = SL(hh)
                            out = bank[sl, :].rearrange("p (b v) -> p b v", v=64)
                            P.pe(MM(out, tk[:, 2, sl], Vbm[:, bs, sl], True, False), r=[tkk, "Vb"], w=[bk])
                            P.pe(MM(out, tk[:, 1, sl], Ubm[:, bs, sl], False, True), r=[tkk, "Kb"], w=[bk])
                        P.dve(TT(Mtmp2[:], bank[:, :].rearrange("p (b v) -> p b v", v=64), Ms[:, bs, :], ALU.add), r=[bk, "Ms"], w=["sgms"])
                        P.dve(TT(Ms[:, bs, :], Mtmp2[:], pc[:, bs].unsqueeze(2).to_broadcast([128, 8, 64]), ALU.mult),
                              r=["sgms", "pc"], w=["Ms"])
                P.act(CPA(ot[:, CS(c)], bO[:, 0:128]), r=[kO], w=[ko])

        ACCKEYS = [f"acc{j}_{tb}" for j in range(8) for tb in range(3)]

        def load_sample_state(j):
            for h in range(2):
                P.dma("sp", Sin[0:64, :, h * 64:(h + 1) * 64], swkv_d[:, 2 * j + h].rearrange("b v k -> v b k"),
                      w=["Sin"], stream=f"Sin{h}")
            for half in range(2):
                bank, bk = ps[5 + half], psk[5 + half]
                for bb in range(8):
                    b = half * 8 + bb
                    P.pe(TR(bank[:, bb * 64:(bb + 1) * 64], Sin[0:64, b, :], ident[0:64, 0:64]), r=["Sin", "consts"], w=[bk])
                bs = slice(8 * half, 8 * half + 8)
                P.dve(CP(Ms[:, bs, :], bank[:, :].rearrange("p (b v) -> p b v", v=64)), r=[bk], w=["Ms"])
                P.act(CPA(Msb[:, bs, :], bank[:, :].rearrange("p (b v) -> p b v", v=64)), r=[bk], w=["Msb"])

        def store_sample_state(j):
            for q in range(4):
                bank, bk = ps[5 + q % 2], psk[5 + q % 2]
                for bb in range(4):
                    b = q * 4 + bb
                    P.pe(TR(bank[0:64, bb * 128:(bb + 1) * 128], Ms[:, b, :], ident), r=["Ms", "consts"], w=[bk])
                P.dve(CP(Sin[0:64, 4 * q:4 * q + 4, :], bank[0:64, :].rearrange("p (b c) -> p b c", c=128)), r=[bk], w=["Sin"])
            for h in range(2):
                P.dma("sp", wkvs_d[:, 2 * j + h].rearrange("b v k -> v b k"), Sin[0:64, :, h * 64:(h + 1) * 64],
                      r=["Sin"], w=[f"o_wkvs{j}_{h}"], stream=f"o_Sin{h}")

        def rwkv_phase(p_, tblk):
            P.dve(MSET(dummy[:, 0:1], 0.0), w=ACCKEYS + RKEYS + DONOR_KEYS + UKEYS + TKEYS)
            if p_ == 0:
                rwkv_setup()
                P.dve(MSET(dummy[:, 2:3], 0.0), w=TKEYS)
            RCUT = int(os.environ.get('RCUT', '99'))
            if RCUT >= 1:
                lora_part(p_, tblk)
            npb = sum(1 for (_, n) in tblk if n == 512)
            for j in range(8 if RCUT >= 2 else 0):
                if RCUT < 9 and j > 0:
                    break
                s = w_acquire(G_RWKV + j)
                P.act(CPA(M0b[:, :], Mst[:, j, :]), r=["Mst"], w=["M0b"])
                rb = rblocks(tblk)
                nprb = sum(1 for (_, n) in rb if n == 256)
                for bi_, (c0, n) in enumerate(rb):
                    sample = (n == 128)
                    par = bi_ % 2; tb = c0 // 512
                    if sample:
                        load_sample_state(j)
                    prep(j, tb, c0, n, s, bi_ == 0, bi_ == nprb - 1, par)
                    ot, ko = TMp(10, par)
                    if not sample:
                        scan_block(j, 2, False, ot, ko, par)
                    else:
                        scan_block(j, 1, True, ot, ko, par)
                        store_sample_state(j)
                    post(j, tb, c0, n, par)
            P.dve(MSET(dummy[:, 3:4], 0.0), w=TKEYS)
            if p_ == 0:
                for half, (a, b_) in enumerate(((0, 13), (13, 25))):
                    for ti in range(a, b_):
                        q, o = divmod((ti - a) * 128, 512)
                        P.pe(TR(ps[1 + q][0:NB, o:o + 128], pls[:, ti, :], ident), r=["pls", "consts"], w=[psk[1 + q]])
                    for q in range(4):
                        lo = a * 128 + q * 512; hi = min(lo + 512, b_ * 128)
                        if hi > lo:
                            t, tk_ = tmp()
                            P.dve(CP(t[0:NB, 0:hi - lo], ps[1 + q][0:NB, 0:hi - lo]), r=[psk[1 + q]], w=[tk_])
                            P.dma("sp", shifts_d[:, lo:hi], t[0:NB, 0:hi - lo], r=[tk_], w=[f"o_shs{lo}"], stream="o_" + tk_)
            else:
                P.pe(TR(ps[1][0:25, 0:128], pcarry[:, :], ident), r=["pcarry", "consts"], w=[psk[1]])
                t, tk_ = tmp()
                P.dve(CP(t[0:25, 0:128], ps[1][0:25, 0:128]), r=[psk[1]], w=[tk_])
                P.dma("sp", shiftp_d[0].rearrange("(c p) -> c p", p=128), t[0:25, 0:128], r=[tk_], w=["o_shp"], stream="o_" + tk_)
                for half in range(2):
                    bank, bk = ps[2 + half], psk[2 + half]
                    t, tk_ = tmp()
                    for jj in range(4):
                        P.pe(TR(bank[0:64, jj * 128:(jj + 1) * 128], Mst[:, half * 4 + jj, :], ident), r=["Mst", "consts"], w=[bk])
                    P.dve(CP(t[0:64, :], bank[0:64, :]), r=[bk], w=[tk_])
                    dst = wkvp_d[8 * half:8 * half + 8].rearrange("h v k -> v h k")
                    P.dma("sp", dst, t[0:64, :].rearrange("p (h k) -> p h k", k=64), r=[tk_], w=[f"o_wkvp{half}"], stream="o_" + tk_)
            P.dve(MSET(dummy[:, 1:2], 0.0), w=ACCKEYS + RKEYS + DONOR_KEYS + UKEYS + TKEYS)


        env = dict(locals())
        for p_ in range((2 if 'onepass' not in stages else 1) if LVL >= 4 else 0):
            tblk = PASS_TBLK[p_]
            for li, gi in enumerate(PASS_TILES[p_]):
                norm_T(xall_d[gi], V_NORM_IN, xnT[:, :, li * 128:(li + 1) * 128], [f"xnT{li // 4}"])
            nbr = 0
            if 'cutA' in stages:
                break
            if "rwkv" in stages:
                rwkv_phase(p_, tblk)
                branch(1, nbr == 0, tblk); nbr += 1
            if "conv" in stages:
                conv_phase(p_, tblk)
                branch(0, nbr == 0, tblk); nbr += 1
            if "mem" in stages:
                mem_phase(p_, tblk)
                branch(2, nbr == 0, tblk); nbr += 1
            if 'noout' not in stages:
                out_phase(p_, tblk)
        okeys = [k for k in P.lastw if k.startswith("o_")]
        P.add("sp", lambda e: e.nop(), r=okeys)
        n_ops = P.emit(es)
        print("ops", n_ops, "est_us", getattr(P, "est_us", None), "tsw", getattr(P, "n_tsw", 0), "waits", getattr(P, "n_wait", 0), "attached", getattr(P, "n_att", 0))
    return nc


def rwkv_phase(env, p_, tblk):
    raise NotImplementedError


_NC_CACHE = {}


def kernel(**inputs):
    stages = inputs.pop("_stages", ("rwkv", "conv", "mem"))
    maps = _host_inputs(inputs)
    key = tuple(stages)
    if key not in _NC_CACHE:
        _NC_CACHE[key] = build_nc(stages)
    nc = _NC_CACHE[key]
    res = run_bass_kernel_spmd(nc, maps, core_ids=list(range(NCORES)))
    R = res.results
    B = NCORES
    y_prompt = np.stack([R[c]["y"][0:16].reshape(T, D) for c in range(B)])
    y_sample = np.concatenate([R[c]["y"][16].reshape(NB, NT, D) for c in range(B)], axis=0)
    mk = np.stack([R[c]["mk"].reshape(256, 4, 256) for c in range(B)])[None]
    mv = np.stack([R[c]["mv"].reshape(256, 4, 256) for c in range(B)])[None]
    conv_p = np.stack([R[c]["convp"] for c in range(B)])[None]
    shift_p = np.stack([R[c]["shiftp"][0] for c in range(B)])[None]
    wkv_p = np.stack([R[c]["wkvp"] for c in range(B)])[None]
    conv_s = np.concatenate([R[c]["convs"].reshape(NB, 2, D) for c in range(B)], axis=0)[None]
    shift_s = np.concatenate([R[c]["shifts"] for c in range(B)], axis=0)[None]
    wkv_s = np.concatenate([R[c]["wkvs"] for c in range(B)], axis=0)[None]
    f = lambda a: np.ascontiguousarray(a, dtype=np.float32)
    return tuple(f(a) for a in (y_prompt, y_sample, mk, mv, conv_p, shift_p, wkv_p, conv_s, shift_s, wkv_s))
```

```python
import os
import numpy as np
from contextlib import ExitStack
import concourse.bass as bass
import concourse.mybir as mybir
from concourse.bass_utils import run_bass_kernel_spmd

F32 = mybir.dt.float32
BF16 = mybir.dt.bfloat16
AF = mybir.ActivationFunctionType
ALU = mybir.AluOpType
AX = mybir.AxisListType

NCORES = 8
D = 1024
T = 2048
NTOK = 2176
NB = 16
NT = 8
TBLK = [(0, 512), (512, 512), (1024, 512), (1536, 512), (2048, 128)]
EPS = 1e-6
GN_EPS = 64 * 1e-5

O_H, O_B, O_C, O_GC, O_R, O_K, O_V, O_WD, O_AD, O_GR, O_Q, O_GM, O_MC, O_MR, O_MM = (
    0, 1024, 2048, 3072, 4096, 5120, 6144, 7168, 7232, 7296, 8320, 9344, 10368, 11392, 12416)
G_CONV = 0
G_RWKV = 8
G_LORA = 16
G_MEM = 17
G_GATE = 21
G_BR = 27
G_OUT = 33
G_KV = 35
NGROUPS = 39

V_NORM_IN, V_CW0, V_CW1, V_CW2, V_CB, V_MU, V_W0, V_A0, V_KK, V_KA, V_RK, V_LNG, V_LNB, V_NMEM = (
    0, 8, 16, 24, 32, 40, 65, 73, 81, 89, 97, 105, 113, 121)
NV = 129


class _Op:
    __slots__ = ("eng", "fn", "deps", "raw", "dma", "stream", "signal", "sig_idx", "idx", "force", "cost", "lat", "tset")


class Prog:
    def __init__(self, nc):
        self.nc = nc
        self.ops = []
        self.lastw = {}
        self.readers = {}
        self.engs = {"pe": nc.tensor, "act": nc.scalar, "dve": nc.vector, "pool": nc.gpsimd, "sp": nc.sync}

    DEFC = {"pe": 0.12, "act": 0.45, "dve": 0.40, "pool": 0.60, "sp": 0.10}

    def add(self, eng, fn, r=(), w=(), dma=False, stream=None, force=False, cost=None, lat=None):
        op = _Op()
        op.force = force
        if cost is None and not dma and hasattr(fn, "free"):
            a, b = {"pe": (0.07, 0.00042), "act": (0.22, 0.00085), "dve": (0.12, 0.0011), "pool": (0.15, 0.0020), "sp": (0.1, 0.0)}[eng]
            cost = a + b * fn.free
        op.cost = cost if cost is not None else ((1.8 if eng == "pool" else 0.15) if dma else self.DEFC[eng])
        op.lat = lat if lat is not None else (4.0 if dma else 0.0)
        op.tset = getattr(fn, "tset", 0)
        op.eng, op.fn, op.dma, op.stream = eng, fn, dma, stream
        op.signal = False
        op.idx = len(self.ops)
        deps, raw = set(), set()
        psr = [k for k in r if k.startswith("ps")]
        for k in psr:
            if k in self.lastw:
                raw.add(self.lastw[k])
        r = [k for k in r if not k.startswith("ps")]
        w = list(w) + psr
        for k in r:
            if k in self.lastw:
                deps.add(self.lastw[k]); raw.add(self.lastw[k])
        for k in w:
            if k in self.lastw:
                deps.add(self.lastw[k]); raw.add(self.lastw[k])
            for x in self.readers.get(k, ()):
                deps.add(x)
        deps.discard(op.idx)
        op.deps, op.raw = deps, raw
        for k in r:
            self.readers.setdefault(k, []).append(op.idx)
        for k in w:
            self.lastw[k] = op.idx
            self.readers[k] = []
        self.ops.append(op)
        return op.idx

    def act(self, fn, r=(), w=()): return self.add("act", fn, r, w)
    def dve(self, fn, r=(), w=()): return self.add("dve", fn, r, w)
    def pool(self, fn, r=(), w=()): return self.add("pool", fn, r, w)
    def pe(self, fn, r=(), w=(), force=False): return self.add("pe", fn, r, w, force=force)

    def dma(self, eng, out, in_, r=(), w=(), stream=None, lat=None):
        assert stream is not None
        return self.add(eng, lambda e: e.dma_start(out=out, in_=in_), r, w, dma=True, stream=stream, lat=lat)

    def schedule(self, window=800):
        import heapq
        ops = self.ops; n = len(ops)
        if os.environ.get("KNOSCHED"):
            return list(range(n))
        succ = [[] for _ in ops]; indeg = [0] * n
        for op in ops:
            indeg[op.idx] = len(op.deps)
            for d in op.deps:
                succ[d].append(op.idx)
        start = [0.0] * n; finish = [0.0] * n; rtime = [0.0] * n
        blev = [0.0] * n
        for i in range(n - 1, -1, -1):
            op = ops[i]
            m = 0.0
            for sidx in succ[i]:
                v = blev[sidx] + (0.55 if ops[sidx].eng != op.eng else 0.12)
                if v > m:
                    m = v
            blev[i] = op.cost + op.lat + m
        PRI = os.environ.get("KPRI", "cp")
        ready = {e: [] for e in self.engs}
        efree = {e: 0.0 for e in self.engs}
        done = [False] * n
        for op in ops:
            if indeg[op.idx] == 0:
                ready[op.eng].append(op.idx)
        order = []; base = 0
        act_set = [0]
        TSW = 1.3
        while len(order) < n:
            while base < n and done[base]:
                base += 1
            lim = base + window
            best = None
            for e, lst in ready.items():
                cand = None; cand_key = None
                for i in lst:
                    if i >= lim:
                        continue
                    st = max(efree[e], rtime[i])
                    if e == "act" and ops[i].tset and ops[i].tset != act_set[0]:
                        st = max(efree[e] + TSW, rtime[i])
                    pr = i if PRI == "idx" else -blev[i]
                    key = (st, pr) if st > efree[e] else (efree[e], pr)
                    if cand is None or key < cand_key:
                        cand, cand_key = i, key
                if cand is not None and (best is None or cand_key < best[0]):
                    best = (cand_key, cand, e)
            assert best is not None, "scheduler stuck"
            (st, _pr), i, e = best
            op = ops[i]
            ready[e].remove(i)
            if e == "act" and op.tset:
                if op.tset != act_set[0]:
                    self.n_tsw = getattr(self, "n_tsw", 0) + 1
                act_set[0] = op.tset
            start[i] = st; efree[e] = st + op.cost; finish[i] = st + op.cost + op.lat
            done[i] = True; order.append(i)
            for sidx in succ[i]:
                sop = ops[sidx]
                if sop.eng == op.eng and not op.dma and not sop.dma and op.eng == "pe" and not sop.force:
                    t = start[i] + 0.01
                elif sop.eng == op.eng and not op.dma:
                    t = finish[i] + 0.12
                else:
                    t = finish[i] + 0.55
                if t > rtime[sidx]:
                    rtime[sidx] = t
                indeg[sidx] -= 1
                if indeg[sidx] == 0:
                    ready[sop.eng].append(sidx)
        self.est_us = max(finish)
        if os.environ.get("KDUMP"):
            a, b = [int(x) for x in os.environ["KDUMP"].split(":")]
            for i in range(a, b):
                op = ops[i]
                crit = max(op.deps, key=lambda d: finish[d]) if op.deps else -1
                print(f"   op {i} {op.eng:4s} start {start[i]:8.2f} fin {finish[i]:8.2f} rtime {rtime[i]:8.2f} critdep {crit} ({ops[crit].eng if crit >= 0 else ''} fin {finish[crit] if crit >= 0 else 0:8.2f}) dma={op.dma}")
        if os.environ.get("KSTAT"):
            tot = {e: 0.0 for e in self.engs}
            for op in ops:
                tot[op.eng] += op.cost
            print("model engine busy us:", {k: round(v) for k, v in tot.items()})
            for name, idx in getattr(self, "marks", []):
                if idx < n:
                    print(f"  mark {name:24s} op {idx:6d} start {start[idx]:9.1f}")
        return order

    def emit(self, es):
        nc = self.nc
        ops = self.ops
        need = [[] for _ in ops]
        for op in ops:
            for d in sorted(op.deps):
                dep = ops[d]
                if dep.dma:
                    need[op.idx].append(d)
                elif op.dma or dep.eng != op.eng:
                    dep.signal = True
                    need[op.idx].append(d)
                elif op.eng != "pe" or op.force:
                    dep.signal = True
                    need[op.idx].append(d)
        sems = {}
        for e in ("pe", "act", "dve", "pool", "sp"):
            sems[e] = es.enter_context(nc.semaphore("s_" + e))
        streams = {}
        for op in ops:
            if op.dma and op.stream not in streams:
                streams[op.stream] = es.enter_context(nc.semaphore("d_" + op.stream))
        cnt = {k: 0 for k in sems}
        scnt = {k: 0 for k in streams}
        waited = {e: {} for e in self.engs}
        order = self.schedule()
        for oi in order:
            op = ops[oi]
            wl = {}
            for d in need[op.idx]:
                dep = ops[d]
                if dep.dma:
                    key, val = ("d", dep.stream), 16 * dep.sig_idx
                else:
                    key, val = ("e", dep.eng), dep.sig_idx
                if wl.get(key, 0) < val:
                    wl[key] = val
            eng = self.engs[op.eng]
            wd = waited[op.eng]
            pend = []
            for key, val in wl.items():
                if wd.get(key, 0) >= val:
                    continue
                wd[key] = val
                sem = streams[key[1]] if key[0] == "d" else sems[key[1]]
                pend.append((sem, val))
            attach = None
            if pend and not op.dma and not os.environ.get("KNOATTACH"):
                attach = pend.pop()
            for sem, val in pend:
                eng.wait_ge(sem, val)
            ins = op.fn(eng)
            if attach is not None:
                ins._wait_ge(attach[0], attach[1])
            self.n_wait = getattr(self, "n_wait", 0) + len(pend)
            self.n_att = getattr(self, "n_att", 0) + (1 if attach else 0)
            if op.dma:
                scnt[op.stream] += 1
                op.sig_idx = scnt[op.stream]
                ins.then_inc(streams[op.stream], 16)
            elif op.signal:
                cnt[op.eng] += 1
                op.sig_idx = cnt[op.eng]
                ins.then_inc(sems[op.eng], 1)
            else:
                op.sig_idx = cnt[op.eng] + 1
        return len(ops)


def MM(out, lhsT, rhs, start, stop):
    return _tag(lambda e: e.matmul(out, lhsT, rhs, start=start, stop=stop), out)

def TR(out, in_, ident):
    return _tag(lambda e: e.transpose(out, in_, ident), out)

def _fs(ap):
    n = 1
    for d in ap.shape[1:]:
        n *= int(d)
    return n

def _tag(fn, out):
    fn.free = _fs(out)
    return fn

def ACTF(out, in_, func, bias=None, scale=None, accum=None):
    kw = {}
    if bias is not None: kw["bias"] = bias
    if scale is not None: kw["scale"] = scale
    if accum is not None: kw["accum_out"] = accum
    fn = _tag(lambda e: e.activation(out=out, in_=in_, func=func, **kw), out)
    fn.tset = {AF.Sigmoid: 1, AF.Silu: 2, AF.Exp: 3, AF.Ln: 3}.get(func, 0)
    return fn

def TS(out, in0, s1, s2, op0, op1=None):
    if op1 is None:
        return _tag(lambda e: e.tensor_scalar(out=out, in0=in0, scalar1=s1, scalar2=None, op0=op0), out)
    return _tag(lambda e: e.tensor_scalar(out=out, in0=in0, scalar1=s1, scalar2=s2, op0=op0, op1=op1), out)

def TT(out, in0, in1, op):
    return _tag(lambda e: e.tensor_tensor(out=out, in0=in0, in1=in1, op=op), out)

def STT(out, in0, scalar, in1, op0, op1):
    fn = _tag(lambda e: e.scalar_tensor_tensor(out=out, in0=in0, scalar=scalar, in1=in1, op0=op0, op1=op1), out)
    fn.free *= 1.6
    return fn

def CP(out, in_):
    return _tag(lambda e: e.tensor_copy(out=out, in_=in_), out)

def RCP(out, in_):
    return _tag(lambda e: e.reciprocal(out=out, in_=in_), out)

def MSET(out, v):
    return lambda e: e.memset(out, v)

def CPA(out, in_):
    return _tag(lambda e: e.copy(out=out, in_=in_), out)

def MULA(out, in_, m):
    return _tag(lambda e: e.mul(out=out, in_=in_, mul=m), out)


def _tile_w(w, cols):
    out = np.zeros((128, 8, 512), np.float32)
    sub = w[:, cols]
    out[:, :, :sub.shape[1]] = sub.reshape(8, 128, -1).transpose(1, 0, 2)
    return out


def _build_wall(w_in, w_branch, w_out, w_mem_kv):
    ar = np.arange
    groups = []
    for j in range(8):
        groups.append(_tile_w(w_in, np.concatenate([o + j * 128 + ar(128) for o in (O_H, O_B, O_C, O_GC)])))
    for j in range(8):
        groups.append(_tile_w(w_in, np.concatenate([o + j * 128 + ar(128) for o in (O_R, O_K, O_V, O_GR)])))
    groups.append(_tile_w(w_in, O_WD + ar(128)))
    for h in range(4):
        groups.append(_tile_w(w_in, np.concatenate([O_Q + h * 256 + ar(256), O_GM + h * 256 + ar(256)])))
    for o in (O_MC, O_MR, O_MM):
        for hf in range(2):
            groups.append(_tile_w(w_in, o + hf * 512 + ar(512)))
    for i in range(3):
        for hf in range(2):
            groups.append(_tile_w(w_branch[i], hf * 512 + ar(512)))
    for hf in range(2):
        groups.append(_tile_w(w_out, hf * 512 + ar(512)))
    for q in range(4):
        groups.append(_tile_w(w_mem_kv, q * 512 + ar(512)))
    assert len(groups) == NGROUPS
    return np.ascontiguousarray(np.stack(groups))


def _fm(v):
    return np.ascontiguousarray(v.reshape(-1, 128).T)


def _host_inputs(inp):
    f = lambda a: np.ascontiguousarray(np.asarray(a, dtype=np.float32))
    x_prompt, x_sample, mem_prompt = f(inp["x_prompt"]), f(inp["x_sample"]), f(inp["mem_prompt"])
    ck, cv = f(inp["cache_mem_k"])[0], f(inp["cache_mem_v"])[0]
    sconv, sshift, swkv = f(inp["state_conv"])[0], f(inp["state_shift"])[0], f(inp["state_wkv"])[0]
    wall = _build_wall(f(inp["w_in"])[0], f(inp["w_branch"])[0], f(inp["w_out"])[0], f(inp["w_mem_kv"])[0])
    vecs = np.zeros((128, NV), np.float32)
    cw = f(inp["conv_w"])[0]
    for off, v in ((V_NORM_IN, f(inp["norm_in"])[0]), (V_CW0, cw[0]), (V_CW1, cw[1]), (V_CW2, cw[2]),
                   (V_CB, f(inp["conv_b"])[0]), (V_MU, f(inp["shift_mu"])[0]), (V_W0, f(inp["decay_w0"])[0]),
                   (V_A0, f(inp["icl_a0"])[0]), (V_KK, f(inp["k_k"])[0]), (V_KA, f(inp["k_a"])[0]),
                   (V_RK, f(inp["r_k"])[0].reshape(-1)), (V_LNG, f(inp["ln_x_g"])[0]), (V_LNB, f(inp["ln_x_b"])[0]),
                   (V_NMEM, f(inp["norm_mem"])[0])):
        m = _fm(v)
        vecs[:, off:off + m.shape[1]] = m
    gfin = np.ascontiguousarray(np.broadcast_to(f(inp["norm_final"])[None, :], (128, D)))
    lora = np.ascontiguousarray(np.concatenate([f(inp["decay_up"])[0], f(inp["icl_up"])[0]], axis=0))
    consts = _make_consts()
    maps = []
    for c in range(NCORES):
        xall = np.concatenate([x_prompt[c].reshape(16, 128, D),
                               x_sample[NB * c:NB * (c + 1)].reshape(1, 128, D)], axis=0)
        maps.append({
            "xall": np.ascontiguousarray(xall),
            "mem": np.ascontiguousarray(mem_prompt[c].reshape(2, 128, D)),
            "ck": np.ascontiguousarray(ck[NB * c:NB * (c + 1)].reshape(NB, 256, D)),
            "cv": np.ascontiguousarray(cv[NB * c:NB * (c + 1)].reshape(NB, 256, D)),
            "sconv": np.ascontiguousarray(sconv[NB * c:NB * (c + 1)].reshape(NB * 2, D)),
            "sshift": np.ascontiguousarray(sshift[NB * c:NB * (c + 1)]),
            "swkv": np.ascontiguousarray(swkv[NB * c:NB * (c + 1)]),
            "wall": wall, "vecs": vecs, "gfin": gfin, "lora": lora, "consts": consts,
        })
    return maps


C_IDENT, C_TRIL_S, C_TRIL_I, C_BD_S, C_BD_I, C_BLK, C_RESET, C_LT, C_BD_LT = 0, 128, 256, 384, 512, 640, 768, 896, 1024
NCONST = 1152


def _make_consts():
    c = np.zeros((128, NCONST), np.float32)
    i = np.arange(128)
    c[:, C_IDENT:C_IDENT + 128] = np.eye(128)
    c[:, C_TRIL_S:C_TRIL_S + 128] = (i[:, None] < i[None, :])
    c[:, C_TRIL_I:C_TRIL_I + 128] = (i[:, None] <= i[None, :])
    same = (i[:, None] // NT) == (i[None, :] // NT)
    c[:, C_BD_S:C_BD_S + 128] = (i[:, None] < i[None, :]) & same
    c[:, C_BD_I:C_BD_I + 128] = (i[:, None] <= i[None, :]) & same
    c[:, C_BLK:C_BLK + 128] = (i[:, None] // 64) == (i[None, :] // 64)
    c[:, C_RESET:C_RESET + 16] = (i[:, None] // NT) == np.arange(16)[None, :]
    c[:, C_LT:C_LT + 128] = (i[None, :] < i[:, None])
    c[:, C_BD_LT:C_BD_LT + 128] = (i[None, :] < i[:, None]) & same
    return c


NTP = 1152
PASS_TILES = [list(range(8)) + [16], list(range(8, 16))]
PASS_TBLK = [[(0, 512), (512, 512), (1024, 128)], [(0, 512), (512, 512)]]


def build_nc(stages=("rwkv", "conv", "mem")):
    nc = bass.Bass("TRN2", target_bir_lowering=False)
    din = lambda name, shape: nc.dram_tensor(name, list(shape), F32, kind="ExternalInput").ap()
    dout = lambda name, shape: nc.dram_tensor(name, list(shape), F32, kind="ExternalOutput").ap()
    xall_d = din("xall", (17, 128, D)); mem_d = din("mem", (2, 128, D))
    ck_d = din("ck", (NB, 256, D)); cv_d = din("cv", (NB, 256, D))
    sconv_d = din("sconv", (NB * 2, D)); sshift_d = din("sshift", (NB, 3200)); swkv_d = din("swkv", (NB, 16, 64, 64))
    wall_d = din("wall", (NGROUPS, 128, 8, 512)); vecs_d = din("vecs", (128, NV)); gfin_d = din("gfin", (128, D))
    lora_d = din("lora", (128, 1024)); consts_d = din("consts", (128, NCONST))
    y_d = dout("y", (17, 128, D)); mk_d = dout("mk", (256, D)); mv_d = dout("mv", (256, D))
    convp_d = dout("convp", (2, D)); shiftp_d = dout("shiftp", (1, 3200)); wkvp_d = dout("wkvp", (16, 64, 64))
    convs_d = dout("convs", (NB * 2, D)); shifts_d = dout("shifts", (NB, 3200)); wkvs_d = dout("wkvs", (NB, 16, 64, 64))

    with ExitStack() as es:
        sb = lambda name, shape, dt=F32: es.enter_context(nc.sbuf_tensor("s_" + name, list(shape), dt))
        P = Prog(nc)
        vecs = sb("vecs", (128, NV)); consts = sb("consts", (128, NCONST)); gfin = sb("gfin", (128, D))
        identb = sb("identb", (128, 128), BF16); onesb = sb("onesb", (128, 128), BF16)
        xt = sb("xt", (128, D)); xnb = sb("xnb", (128, D), BF16)
        xt2 = sb("xt2", (128, D)); xnb2 = sb("xnb2", (128, D), BF16)
        ssb = sb("ssb", (128, 16))
        xnT = sb("xnT", (128, 8, NTP), BF16); yT = sb("yT", (128, 8, NTP), BF16)
        acc = sb("acc", (128, 8, NTP))
        NS = 4
        ring = [sb(f"ring{s}", (128, 8, 512), BF16) for s in range(NS)]
        NTMP = 12
        tmps = [sb(f"tmp{i}", (128, 512)) for i in range(NTMP)]
        memnT = sb("memnT", (128, 8, 256), BF16); KT = sb("KT", (128, 8, 256), BF16); Vp = sb("Vp", (128, 2, D), BF16)
        U = sb("U", (128, 1026)); us = sb("us", (128, NB, 10)); ucarry = sb("ucarry", (128, 8, 2)); uls = sb("uls", (128, 8, NB, 2))
        scT = sb("scT", (128, 8, NB * 2))
        qTs = sb("qTs", (128, 8, 128), BF16); sgms = sb("sgms", (128, 8, 128))
        Kb = sb("Kb", (128, 2, D), BF16); Vb = sb("Vb", (128, 2, D), BF16); KTb = sb("KTb", (128, 8, 256), BF16)
        ps = [es.enter_context(nc.psum_tensor(f"ps{i}", [128, 512], F32)) for i in range(8)]
        psk = [f"ps{i}" for i in range(8)]
        ident = consts[:, C_IDENT:C_IDENT + 128]
        psT = ps[7][:, :].bitcast(BF16)

        tctr = [0]
        tmp_excl = set()
        def tmp():
            i = tctr[0] % NTMP; tctr[0] += 1
            while i in tmp_excl:
                i = tctr[0] % NTMP; tctr[0] += 1
            return tmps[i], f"tmp{i}"

        vcol = lambda off, j: vecs[:, off + j:off + j + 1]

        P.dma("sp", vecs[:], vecs_d[:, :], w=["vecs"], stream="c_vecs")
        P.dma("sp", consts[:], consts_d[:, :], w=["consts"], stream="c_consts")
        P.dma("sp", gfin[:], gfin_d[:, :], w=["gfin"], stream="c_gfin")
        P.dve(CP(identb[:], ident), r=["consts"], w=["identb"])
        P.dve(MSET(onesb[:], 1.0), w=["onesb"])
        P.dve(MSET(ucarry[:], 0.0), w=["ucarry"])

        KV_LATE = "mem" in stages
        sched = [] if KV_LATE else [G_KV + q for q in range(4)]
        for p_ in range(2):
            if "rwkv" in stages:
                sched += [G_LORA] + [G_RWKV + j for j in range(8)] + [G_GATE + 2, G_BR + 2, G_GATE + 3, G_BR + 3]
            if "conv" in stages:
                sched += [G_CONV + j for j in range(8)] + [G_GATE + 0, G_BR + 0, G_GATE + 1, G_BR + 1]
            if "mem" in stages:
                if p_ == 0:
                    sched += [G_KV + q for q in range(4)]
                sched += [G_MEM + h for h in range(4)] + [G_GATE + 4, G_BR + 4, G_GATE + 5, G_BR + 5]
            if 'noout' not in stages:
                sched += [G_OUT, G_OUT + 1]
        wst = {"i": 0, "loaded": 0}
        def w_prefetch(upto):
            while wst["loaded"] < min(upto, len(sched)):
                k = wst["loaded"]; s = k % NS
                P.dma("pool", ring[s][:], wall_d[sched[k]], w=[f"ring{s}"], stream=f"ring{s}", lat=14.0)
                wst["loaded"] += 1
        def w_acquire(g):
            if not hasattr(P, "marks"):
                P.marks = []
            P.marks.append((f"grp{g}", len(P.ops)))
            i = wst["i"]
            if os.environ.get('RCUT'):
                sched[i] = g
            assert sched[i] == g, (i, sched[i], g)
            w_prefetch(i + NS - 1)
            wst["i"] += 1
            return i % NS

        def mm8(bank, bkey, slot, c0, rhs_of_kc, rkeys):
            for kc in range(8):
                P.add("pe", MM(bank, ring[slot][:, kc, c0:c0 + 128], rhs_of_kc(kc), kc == 0, kc == 7),
                      r=[f"ring{slot}"] + rkeys, w=[bkey], cost=0.06 + 0.00041 * int(bank.shape[-1]))

        nctr = [0]
        def norm_T(src, voff, dst3, dkeys):
            q = nctr[0] % 2; nctr[0] += 1
            xt_, kx = (xt, "xt") if q == 0 else (xt2, "xt2")
            xn_, kn = (xnb, "xnb") if q == 0 else (xnb2, "xnb2")
            pT_ = ps[7 - q][:, :].bitcast(BF16); kp = psk[7 - q]
            o = 8 * q
            P.dma("sp", xt_[:], src, w=[kx], stream=kx)
            P.act(ACTF(xn_[:], xt_[:], AF.Square, accum=ssb[:, o:o + 1]), r=[kx], w=[kn, f"ss{o}"])
            P.act(ACTF(ssb[:, o + 1:o + 2], ssb[:, o:o + 1], AF.Ln, bias=EPS, scale=1.0 / D), r=[f"ss{o}"], w=[f"ss{o + 1}"])
            P.act(ACTF(ssb[:, o + 2:o + 3], ssb[:, o + 1:o + 2], AF.Exp, scale=-0.5), r=[f"ss{o + 1}"], w=[f"ss{o + 2}"])
            P.dve(TS(xn_[:], xt_[:], ssb[:, o + 2:o + 3], None, ALU.mult), r=[kx, f"ss{o + 2}"], w=[kn])
            for c in range(8):
                P.pe(TR(pT_[:, c * 128:(c + 1) * 128], xn_[:, c * 128:(c + 1) * 128], identb[:]),
                     r=[kn, "identb"], w=[kp])
            for c in range(8):
                if c % 2 == 0:
                    P.dve(TS(dst3[:, c, :], pT_[:, c * 128:(c + 1) * 128], vcol(voff, c), None, ALU.mult),
                          r=[kp, "vecs"], w=dkeys)
                else:
                    P.act(MULA(dst3[:, c, :], pT_[:, c * 128:(c + 1) * 128], vcol(voff, c)), r=[kp, "vecs"], w=dkeys)

        LVL = int(os.environ.get('KCUT', '99'))

        t0, k0 = tmp()
        P.dma("sp", t0[0:32, :], sconv_d[:, 0:512], w=[k0], stream="c_sc0")
        t1, k1 = tmp()
        P.dma("sp", t1[0:32, :], sconv_d[:, 512:1024], w=[k1], stream="c_sc1")
        for j in range(8 if LVL >= 2 else 0):
            src_t = t0 if j < 4 else t1
            P.pe(TR(ps[6][:, j * 32:(j + 1) * 32], src_t[0:32, (j % 4) * 128:(j % 4 + 1) * 128], ident[0:32, 0:32]),
                 r=[k0, k1, "consts"], w=[psk[6]])
        if LVL >= 2:
            P.act(CPA(scT[:].rearrange("p j s -> p (j s)"), ps[6][:, 0:256]), r=[psk[6]], w=["scT"])

        def kv_phase():
            for i in range(2):
                norm_T(mem_d[i], V_NMEM, memnT[:, :, i * 128:(i + 1) * 128], ["memnT"])
            for q in range(4 if LVL >= 3 else 0):
                s = w_acquire(G_KV + q)
                KSUB = int(os.environ.get('KSUB', '9'))
                if KSUB < 1:
                    P.pe(MM(ps[0][:, 0:128], ring[s][:, 0, 0:128], memnT[:, 0, 0:128], True, True), r=[f'ring{s}', 'memnT'], w=[psk[0]])
                    continue
                if q < 2:
                    for ct in range(4):
                        bank, bk = ps[ct % 2], psk[ct % 2]
                        mm8(bank[:, 0:256], bk, s, ct * 128, lambda kc: memnT[:, kc, :], ["memnT"])
                        P.act(CPA(KT[:, q * 4 + ct, :], bank[:, 0:256]), r=[bk], w=["KT"])
                for mt in range(2 if KSUB >= 2 else 0):
                    bank, bk = ps[2 + mt], psk[2 + mt]
                    for kc in range(8):
                        P.pe(MM(bank[:, :], memnT[:, kc, mt * 128:(mt + 1) * 128], ring[s][:, kc, :], kc == 0, kc == 7),
                             r=[f"ring{s}", "memnT"], w=[bk])
                    t, tk = tmp()
                    P.dve(CP(t[:], bank[:, :]), r=[bk], w=[tk])
                    if q >= 2:
                        P.act(CPA(Vp[:, mt, (q - 2) * 512:(q - 1) * 512], bank[:, :]), r=[bk], w=["Vp"])
                    dst = (mk_d if q < 2 else mv_d)[mt * 128:(mt + 1) * 128, (q % 2) * 512:(q % 2 + 1) * 512]
                    P.dma("sp", dst, t[:], r=[tk], w=[f"o_mkv{q}{mt}"], stream="o_" + tk)


        if not KV_LATE:
            kv_phase()

        def branch(i, first, tblk):
            for hf in range(2):
                sg = w_acquire(G_GATE + 2 * i + hf)
                sw = w_acquire(G_BR + 2 * i + hf)
                cnt = 0
                for ct in range(4):
                    j = hf * 4 + ct
                    for tb, (c0, n) in enumerate(tblk):
                        par = cnt % 4; cnt += 1
                        bA, kA = ps[par * 2], psk[par * 2]
                        bB, kB = ps[par * 2 + 1], psk[par * 2 + 1]
                        mm8(bA[:, 0:n], kA, sg, ct * 128, lambda kc: xnT[:, kc, c0:c0 + n], [f"xnT{tb}"])
                        mm8(bB[:, 0:n], kB, sw, ct * 128, lambda kc: yT[:, kc, c0:c0 + n], [f"yT{tb}"])
                        t, tk = tmp()
                        P.act(ACTF(t[:, 0:n], bA[:, 0:n], AF.Sigmoid), r=[kA], w=[tk])
                        ak = f"acc{j}_{tb}"
                        if first:
                            P.dve(TT(acc[:, j, c0:c0 + n], bB[:, 0:n], t[:, 0:n], ALU.mult), r=[kB, tk], w=[ak])
                        else:
                            t2, tk2 = tmp()
                            P.dve(TT(t2[:, 0:n], bB[:, 0:n], t[:, 0:n], ALU.mult), r=[kB, tk], w=[tk2])
                            P.pool(TT(acc[:, j, c0:c0 + n], acc[:, j, c0:c0 + n], t2[:, 0:n], ALU.add), r=[ak, tk2], w=[ak])

        def conv_phase(p_, tblk):
            for j in range(8):
                s = w_acquire(G_CONV + j)
                P.pool(CP(U[:, 0:2], ucarry[:, j, :]), r=["ucarry"], w=["U"])
                if p_ == 0:
                    P.pool(CP(us[:, :, 0:2], scT[:, j, :].rearrange("p (b s) -> p b s", s=2)), r=["scT"], w=["us"])
                for tb, (c0, n) in enumerate(tblk):
                    o = (tb % 2) * 4
                    bh, bB, bC, bg = ps[o:o + 4]; kh, kB, kC, kg = psk[o:o + 4]
                    for ct, (bank, bk) in enumerate(((bh, kh), (bB, kB), (bC, kC), (bg, kg))):
                        mm8(bank[:, 0:n], bk, s, ct * 128, lambda kc: xnT[:, kc, c0:c0 + n], [f"xnT{tb}"])
                    th, tkh = tmp()
                    P.act(CPA(th[:, 0:n], bh[:, 0:n]), r=[kh], w=[tkh])
                    if n == 512:
                        ucur, um1, um2 = U[:, 2 + c0:2 + c0 + n], U[:, 1 + c0:1 + c0 + n], U[:, c0:c0 + n]
                        v3 = lambda a: a
                        ukey = "U"
                    else:
                        ucur, um1, um2 = us[:, :, 2:10], us[:, :, 1:9], us[:, :, 0:8]
                        v3 = lambda a: a.rearrange("p (b t) -> p b t", t=NT)
                        ukey = "us"
                    P.dve(TT(ucur, v3(bC[:, 0:n]), v3(th[:, 0:n]), ALU.mult), r=[kC, tkh], w=[ukey])
                    a1, ka1 = tmp()
                    P.dve(TS(v3(a1[:, 0:n]), ucur, vcol(V_CW2, j), vcol(V_CB, j), ALU.mult, ALU.add), r=[ukey, "vecs"], w=[ka1])
                    a2, ka2 = tmp()
                    P.dve(STT(v3(a2[:, 0:n]), um1, vcol(V_CW1, j), v3(a1[:, 0:n]), ALU.mult, ALU.add), r=[ukey, ka1, "vecs"], w=[ka2])
                    a3, ka3 = tmp()
                    P.dve(STT(v3(a3[:, 0:n]), um2, vcol(V_CW0, j), v3(a2[:, 0:n]), ALU.mult, ALU.add), r=[ukey, ka2, "vecs"], w=[ka3])
                    sg, ksg = tmp()
                    P.act(ACTF(sg[:, 0:n], bg[:, 0:n], AF.Silu), r=[kg], w=[ksg])
                    a4, ka4 = tmp()
                    P.dve(TT(a4[:, 0:n], bB[:, 0:n], a3[:, 0:n], ALU.mult), r=[kB, ka3], w=[ka4])
                    P.pool(TT(yT[:, j, c0:c0 + n], a4[:, 0:n], sg[:, 0:n], ALU.mult), r=[ka4, ksg], w=[f"yT{tb}"])
                P.pool(CP(ucarry[:, j, :], U[:, 1024:1026]), r=["U"], w=["ucarry"])
                if p_ == 0:
                    P.pool(CP(uls[:, j, :, :], us[:, :, 8:10]), r=["us"], w=["uls"])
            for hf in range(2):
                for jj in range(4):
                    j = hf * 4 + jj
                    if p_ == 1:
                        P.pe(TR(ps[0][0:2, jj * 128:(jj + 1) * 128], ucarry[:, j, :], ident), r=["ucarry", "consts"], w=[psk[0]])
                    else:
                        P.pe(TR(ps[1][0:32, jj * 128:(jj + 1) * 128], uls[:, j, :, :].rearrange("p b s -> p (b s)"), ident),
                             r=["uls", "consts"], w=[psk[1]])
                t, tk = tmp()
                if p_ == 1:
                    P.dve(CP(t[0:2, :], ps[0][0:2, :]), r=[psk[0]], w=[tk])
                    P.dma("sp", convp_d[:, hf * 512:(hf + 1) * 512], t[0:2, :], r=[tk], w=[f"o_convp{hf}"], stream="o_" + tk)
                else:
                    P.dve(CP(t[0:32, :], ps[1][0:32, :]), r=[psk[1]], w=[tk])
                    P.dma("sp", convs_d[:, hf * 512:(hf + 1) * 512], t[0:32, :], r=[tk], w=[f"o_convs{hf}"], stream="o_" + tk)

        def mem_phase(p_, tblk):
            for h in range(4):
                s = w_acquire(G_MEM + h)
                for tb, (c0, n) in enumerate(tblk):
                    bq = [ps[0], ps[1]]; bg = [ps[2], ps[3]]
                    for ct in range(4):
                        mm8(ps[ct][:, 0:n], psk[ct], s, ct * 128, lambda kc: xnT[:, kc, c0:c0 + n], [f"xnT{tb}"])
                    if n == 128:
                        for dc in range(2):
                            P.act(CPA(qTs[:, 2 * h + dc, :], bq[dc][:, 0:128]), r=[psk[dc]], w=["qTs"])
                            P.act(ACTF(sgms[:, 2 * h + dc, :], bg[dc][:, 0:128], AF.Silu), r=[psk[2 + dc]], w=["sgms"])
                        continue
                    qt, kq = tmp()
                    qT = qt[:, :].bitcast(BF16)
                    sg0, ks0 = tmp(); sg1, ks1 = tmp()
                    for dc in range(2):
                        P.act(CPA(qT[:, dc * 512:(dc + 1) * 512], bq[dc][:, :]), r=[psk[dc]], w=[kq])
                    P.act(ACTF(sg0[:, :], bg[0][:, :], AF.Silu), r=[psk[2]], w=[ks0])
                    P.act(ACTF(sg1[:, :], bg[1][:, :], AF.Silu), r=[psk[3]], w=[ks1])
                    et, ke = tmp()
                    E = et[:, :].bitcast(BF16)
                    for mt in range(2):
                        bS, kS = ps[4 + mt], psk[4 + mt]
                        for dc in range(2):
                            P.pe(MM(bS[:, :], KT[:, 2 * h + dc, mt * 128:(mt + 1) * 128], qT[:, dc * 512:(dc + 1) * 512],
                                    dc == 0, dc == 1), r=["KT", kq], w=[kS])
                        P.act(ACTF(E[:, mt * 512:(mt + 1) * 512], bS[:, :], AF.Exp, scale=1.0 / 16.0), r=[kS], w=[ke])
                    for mt in range(2):
                        P.pe(MM(ps[6][:, :], onesb[:], E[:, mt * 512:(mt + 1) * 512], mt == 0, mt == 1),
                             r=["onesb", ke], w=[psk[6]])
                    rd, krd = tmp()
                    P.dve(RCP(rd[:, :], ps[6][:, :]), r=[psk[6]], w=[krd])
                    for dvc in range(2):
                        for mt in range(2):
                            P.pe(MM(ps[7][:, :], Vp[:, mt, h * 256 + dvc * 128:h * 256 + (dvc + 1) * 128],
                                    E[:, mt * 512:(mt + 1) * 512], mt == 0, mt == 1), r=["Vp", ke], w=[psk[7]])
                        o1, ko1 = tmp()
                        P.dve(TT(o1[:, :], ps[7][:, :], rd[:, :], ALU.mult), r=[psk[7], krd], w=[ko1])
                        sgd, ksd = (sg0, ks0) if dvc == 0 else (sg1, ks1)
                        P.pool(TT(yT[:, 2 * h + dvc, c0:c0 + n], o1[:, :], sgd[:, :], ALU.mult), r=[ko1, ksd], w=[f"yT{tb}"])
            if p_ != 0:
                return
            tmp_excl.update({4, 5, 6, 7})
            Kset = [[(Kb[:, 0, :], "Kb0"), (Kb[:, 1, :], "Kb1")],
                    [(tmps[4][:, :].bitcast(BF16), "tmp4"), (tmps[5][:, :].bitcast(BF16), "tmp5")]]
            Vset = [[(Vb[:, 0, :], "Vb0"), (Vb[:, 1, :], "Vb1")],
                    [(tmps[6][:, :].bitcast(BF16), "tmp6"), (tmps[7][:, :].bitcast(BF16), "tmp7")]]
            for b in range(NB):
                Kc, Vc = Kset[b % 2], Vset[b % 2]
                for mt in range(2):
                    P.dma("pool", Kc[mt][0], ck_d[b, mt * 128:(mt + 1) * 128, :], w=[Kc[mt][1]], stream="k_" + Kc[mt][1], lat=8.0)
                    P.dma("pool", Vc[mt][0], cv_d[b, mt * 128:(mt + 1) * 128, :], w=[Vc[mt][1]], stream="k_" + Vc[mt][1], lat=8.0)
                for mt in range(2):
                    for dch in range(8):
                        P.pe(TR(psT[:, dch * 128:(dch + 1) * 128], Kc[mt][0][:, dch * 128:(dch + 1) * 128], identb[:]),
                             r=[Kc[mt][1], "identb"], w=[psk[7]])
                    P.act(CPA(KTb[:, :, mt * 128:(mt + 1) * 128], psT.rearrange("p (c t) -> p c t", c=8)), r=[psk[7]], w=["KTb"])
                bS, kS = ps[4], psk[4]
                for h in range(4):
                    for mt in range(2):
                        for dc in range(2):
                            P.pe(MM(bS[:, (h * 2 + mt) * 8:(h * 2 + mt + 1) * 8], KTb[:, 2 * h + dc, mt * 128:(mt + 1) * 128],
                                    qTs[:, 2 * h + dc, b * 8:(b + 1) * 8], dc == 0, dc == 1), r=["KTb", "qTs"], w=[kS])
                et, ke = tmp()
                E = et[:, 0:32].bitcast(BF16)
                P.act(ACTF(E, bS[:, 0:64], AF.Exp, scale=1.0 / 16.0), r=[kS], w=[ke])
                E4 = E.rearrange("p (h m t) -> p h m t", h=4, m=2)
                for mt in range(2):
                    P.pe(MM(ps[5][:, 0:32], onesb[:], E4[:, :, mt, :], mt == 0, mt == 1), r=["onesb", ke], w=[psk[5]])
                rd, krd = tmp()
                P.dve(RCP(rd[:, 0:32], ps[5][:, 0:32]), r=[psk[5]], w=[krd])
                for h in range(4):
                    for dvc in range(2):
                        for mt in range(2):
                            P.pe(MM(ps[6][:, (h * 2 + dvc) * 8:(h * 2 + dvc + 1) * 8],
                                    Vc[mt][0][:, h * 256 + dvc * 128:h * 256 + (dvc + 1) * 128],
                                    E[:, (h * 2 + mt) * 8:(h * 2 + mt + 1) * 8], mt == 0, mt == 1), r=[Vc[mt][1], ke], w=[psk[6]])
                o1, ko1 = tmp()
                rd4 = rd[:, 0:32].rearrange("p (h t) -> p h t", h=4).unsqueeze(2).to_broadcast([128, 4, 2, 8])
                P.dve(TT(o1[:, 0:64].rearrange("p (h d t) -> p h d t", h=4, d=2),
                         ps[6][:, 0:64].rearrange("p (h d t) -> p h d t", h=4, d=2), rd4, ALU.mult), r=[psk[6], krd], w=[ko1])
                P.pool(TT(yT[:, :, 1024 + b * 8:1024 + (b + 1) * 8], o1[:, 0:64].rearrange("p (c t) -> p c t", c=8),
                          sgms[:, :, b * 8:(b + 1) * 8], ALU.mult), r=[ko1, "sgms"], w=["yT2"])
            tmp_excl.clear()

        def out_phase(p_, tblk):
            s0 = w_acquire(G_OUT); s1 = w_acquire(G_OUT + 1)
            for j in range(8):
                for tb, (c0, n) in enumerate(tblk):
                    eng = P.pool if (j + tb) % 2 else P.dve
                    eng(CP(yT[:, j, c0:c0 + n], acc[:, j, c0:c0 + n]), r=[f"acc{j}_{tb}"], w=[f"yT{tb}"])
            for li, gi in enumerate(PASS_TILES[p_]):
                tb = li // 4
                q = li % 2
                xt_, kx = (xt, "xt") if q == 0 else (xt2, "xt2")
                xn_, kn = (xnb, "xnb") if q == 0 else (xnb2, "xnb2")
                o = 8 * q
                P.dma("sp", xt_[:], xall_d[gi], w=[kx], stream=kx)
                hh = (tmp(), tmp())
                for hf, s in enumerate((s0, s1)):
                    bank, bk = ps[(li % 2) * 2 + hf], psk[(li % 2) * 2 + hf]
                    for kc in range(8):
                        P.pe(MM(bank[:, :], yT[:, kc, li * 128:(li + 1) * 128], ring[s][:, kc, :], kc == 0, kc == 7),
                             r=[f"yT{tb}", f"ring{s}"], w=[bk])
                    P.dve(TT(hh[hf][0][:, :], bank[:, :], xt_[:, hf * 512:(hf + 1) * 512], ALU.add), r=[bk, kx], w=[hh[hf][1]])
                    P.act(ACTF(xn_[:, hf * 512:(hf + 1) * 512], hh[hf][0][:, :], AF.Square, accum=ssb[:, o + 4 + hf:o + 5 + hf]),
                          r=[hh[hf][1]], w=[kn, f"ss{o + 4 + hf}"])
                P.dve(TT(ssb[:, o + 6:o + 7], ssb[:, o + 4:o + 5], ssb[:, o + 5:o + 6], ALU.add), r=[f"ss{o + 4}", f"ss{o + 5}"], w=[f"ss{o + 6}"])
                P.act(ACTF(ssb[:, o + 7:o + 8], ssb[:, o + 6:o + 7], AF.Ln, bias=EPS, scale=1.0 / D), r=[f"ss{o + 6}"], w=[f"ss{o + 7}"])
                P.act(ACTF(ssb[:, o + 3:o + 4], ssb[:, o + 7:o + 8], AF.Exp, scale=-0.5), r=[f"ss{o + 7}"], w=[f"ss{o + 3}"])
                for hf in range(2):
                    P.dve(STT(hh[hf][0][:, :], hh[hf][0][:, :], ssb[:, o + 3:o + 4], gfin[:, hf * 512:(hf + 1) * 512], ALU.mult, ALU.mult),
                          r=[hh[hf][1], f"ss{o + 3}", "gfin"], w=[hh[hf][1]])
                    P.dma("sp", y_d[gi][:, hf * 512:(hf + 1) * 512], hh[hf][0][:, :], r=[hh[hf][1]], w=[f"o_y{gi}_{hf}"],
                          stream="o_" + hh[hf][1])

        C0 = float(np.exp(-0.5))
        accf = acc[:].rearrange("p j t -> p (j t)")
        roff = [0]
        def ralloc(n32):
            a = accf[:, roff[0]:roff[0] + n32]; roff[0] += n32
            assert roff[0] <= 8 * NTP
            return a
        shb = [ralloc(514).rearrange("p (a t) -> p a t", a=2) for _ in range(4)]
        shs = [ralloc(144).rearrange("p (b t) -> p b t", t=9) for _ in range(4)]
        wdad = ralloc(576).bitcast(BF16)
        AR = ralloc(512).bitcast(BF16).rearrange("p (a t) -> p a t", a=2)
        BTt = ralloc(256).bitcast(BF16); KTt = ralloc(256).bitcast(BF16); VTt = ralloc(256).bitcast(BF16)
        Ms = ralloc(1024).rearrange("p (b v) -> p b v", v=64)
        Msb = ralloc(512).bitcast(BF16).rearrange("p (b v) -> p b v", v=64)
        Sin = ralloc(2048).rearrange("p (b c) -> p b c", c=128)
        UTs = ralloc(128)
        RKEYS = ["shb0_0", "shb1_0", "shb2_0", "shb3_0", "shb0_1", "shb1_1", "shb2_1", "shb3_1", "AR0", "AR1", "BTt0", "BTt1", "KTt0", "KTt1", "VTt0", "VTt1", "pc0", "pc1", "shs0", "shs1", "shs2", "shs3", "wdad", "AR", "BTt", "KTt", "VTt",
                 "Ms", "Msb", "Sin", "UTs"]
        loraW = sb("loraW", (128, D), BF16); blkb = sb("blkb", (128, 128), BF16)
        Mst = sb("Mst", (128, 8, 64)); M0b = sb("M0b", (128, 64), BF16)
        pcarry = sb("pcarry", (128, 25)); pls = sb("pls", (128, 25, NB)); shT = sb("shT", (128, 25, NB))
        pc = sb("pc", (128, 16)); maskP = sb("maskP", (128, 512)); maskS = sb("maskS", (128, 128))
        dummy = sb("dummy", (128, 4))
        big16 = sb("big16", (128, 2688 + 1024), BF16); Utok = sb("Utok", (128, 128), BF16)
        Mtmp = sb("Mtmp", (128, 64))
        Ubm = Kb[:].rearrange("p a n -> p (a n)").rearrange("p (b c) -> p b c", c=128)
        Mmask = consts[:, C_RESET:C_RESET + 16]

        hv = sb("hv", (128, 32))

        def rwkv_setup():
            for q_, off_ in enumerate((V_W0, V_A0, V_LNG, V_LNB)):
                P.dve(TS(hv[:, 8 * q_:8 * q_ + 8], vecs[:, off_:off_ + 8], 0.5, None, ALU.mult), r=["vecs"], w=["hv"])
            P.dma("pool", loraW[:], lora_d[:, :], w=["loraW"], stream="c_lora")
            P.dve(CP(blkb[:], consts[:, C_BLK:C_BLK + 128]), r=["consts"], w=["blkb"])
            P.dve(MSET(Mst[:], 0.0), w=["Mst"])
            P.dve(MSET(pcarry[:], 0.0), w=["pcarry"])
            P.dve(MSET(maskP[:], 1.0), w=["maskP"])
            P.dve(MSET(maskP[:].rearrange("p (c t) -> p c t", t=128)[:, :, 0:1], 0.0), w=["maskP"])
            P.dve(MSET(maskS[:], 1.0), w=["maskS"])
            P.dve(MSET(maskS[:].rearrange("p (c t) -> p c t", t=NT)[:, :, 0:1], 0.0), w=["maskS"])
            for half, (a, b_) in enumerate(((0, 13), (13, 25))):
                t, tk = tmp(); t2, tk2 = tmp(); t3, tk3 = tmp(); t4, tk4 = tmp()
                tl = [(t, tk), (t2, tk2), (t3, tk3), (t4, tk4)]
                for q in range(4):
                    lo = a * 128 + q * 512; hi = min(lo + 512, b_ * 128)
                    if hi > lo:
                        P.dma("sp", tl[q][0][0:NB, 0:hi - lo], sshift_d[:, lo:hi], w=[tl[q][1]], stream="l_" + tl[q][1])
                for ti in range(a, b_):
                    q, o = divmod((ti - a) * 128, 512)
                    P.pe(TR(ps[5][:, (ti - a) * 16:(ti - a + 1) * 16], tl[q][0][0:NB, o:o + 128], ident[0:NB, 0:NB]),
                         r=[tl[q][1], "consts"], w=[psk[5]])
                P.act(CPA(shT[:, a:b_, :].rearrange("p c b -> p (c b)"), ps[5][:, 0:(b_ - a) * 16]), r=[psk[5]], w=["shT"])

        def shift_lerp(bank, bkey, n, bi, tile_idx, out, okey, first, last, par):
            mu = vcol(V_MU, tile_idx)
            d, dk = tmps[11][:, par * 256:(par + 1) * 256], f"tmp11_{par}"
            if n == 256:
                b_, k_ = shb[bi][:, par, :], f"shb{bi}_{par}"
                o_, ok_ = shb[bi][:, 1 - par, :], f"shb{bi}_{1 - par}"
                if first:
                    P.pool(CP(b_[:, 0:1], pcarry[:, tile_idx:tile_idx + 1]), r=["pcarry"], w=[k_])
                else:
                    P.pool(CP(b_[:, 0:1], o_[:, 256:257]), r=[ok_], w=[k_])
                P.act(CPA(b_[:, 1:257], bank[:, 0:256]), r=[bkey], w=[k_])
                P.dve(TT(d[:, :], b_[:, 0:256], b_[:, 1:257], ALU.subtract), r=[k_], w=[dk])
                P.dve(STT(out[:, 0:256], d[:, :], mu, b_[:, 1:257], ALU.mult, ALU.add), r=[dk, k_, "vecs"], w=[okey])
                if last:
                    P.pool(CP(pcarry[:, tile_idx:tile_idx + 1], b_[:, 256:257]), r=[k_], w=["pcarry"])
            else:
                b_, k_ = shs[bi], f"shs{bi}"
                P.pool(CP(b_[:, :, 0:1], shT[:, tile_idx, :].unsqueeze(2)), r=["shT"], w=[k_])
                P.act(CPA(b_[:, :, 1:9], bank[:, 0:128].rearrange("p (b t) -> p b t", t=NT)), r=[bkey], w=[k_])
                d3 = d[:, 0:128].rearrange("p (b t) -> p b t", t=NT)
                P.dve(TT(d3, b_[:, :, 0:8], b_[:, :, 1:9], ALU.subtract), r=[k_], w=[dk])
                P.dve(STT(out[:, 0:128].rearrange("p (b t) -> p b t", t=NT), d3, mu, b_[:, :, 1:9], ALU.mult, ALU.add),
                      r=[dk, k_, "vecs"], w=[okey])
                P.pool(CP(pls[:, tile_idx, :].unsqueeze(2), b_[:, :, 8:9]), r=[k_], w=["pls"])

        TM = lambda i: (tmps[i], f"tmp{i}")
        TMp = lambda i, par: (tmps[i][:, par * 256:(par + 1) * 256], f"tmp{i}_{par}")
        TKEYS = [f"tmp{i}" for i in range(NTMP)] + [f"tmp{i}_{p}" for i in range(NTMP) for p in range(2)]
        pc2 = sb("pc2", (128, 4))
        psT0 = ps[0][:, :].bitcast(BF16)
        Utok2 = sb("Utok2", (128, 128), BF16)
        Vbm = Vb[:].rearrange("p a n -> p (a n)").rearrange("p (b c) -> p b c", c=128)
        Mtmp2 = sgms[:, 0:4, :].rearrange("p a (c v) -> p (a c) v", v=64)

        def rblocks(tblk):
            out = []
            for (c0, n) in tblk:
                if n == 512:
                    out += [(c0, 256), (c0 + 256, 256)]
                else:
                    out.append((c0, n))
            return out

        def lora_part(p_, tblk):
            s = w_acquire(G_LORA)
            rb = rblocks(tblk)
            npb = sum(1 for (_, n) in rb if n == 256)
            for bi_, (c0, n) in enumerate(rb):
                par = bi_ % 2; tb = c0 // 512
                bank, bk = ps[4 * par], psk[4 * par]
                mm8(bank[:, 0:n], bk, s, 0, lambda kc: xnT[:, kc, c0:c0 + n], [f"xnT{tb}"])
                o, ok = TMp(0, par)
                shift_lerp(bank, bk, n, 3, 24, o, ok, bi_ == 0, n == 256 and bi_ == npb - 1, par)
                P.act(ACTF(wdad[0:64, c0:c0 + n], o[0:64, 0:n], AF.Tanh), r=[ok], w=["wdad"])
                P.act(CPA(wdad[64:128, c0:c0 + n], o[64:128, 0:n]), r=[ok], w=["wdad"])

        def prep(j, tb, c0, n, s, first, last, par):
            P.marks.append((f"prep{j}_{tb}", len(P.ops)))
            B = 4 * par
            ARp = AR[:, :, par * 256:(par + 1) * 256]; kAR = f"AR{par}"
            BTp = BTt[:, par * 256:(par + 1) * 256]; kBT = f"BTt{par}"
            KTp = KTt[:, par * 256:(par + 1) * 256]; kKT = f"KTt{par}"
            VTp = VTt[:, par * 256:(par + 1) * 256]; kVT = f"VTt{par}"
            N = slice(0, n)
            for ct in range(4):
                mm8(ps[B + ct][:, 0:n], psk[B + ct], s, ct * 128, lambda kc: xnT[:, kc, c0:c0 + n], [f"xnT{tb}"])
            (rs, krs), (ks, kks), (vs, kvs) = TMp(0, par), TMp(1, par), TMp(2, par)
            (t3, k3), (t4, k4), (t5, k5), (t6, k6), (t7, k7), (t8, k8), (t9, k9) = [TMp(i, par) for i in range(3, 10)]
            P.act(ACTF(t9[:, N], ps[B + 3][:, N], AF.Tanh, scale=0.5), r=[psk[B + 3]], w=[k9])
            P.dve(STT(t9[:, N], t9[:, N], 1.0, ps[B + 3][:, N], ALU.add, ALU.mult), r=[k9, psk[B + 3]], w=[k9])
            shift_lerp(ps[B + 0], psk[B + 0], n, 0, j, rs, krs, first, last, par)
            shift_lerp(ps[B + 1], psk[B + 1], n, 1, 8 + j, ks, kks, first, last, par)
            shift_lerp(ps[B + 2], psk[B + 2], n, 2, 16 + j, vs, kvs, first, last, par)
            P.pe(MM(ps[B + 0][:, 0:n], loraW[0:64, j * 128:(j + 1) * 128], wdad[0:64, c0:c0 + n], True, True),
                 r=["loraW", "wdad"], w=[psk[B + 0]])
            P.pe(MM(ps[B + 1][:, 0:n], loraW[64:128, j * 128:(j + 1) * 128], wdad[64:128, c0:c0 + n], True, True),
                 r=["loraW", "wdad"], w=[psk[B + 1]])
            P.pool(CP(VTp[:, N], vs[:, N]), r=[kvs], w=[kVT])
            P.act(ACTF(t3[:, N], ps[B + 0][:, N], AF.Tanh, bias=hv[:, j:j + 1], scale=0.5), r=[psk[B + 0], "hv"], w=[k3])
            P.act(ACTF(t4[:, N], ps[B + 1][:, N], AF.Tanh, bias=hv[:, 8 + j:9 + j], scale=0.5), r=[psk[B + 1], "hv"], w=[k4])
            P.pool(TS(t3[:, N], t3[:, N], 0.5, 0.5, ALU.mult, ALU.add), r=[k3], w=[k3])
            P.pool(TS(t4[:, N], t4[:, N], 0.5, 0.5, ALU.mult, ALU.add), r=[k4], w=[k4])
            mask, mkey = (maskP, "maskP") if n == 256 else (maskS, "maskS")
            P.dve(lambda e: e.tensor_tensor_scan(out=t5[:, N], data0=mask[:, N], data1=t3[:, N], initial=0.0,
                                                 op0=ALU.mult, op1=ALU.add), r=[mkey, k3], w=[k5])
            P.dve(TT(t3[:, N], t5[:, N], t3[:, N], ALU.subtract), r=[k5, k3], w=[k3])
            P.act(ACTF(t6[:, N], t5[:, N], AF.Exp, scale=-C0), r=[k5], w=[k6])
            P.act(ACTF(t5[:, N], t5[:, N], AF.Exp, scale=C0), r=[k5], w=[k5])
            P.act(ACTF(t3[:, N], t3[:, N], AF.Exp, scale=-C0), r=[k3], w=[k3])
            if n == 256:
                P.pool(CP(pc2[:, 2 * par:2 * par + 2].unsqueeze(2), t6[:, N].rearrange("p (c t) -> p c t", t=128)[:, :, 127:128]), r=[k6], w=[f"pc{par}"])
            else:
                P.pool(CP(pc[:, 0:16].unsqueeze(2), t6[:, N].rearrange("p (c t) -> p c t", t=NT)[:, :, NT - 1:NT]), r=[k6], w=["pc"])
            P.pool(TT(ARp[:, 1, N], rs[:, N], t6[:, N], ALU.mult), r=[krs, k6], w=[kAR])
            P.dve(TS(t6[:, N], ks[:, N], vcol(V_KK, j), None, ALU.mult), r=[kks, "vecs"], w=[k6])
            t7b = t7[:, 0:128].bitcast(BF16)
            P.dve(TT(t7b[:, N], t6[:, N], t6[:, N], ALU.mult), r=[k6], w=[k7])
            P.pe(MM(ps[B + 2][:, N], blkb[:], t7b[:, N], True, True), r=["blkb", k7], w=[psk[B + 2]])
            P.act(ACTF(t7[:, N], ps[B + 2][:, N], AF.Ln, bias=1e-12), r=[psk[B + 2]], w=[k7])
            P.act(ACTF(t7[:, N], t7[:, N], AF.Exp, scale=-0.5), r=[k7], w=[k7])
            P.dve(TT(t6[:, N], t6[:, N], t7[:, N], ALU.mult), r=[k6, k7], w=[k6])
            P.dve(STT(ARp[:, 0, N], t6[:, N], -1.0, t3[:, N], ALU.mult, ALU.mult), r=[k6, k3], w=[kAR])
            P.pool(TT(t3[:, N], t6[:, N], t4[:, N], ALU.mult), r=[k6, k4], w=[k3])
            P.pool(TT(BTp[:, N], t3[:, N], t5[:, N], ALU.mult), r=[k3, k5], w=[kBT])
            P.dve(TS(t7[:, N], t4[:, N], -1.0, vcol(V_KA, j), ALU.add, ALU.mult), r=[k4, "vecs"], w=[k7])
            P.dve(STT(t7[:, N], t7[:, N], 1.0, ks[:, N], ALU.add, ALU.mult), r=[k7, kks], w=[k7])
            P.pool(TT(KTp[:, N], t7[:, N], t5[:, N], ALU.mult), r=[k7, k5], w=[kKT])
            t3b = t3[:, 0:128].bitcast(BF16)
            P.dve(STT(t3b[:, N], rs[:, N], vcol(V_RK, j), t7[:, N], ALU.mult, ALU.mult), r=[krs, k7, "vecs", k3], w=[k3])
            P.pe(MM(ps[B + 3][:, N], blkb[:], t3b[:, N], True, True), r=["blkb", k3], w=[psk[B + 3]])
            P.dve(STT(t8[:, N], ps[B + 3][:, N], 0.5, vs[:, N], ALU.mult, ALU.mult), r=[psk[B + 3], kvs], w=[k8])

        def post(j, tb, c0, n, par):
            P.marks.append((f"post{j}_{tb}", len(P.ops)))
            B = 4 * par
            N = slice(0, n)
            (ot, ko), (t8, k8), (t9, k9) = TMp(10, par), TMp(8, par), TMp(9, par)
            (a0, ka0), (a1, ka1), (a2, ka2) = TMp(0, par), TMp(1, par), TMp(2, par)
            ob = a0[:, 0:128].bitcast(BF16); o2b = a1[:, 0:128].bitcast(BF16)
            P.pool(CP(ob[:, N], ot[:, N]), r=[ko], w=[ka0])
            P.dve(TT(o2b[:, N], ot[:, N], ot[:, N], ALU.mult), r=[ko], w=[ka1])
            P.pe(MM(ps[B + 2][:, N], blkb[:], ob[:, N], True, True), r=["blkb", ka0], w=[psk[B + 2]])
            P.pe(MM(ps[B + 3][:, N], blkb[:], o2b[:, N], True, True), r=["blkb", ka1], w=[psk[B + 3]])
            P.act(MULA(a2[:, N], ps[B + 2][:, N], 1.0 / 64), r=[psk[B + 2]], w=[ka2])
            P.dve(TT(a0[:, N], a2[:, N], a2[:, N], ALU.mult), r=[ka2], w=[ka0])
            P.dve(STT(a1[:, N], ps[B + 3][:, N], 1.0 / 64, a0[:, N], ALU.mult, ALU.subtract), r=[psk[B + 3], ka0], w=[ka1])
            P.act(ACTF(a1[:, N], a1[:, N], AF.Ln, bias=GN_EPS), r=[ka1], w=[ka1])
            P.act(ACTF(a1[:, N], a1[:, N], AF.Exp, scale=-0.5), r=[ka1], w=[ka1])
            P.dve(TT(a0[:, N], ot[:, N], a2[:, N], ALU.subtract), r=[ko, ka2], w=[ka0])
            P.dve(TT(a0[:, N], a0[:, N], a1[:, N], ALU.mult), r=[ka0, ka1], w=[ka0])
            P.dve(TS(a0[:, N], a0[:, N], hv[:, 16 + j:17 + j], hv[:, 24 + j:25 + j], ALU.mult, ALU.add), r=[ka0, "hv"], w=[ka0])
            P.pool(TT(a0[:, N], a0[:, N], t8[:, N], ALU.add), r=[ka0, k8], w=[ka0])
            P.pool(TT(yT[:, j, c0:c0 + n], a0[:, N], t9[:, N], ALU.mult), r=[ka0, k9], w=[f"yT{tb}"])

        DONORS = [(memnT[:].rearrange("p a b -> p (a b)"), 2048), (KTb[:].rearrange("p a b -> p (a b)"), 2048),
                  (qTs[:].rearrange("p a b -> p (a b)"), 1024), (U[:, 0:1024].bitcast(BF16), 2048),
                  (xt[:, :].bitcast(BF16), 2048), (xnb[:, :], 1024), (big16[:, :], 2688 + 1024)]
        DONOR_KEYS = ["memnT", "KTb", "qTs", "U", "xt", "xnb", "Kb", "Vb", "Kb0", "Kb1", "Vb0", "Vb1", "sgms"]
        dst_ = {"d": 0, "o": 0}
        def dalloc(n):
            while DONORS[dst_["d"]][1] - dst_["o"] < n:
                dst_["d"] += 1; dst_["o"] = 0
            a = DONORS[dst_["d"]][0][:, dst_["o"]:dst_["o"] + n]; dst_["o"] += n
            return a
        NU = 8
        uSB = [dalloc(256) for _ in range(NU)]; uSK = [dalloc(256) for _ in range(NU)]
        uXY = [dalloc(256).rearrange("p (a t) -> p a t", a=2) for _ in range(NU)]
        uZ = [dalloc(128) for _ in range(NU)]; uLV = [dalloc(64) for _ in range(NU)]
        cTok = [dalloc(512).rearrange("p (q t) -> p q t", q=4) for _ in range(4)]
        cWT = [dalloc(128) for _ in range(4)]; cU0 = [dalloc(256).bitcast(F32) for _ in range(4)]
        UKEYS = ([f"uSB{u}" for u in range(NU)] + [f"uSK{u}" for u in range(NU)] + [f"uXY{u}" for u in range(NU)] +
                 [f"uZ{u}" for u in range(NU)] + [f"uLV{u}" for u in range(NU)] + [f"cTok{c}" for c in range(4)] +
                 [f"cWT{c}" for c in range(4)] + [f"cU0{c}" for c in range(4)])

        def scan_block(j, nch, sample, ot, ko, par):
            NL = 2 if sample else 6
            mSI = consts[:, C_BD_S:C_BD_S + 256] if sample else consts[:, C_TRIL_S:C_TRIL_S + 256]
            mLT = consts[:, C_BD_LT:C_BD_LT + 128] if sample else consts[:, C_LT:C_LT + 128]
            units = [(c, hh) for c in range(nch) for hh in range(2)]
            B = 4 * par
            UB = 4 * par
            CB = 2 * par
            AR_ = AR[:, :, par * 256:(par + 1) * 256]; kAR = f"AR{par}"
            BT_ = BTt[:, par * 256:(par + 1) * 256]; kBT = f"BTt{par}"
            KT_ = KTt[:, par * 256:(par + 1) * 256]; kKT = f"KTt{par}"
            VT_ = VTt[:, par * 256:(par + 1) * 256]; kVT = f"VTt{par}"
            CS = lambda c: slice(c * 128, (c + 1) * 128)
            SL = lambda hh: slice(64 * hh, 64 * hh + 64)
            for c in range(nch):
                pT = ps[B + c][:, :].bitcast(BF16)
                for q, (src, skey) in enumerate(((AR_[:, 0, CS(c)], kAR), (BT_[:, CS(c)], kBT), (KT_[:, CS(c)], kKT), (VT_[:, CS(c)], kVT))):
                    P.pe(TR(pT[:, q * 128:(q + 1) * 128], src, identb[:]), r=[skey, "identb"], w=[psk[B + c]])
            for c in range(nch):
                pT = ps[B + c][:, :].bitcast(BF16)
                P.act(CPA(cTok[CB + c][:].rearrange("p q t -> p (q t)"), pT[:, 0:512]), r=[psk[B + c]], w=[f"cTok{CB + c}"])
            P.marks.append((f"S2_{j}", len(P.ops)))
            for u, (c, hh) in enumerate(units):
                sl = SL(hh)
                P.pe(MM(ps[B + u][:, 0:256], BT_[sl, CS(c)], AR_[sl, :, CS(c)], True, True), r=[kBT, kAR], w=[psk[B + u]])
                P.pe(MM(ps[B + u][:, 256:512], KT_[sl, CS(c)], AR_[sl, :, CS(c)], True, True), r=[kKT, kAR], w=[psk[B + u]])
            for u, (c, hh) in enumerate(units):
                P.dve(TT(uSB[UB + u][:, :], ps[B + u][:, 0:256], mSI, ALU.mult), r=[psk[B + u], "consts"], w=[f"uSB{UB + u}"])
                P.dve(TT(uSK[UB + u][:, :], ps[B + u][:, 256:512], mSI, ALU.mult), r=[psk[B + u], "consts"], w=[f"uSK{UB + u}"])
            for u, (c, hh) in enumerate(units):
                sl = SL(hh)
                P.pe(MM(ps[B + u][:, 0:128], AR_[sl, 0, CS(c)], BT_[sl, CS(c)], True, True), r=[kBT, kAR], w=[psk[B + u]])
                P.pe(MM(ps[B + u][:, 384:448], uSK[UB + u][:, 0:128], cTok[CB + c][:, 3, sl], True, True), r=[f"uSK{UB + u}", f"cTok{CB + c}"], w=[psk[B + u]])
            for u, (c, hh) in enumerate(units):
                P.dve(TT(uXY[UB + u][:, 1, :], ps[B + u][:, 0:128], mLT, ALU.mult), r=[psk[B + u], "consts"], w=[f"uXY{UB + u}"])
                P.act(CPA(uLV[UB + u][:, :], ps[B + u][:, 384:448]), r=[psk[B + u]], w=[f"uLV{UB + u}"])
                P.pool(CP(uXY[UB + u][:, 0, :], uSB[UB + u][:, 0:128]), r=[f"uSB{UB + u}"], w=[f"uXY{UB + u}"])
                P.pool(TT(uZ[UB + u][:, :], uSB[UB + u][:, 0:128], identb[:], ALU.add), r=[f"uSB{UB + u}", "identb"], w=[f"uZ{UB + u}"])
            P.marks.append((f"S3_{j}", len(P.ops)))
            for n_ in range(1, NL + 1):
                for u in range(len(units)):
                    P.pe(MM(ps[B + u][:, 0:128], uXY[UB + u][:, 1, :], uXY[UB + u][:, 0, :], True, True), r=[f"uXY{UB + u}"], w=[psk[B + u]])
                    P.pe(MM(ps[B + u][:, 128:256], uXY[UB + u][:, 0, :], uXY[UB + u][:, 1, :], True, True), r=[f"uXY{UB + u}"], w=[psk[B + u]])
                for u in range(len(units)):
                    P.act(CPA(uXY[UB + u][:].rearrange("p a t -> p (a t)"), ps[B + u][:, 0:256]), r=[psk[B + u]], w=[f"uXY{UB + u}"])
                for u in range(len(units)):
                    P.pe(MM(ps[B + u][:, 256:384], uXY[UB + u][:, 1, :], uZ[UB + u][:, :], True, True), r=[f"uXY{UB + u}", f"uZ{UB + u}"], w=[psk[B + u]])
                for u in range(len(units)):
                    P.dve(TT(uZ[UB + u][:, :], ps[B + u][:, 256:384], uZ[UB + u][:, :], ALU.add), r=[psk[B + u], f"uZ{UB + u}"], w=[f"uZ{UB + u}"])
            P.marks.append((f"S4_{j}", len(P.ops)))
            for u, (c, hh) in enumerate(units):
                sl = SL(hh)
                P.pe(MM(ps[B + 2 * c][sl, 0:128], cTok[CB + c][:, 0, sl], uZ[UB + u][:, :], True, True), r=[f"cTok{CB + c}", f"uZ{UB + u}"], w=[psk[B + 2 * c]])
                if not sample:
                    P.pe(MM(ps[B + 2 * c + 1][:, 64 * hh:64 * hh + 64], uZ[UB + u][:, :], uLV[UB + u][:, :], True, True), r=[f"uZ{UB + u}", f"uLV{UB + u}"], w=[psk[B + 2 * c + 1]])
                else:
                    P.pe(MM(ps[B + 2 * c + 1][sl, 0:128], uLV[UB + u][:, :], uZ[UB + u][:, :], True, True), r=[f"uZ{UB + u}", f"uLV{UB + u}"], w=[psk[B + 2 * c + 1]])
            for c in range(nch):
                P.act(CPA(cWT[CB + c][:, :], ps[B + 2 * c][:, 0:128]), r=[psk[B + 2 * c]], w=[f"cWT{CB + c}"])
                P.dve(CP(cU0[CB + c][:, :], ps[B + 2 * c + 1][:, 0:128]), r=[psk[B + 2 * c + 1]], w=[f"cU0{CB + c}"])
            P.marks.append((f"S5_{j}", len(P.ops)))
            for c in range(nch):
                tk, tkk = cTok[CB + c], f"cTok{CB + c}"
                bU, kU = ps[B + 0], psk[B + 0]
                if not sample:
                    for hh in range(2):
                        P.pe(MM(bU[:, 64 * hh:64 * hh + 64], cWT[CB + c][SL(hh), :], M0b[SL(hh), :], True, True), r=[f"cWT{CB + c}", "M0b"], w=[kU], force=(hh == 1))
                    P.dve(TT(Utok[:, :], bU[:, 0:128], cU0[CB + c][:, :], ALU.add), r=[kU, f"cU0{CB + c}"], w=["Utok"])
                    ut, utk = Utok, "Utok"
                else:
                    for hh in range(2):
                        for b in range(NB):
                            P.pe(MM(bU[SL(hh), 8 * b:8 * b + 8], Msb[SL(hh), b, :], cWT[CB + c][SL(hh), 8 * b:8 * b + 8], True, True),
                                 r=["Msb", f"cWT{CB + c}"], w=[kU], force=(hh == 1 and b == 0))
                    P.dve(TT(Utok[:, :], bU[:, 0:128], cU0[CB + c][:, :], ALU.add), r=[kU, f"cU0{CB + c}"], w=["Utok"])
                    pT = ps[B + 3][:, :].bitcast(BF16)
                    P.pe(TR(pT[:, 0:128], Utok[:, :], identb[:]), r=["Utok", "identb"], w=[psk[B + 3]])
                    P.act(CPA(Utok2[:, :], pT[:, 0:128]), r=[psk[B + 3]], w=["Utok2"])
                    ut, utk = Utok2, "Utok2"
                bO, kO = ps[B + 1], psk[B + 1]
                for hh in range(2):
                    u = UB + 2 * c + hh; sl = SL(hh)
                    out = bO[sl, 0:128]
                    if not sample:
                        P.pe(MM(out, M0b[sl, :], AR_[sl, 1, CS(c)], True, False), r=["M0b", kAR], w=[kO], force=(hh == 1))
                    else:
                        for b in range(NB):
                            P.pe(MM(bO[sl, 8 * b:8 * b + 8], Msb[sl, b, :], AR_[sl, 1, 8 * b:8 * b + 8], b == 0, False),
                                 r=["Msb", kAR], w=[kO], force=(hh == 1 and b == 0))
                    P.pe(MM(out, tk[:, 3, sl], uSK[u][:, 128:256], False, False), r=[tkk, f"uSK{u}"], w=[kO])
                    P.pe(MM(out, ut[:, sl], uSB[u][:, 128:256], False, True), r=[utk, f"uSB{u}"], w=[kO])
                if not sample:
                    bM, kM = ps[B + 2], psk[B + 2]
                    for hh in range(2):
                        sl = SL(hh)
                        out = bM[sl, 128:192]
                        P.pe(MM(out, tk[:, 2, sl], tk[:, 3, sl], True, False), r=[tkk], w=[kM])
                        P.pe(MM(out, tk[:, 1, sl], ut[:, sl], False, True), r=[tkk, utk], w=[kM])
                    P.dve(TT(Mtmp[:, :], bM[:, 128:192], Mst[:, j, :], ALU.add), r=[kM, "Mst"], w=["Mtmp"])
                    P.dve(TS(M0b[:, :], Mtmp[:, :], pc2[:, 2 * par + c:2 * par + c + 1], None, ALU.mult), r=["Mtmp", f"pc{par}"], w=["M0b"])
                    P.pool(TS(Mst[:, j, :], Mtmp[:, :], pc2[:, 2 * par + c:2 * par + c + 1], None, ALU.mult), r=["Mtmp", f"pc{par}"], w=["Mst"])
                else:
                    mm3 = Mmask.unsqueeze(2).to_broadcast([128, NB, 128])
                    P.dve(TT(Ubm[:], ut[:, :].unsqueeze(1).to_broadcast([128, NB, 128]), mm3, ALU.mult), r=[utk, "consts"], w=["Kb"])
                    P.pool(TT(Vbm[:], tk[:, 3, :].unsqueeze(1).to_broadcast([128, NB, 128]), mm3, ALU.mult), r=[tkk, "consts"], w=["Vb"])
                    for half in range(2):
                        bank, bk = ps[6 + half], psk[6 + half]
                        bs = slice(8 * half, 8 * half + 8)
                        for hh in range(2):
                            sl = SL(hh)
                            out = bank[sl, :].rearrange("p (b v) -> p b v", v=64)
                            P.pe(MM(out, tk[:, 2, sl], Vbm[:, bs, sl], True, False), r=[tkk, "Vb"], w=[bk])
                            P.pe(MM(out, tk[:, 1, sl], Ubm[:, bs, sl], False, True), r=[tkk, "Kb"], w=[bk])
                        P.dve(TT(Mtmp2[:], bank[:, :].rearrange("p (b v) -> p b v", v=64), Ms[:, bs, :], ALU.add), r=[bk, "Ms"], w=["sgms"])
                        P.dve(TT(Ms[:, bs, :], Mtmp2[:], pc[:, bs].unsqueeze(2).to_broadcast([128, 8, 64]), ALU.mult),
                              r=["sgms", "pc"], w=["Ms"])
                P.act(CPA(ot[:, CS(c)], bO[:, 0:128]), r=[kO], w=[ko])

        ACCKEYS = [f"acc{j}_{tb}" for j in range(8) for tb in range(3)]

        def load_sample_state(j):
            for h in range(2):
                P.dma("sp", Sin[0:64, :, h * 64:(h + 1) * 64], swkv_d[:, 2 * j + h].rearrange("b v k -> v b k"),
                      w=["Sin"], stream=f"Sin{h}")
            for half in range(2):
                bank, bk = ps[5 + half], psk[5 + half]
                for bb in range(8):
                    b = half * 8 + bb
                    P.pe(TR(bank[:, bb * 64:(bb + 1) * 64], Sin[0:64, b, :], ident[0:64, 0:64]), r=["Sin", "consts"], w=[bk])
                bs = slice(8 * half, 8 * half + 8)
                P.dve(CP(Ms[:, bs, :], bank[:, :].rearrange("p (b v) -> p b v", v=64)), r=[bk], w=["Ms"])
                P.act(CPA(Msb[:, bs, :], bank[:, :].rearrange("p (b v) -> p b v", v=64)), r=[bk], w=["Msb"])

        def store_sample_state(j):
            for q in range(4):
                bank, bk = ps[5 + q % 2], psk[5 + q % 2]
                for bb in range(4):
                    b = q * 4 + bb
                    P.pe(TR(bank[0:64, bb * 128:(bb + 1) * 128], Ms[:, b, :], ident), r=["Ms", "consts"], w=[bk])
                P.dve(CP(Sin[0:64, 4 * q:4 * q + 4, :], bank[0:64, :].rearrange("p (b c) -> p b c", c=128)), r=[bk], w=["Sin"])
            for h in range(2):
                P.dma("sp", wkvs_d[:, 2 * j + h].rearrange("b v k -> v b k"), Sin[0:64, :, h * 64:(h + 1) * 64],
                      r=["Sin"], w=[f"o_wkvs{j}_{h}"], stream=f"o_Sin{h}")

        def rwkv_phase(p_, tblk):
            P.dve(MSET(dummy[:, 0:1], 0.0), w=ACCKEYS + RKEYS + DONOR_KEYS + UKEYS + TKEYS)
            if p_ == 0:
                rwkv_setup()
                P.dve(MSET(dummy[:, 2:3], 0.0), w=TKEYS)
            RCUT = int(os.environ.get('RCUT', '99'))
            if RCUT >= 1:
                lora_part(p_, tblk)
            npb = sum(1 for (_, n) in tblk if n == 512)
            for j in range(8 if RCUT >= 2 else 0):
                if RCUT < 9 and j > 0:
                    break
                s = w_acquire(G_RWKV + j)
                P.act(CPA(M0b[:, :], Mst[:, j, :]), r=["Mst"], w=["M0b"])
                rb = rblocks(tblk)
                nprb = sum(1 for (_, n) in rb if n == 256)
                for bi_, (c0, n) in enumerate(rb):
                    sample = (n == 128)
                    par = bi_ % 2; tb = c0 // 512
                    if sample:
                        load_sample_state(j)
                    prep(j, tb, c0, n, s, bi_ == 0, bi_ == nprb - 1, par)
                    ot, ko = TMp(10, par)
                    if not sample:
                        scan_block(j, 2, False, ot, ko, par)
                    else:
                        scan_block(j, 1, True, ot, ko, par)
                        store_sample_state(j)
                    post(j, tb, c0, n, par)
            P.dve(MSET(dummy[:, 3:4], 0.0), w=TKEYS)
            if p_ == 0:
                for half, (a, b_) in enumerate(((0, 13), (13, 25))):
                    for ti in range(a, b_):
                        q, o = divmod((ti - a) * 128, 512)
                        P.pe(TR(ps[1 + q][0:NB, o:o + 128], pls[:, ti, :], ident), r=["pls", "consts"], w=[psk[1 + q]])
                    for q in range(4):
                        lo = a * 128 + q * 512; hi = min(lo + 512, b_ * 128)
                        if hi > lo:
                            t, tk_ = tmp()
                            P.dve(CP(t[0:NB, 0:hi - lo], ps[1 + q][0:NB, 0:hi - lo]), r=[psk[1 + q]], w=[tk_])
                            P.dma("sp", shifts_d[:, lo:hi], t[0:NB, 0:hi - lo], r=[tk_], w=[f"o_shs{lo}"], stream="o_" + tk_)
            else:
                P.pe(TR(ps[1][0:25, 0:128], pcarry[:, :], ident), r=["pcarry", "consts"], w=[psk[1]])
                t, tk_ = tmp()
                P.dve(CP(t[0:25, 0:128], ps[1][0:25, 0:128]), r=[psk[1]], w=[tk_])
                P.dma("sp", shiftp_d[0].rearrange("(c p) -> c p", p=128), t[0:25, 0:128], r=[tk_], w=["o_shp"], stream="o_" + tk_)
                for half in range(2):
                    bank, bk = ps[2 + half], psk[2 + half]
                    t, tk_ = tmp()
                    for jj in range(4):
                        P.pe(TR(bank[0:64, jj * 128:(jj + 1) * 128], Mst[:, half * 4 + jj, :], ident), r=["Mst", "consts"], w=[bk])
                    P.dve(CP(t[0:64, :], bank[0:64, :]), r=[bk], w=[tk_])
                    dst = wkvp_d[8 * half:8 * half + 8].rearrange("h v k -> v h k")
                    P.dma("sp", dst, t[0:64, :].rearrange("p (h k) -> p h k", k=64), r=[tk_], w=[f"o_wkvp{half}"], stream="o_" + tk_)
            P.dve(MSET(dummy[:, 1:2], 0.0), w=ACCKEYS + RKEYS + DONOR_KEYS + UKEYS + TKEYS)


        env = dict(locals())
        for p_ in range((2 if 'onepass' not in stages else 1) if LVL >= 4 else 0):
            tblk = PASS_TBLK[p_]
            for li, gi in enumerate(PASS_TILES[p_]):
                norm_T(xall_d[gi], V_NORM_IN, xnT[:, :, li * 128:(li + 1) * 128], [f"xnT{li // 4}"])
            nbr = 0
            if 'cutA' in stages:
                break
            if "rwkv" in stages:
                rwkv_phase(p_, tblk)
                branch(1, nbr == 0, tblk); nbr += 1
            if "conv" in stages:
                conv_phase(p_, tblk)
                branch(0, nbr == 0, tblk); nbr += 1
            if "mem" in stages:
                if p_ == 0:
                    kv_phase()
                mem_phase(p_, tblk)
                branch(2, nbr == 0, tblk); nbr += 1
            if 'noout' not in stages:
                out_phase(p_, tblk)
        okeys = [k for k in P.lastw if k.startswith("o_")]
        P.add("sp", lambda e: e.nop(), r=okeys)
        n_ops = P.emit(es)
        print("ops", n_ops, "est_us", getattr(P, "est_us", None), "tsw", getattr(P, "n_tsw", 0), "waits", getattr(P, "n_wait", 0), "attached", getattr(P, "n_att", 0))
    return nc


def rwkv_phase(env, p_, tblk):
    raise NotImplementedError


_NC_CACHE = {}


def kernel(**inputs):
    stages = inputs.pop("_stages", ("rwkv", "conv", "mem"))
    maps = _host_inputs(inputs)
    key = tuple(stages)
    if key not in _NC_CACHE:
        _NC_CACHE[key] = build_nc(stages)
    nc = _NC_CACHE[key]
    res = run_bass_kernel_spmd(nc, maps, core_ids=list(range(NCORES)))
    R = res.results
    B = NCORES
    y_prompt = np.stack([R[c]["y"][0:16].reshape(T, D) for c in range(B)])
    y_sample = np.concatenate([R[c]["y"][16].reshape(NB, NT, D) for c in range(B)], axis=0)
    mk = np.stack([R[c]["mk"].reshape(256, 4, 256) for c in range(B)])[None]
    mv = np.stack([R[c]["mv"].reshape(256, 4, 256) for c in range(B)])[None]
    conv_p = np.stack([R[c]["convp"] for c in range(B)])[None]
    shift_p = np.stack([R[c]["shiftp"][0] for c in range(B)])[None]
    wkv_p = np.stack([R[c]["wkvp"] for c in range(B)])[None]
    conv_s = np.concatenate([R[c]["convs"].reshape(NB, 2, D) for c in range(B)], axis=0)[None]
    shift_s = np.concatenate([R[c]["shifts"] for c in range(B)], axis=0)[None]
    wkv_s = np.concatenate([R[c]["wkvs"] for c in range(B)], axis=0)[None]
    f = lambda a: np.ascontiguousarray(a, dtype=np.float32)
    return tuple(f(a) for a in (y_prompt, y_sample, mk, mv, conv_p, shift_p, wkv_p, conv_s, shift_s, wkv_s))
```

```python
import os
import numpy as np
from contextlib import ExitStack
import concourse.bass as bass
import concourse.mybir as mybir
from concourse.bass_utils import run_bass_kernel_spmd

F32 = mybir.dt.float32
BF16 = mybir.dt.bfloat16
AF = mybir.ActivationFunctionType
ALU = mybir.AluOpType
AX = mybir.AxisListType

NCORES = 8
D = 1024
T = 2048
NTOK = 2176
NB = 16
NT = 8
TBLK = [(0, 512), (512, 512), (1024, 512), (1536, 512), (2048, 128)]
EPS = 1e-6
GN_EPS = 64 * 1e-5

O_H, O_B, O_C, O_GC, O_R, O_K, O_V, O_WD, O_AD, O_GR, O_Q, O_GM, O_MC, O_MR, O_MM = (
    0, 1024, 2048, 3072, 4096, 5120, 6144, 7168, 7232, 7296, 8320, 9344, 10368, 11392, 12416)
G_CONV = 0
G_RWKV = 8
G_LORA = 16
G_MEM = 17
G_GATE = 21
G_BR = 27
G_OUT = 33
G_KV = 35
NGROUPS = 39

V_NORM_IN, V_CW0, V_CW1, V_CW2, V_CB, V_MU, V_W0, V_A0, V_KK, V_KA, V_RK, V_LNG, V_LNB, V_NMEM = (
    0, 8, 16, 24, 32, 40, 65, 73, 81, 89, 97, 105, 113, 121)
NV = 129


class _Op:
    __slots__ = ("eng", "fn", "deps", "raw", "dma", "stream", "signal", "sig_idx", "idx", "force", "cost", "lat", "tset")


class Prog:
    def __init__(self, nc):
        self.nc = nc
        self.ops = []
        self.lastw = {}
        self.readers = {}
        self.engs = {"pe": nc.tensor, "act": nc.scalar, "dve": nc.vector, "pool": nc.gpsimd, "sp": nc.sync}

    DEFC = {"pe": 0.12, "act": 0.45, "dve": 0.40, "pool": 0.60, "sp": 0.10}

    def add(self, eng, fn, r=(), w=(), dma=False, stream=None, force=False, cost=None, lat=None):
        op = _Op()
        op.force = force
        if cost is None and not dma and hasattr(fn, "free"):
            a, b = {"pe": (0.07, 0.00042), "act": (0.22, 0.00085), "dve": (0.12, 0.0011), "pool": (0.15, 0.0020), "sp": (0.1, 0.0)}[eng]
            cost = a + b * fn.free
        op.cost = cost if cost is not None else ((1.8 if eng == "pool" else 0.15) if dma else self.DEFC[eng])
        op.lat = lat if lat is not None else (4.0 if dma else 0.0)
        op.tset = getattr(fn, "tset", 0)
        op.eng, op.fn, op.dma, op.stream = eng, fn, dma, stream
        op.signal = False
        op.idx = len(self.ops)
        deps, raw = set(), set()
        psr = [k for k in r if k.startswith("ps")]
        for k in psr:
            if k in self.lastw:
                raw.add(self.lastw[k])
        r = [k for k in r if not k.startswith("ps")]
        w = list(w) + psr
        for k in r:
            if k in self.lastw:
                deps.add(self.lastw[k]); raw.add(self.lastw[k])
        for k in w:
            if k in self.lastw:
                deps.add(self.lastw[k]); raw.add(self.lastw[k])
            for x in self.readers.get(k, ()):
                deps.add(x)
        deps.discard(op.idx)
        op.deps, op.raw = deps, raw
        for k in r:
            self.readers.setdefault(k, []).append(op.idx)
        for k in w:
            self.lastw[k] = op.idx
            self.readers[k] = []
        self.ops.append(op)
        return op.idx

    def act(self, fn, r=(), w=()): return self.add("act", fn, r, w)
    def dve(self, fn, r=(), w=()): return self.add("dve", fn, r, w)
    def pool(self, fn, r=(), w=()): return self.add("pool", fn, r, w)
    def pe(self, fn, r=(), w=(), force=False): return self.add("pe", fn, r, w, force=force)

    def dma(self, eng, out, in_, r=(), w=(), stream=None, lat=None):
        assert stream is not None
        return self.add(eng, lambda e: e.dma_start(out=out, in_=in_), r, w, dma=True, stream=stream, lat=lat)

    def schedule(self, window=800):
        import heapq
        ops = self.ops; n = len(ops)
        if os.environ.get("KNOSCHED"):
            return list(range(n))
        succ = [[] for _ in ops]; indeg = [0] * n
        for op in ops:
            indeg[op.idx] = len(op.deps)
            for d in op.deps:
                succ[d].append(op.idx)
        start = [0.0] * n; finish = [0.0] * n; rtime = [0.0] * n
        blev = [0.0] * n
        for i in range(n - 1, -1, -1):
            op = ops[i]
            m = 0.0
            for sidx in succ[i]:
                v = blev[sidx] + (0.55 if ops[sidx].eng != op.eng else 0.12)
                if v > m:
                    m = v
            blev[i] = op.cost + op.lat + m
        PRI = os.environ.get("KPRI", "cp")
        ready = {e: [] for e in self.engs}
        efree = {e: 0.0 for e in self.engs}
        done = [False] * n
        for op in ops:
            if indeg[op.idx] == 0:
                ready[op.eng].append(op.idx)
        order = []; base = 0
        act_set = [0]
        TSW = 1.3
        while len(order) < n:
            while base < n and done[base]:
                base += 1
            lim = base + window
            best = None
            for e, lst in ready.items():
                cand = None; cand_key = None
                for i in lst:
                    if i >= lim:
                        continue
                    st = max(efree[e], rtime[i])
                    if e == "act" and ops[i].tset and ops[i].tset != act_set[0]:
                        st = max(efree[e] + TSW, rtime[i])
                    pr = i if PRI == "idx" else -blev[i]
                    key = (st, pr) if st > efree[e] else (efree[e], pr)
                    if cand is None or key < cand_key:
                        cand, cand_key = i, key
                if cand is not None and (best is None or cand_key < best[0]):
                    best = (cand_key, cand, e)
            assert best is not None, "scheduler stuck"
            (st, _pr), i, e = best
            op = ops[i]
            ready[e].remove(i)
            if e == "act" and op.tset:
                if op.tset != act_set[0]:
                    self.n_tsw = getattr(self, "n_tsw", 0) + 1
                act_set[0] = op.tset
            start[i] = st; efree[e] = st + op.cost; finish[i] = st + op.cost + op.lat
            done[i] = True; order.append(i)
            for sidx in succ[i]:
                sop = ops[sidx]
                if sop.eng == op.eng and not op.dma and not sop.dma and op.eng == "pe" and not sop.force:
                    t = start[i] + 0.01
                elif sop.eng == op.eng and not op.dma:
                    t = finish[i] + 0.12
                else:
                    t = finish[i] + 0.55
                if t > rtime[sidx]:
                    rtime[sidx] = t
                indeg[sidx] -= 1
                if indeg[sidx] == 0:
                    ready[sop.eng].append(sidx)
        self.est_us = max(finish)
        if os.environ.get("KDUMP"):
            a, b = [int(x) for x in os.environ["KDUMP"].split(":")]
            for i in range(a, b):
                op = ops[i]
                crit = max(op.deps, key=lambda d: finish[d]) if op.deps else -1
                print(f"   op {i} {op.eng:4s} start {start[i]:8.2f} fin {finish[i]:8.2f} rtime {rtime[i]:8.2f} critdep {crit} ({ops[crit].eng if crit >= 0 else ''} fin {finish[crit] if crit >= 0 else 0:8.2f}) dma={op.dma}")
        if os.environ.get("KSTAT"):
            tot = {e: 0.0 for e in self.engs}
            for op in ops:
                tot[op.eng] += op.cost
            print("model engine busy us:", {k: round(v) for k, v in tot.items()})
            for name, idx in getattr(self, "marks", []):
                if idx < n:
                    print(f"  mark {name:24s} op {idx:6d} start {start[idx]:9.1f}")
        return order

    def emit(self, es):
        nc = self.nc
        ops = self.ops
        need = [[] for _ in ops]
        for op in ops:
            for d in sorted(op.deps):
                dep = ops[d]
                if dep.dma:
                    need[op.idx].append(d)
                elif op.dma or dep.eng != op.eng:
                    dep.signal = True
                    need[op.idx].append(d)
                elif op.eng != "pe" or op.force:
                    dep.signal = True
                    need[op.idx].append(d)
        sems = {}
        for e in ("pe", "act", "dve", "pool", "sp"):
            sems[e] = es.enter_context(nc.semaphore("s_" + e))
        streams = {}
        for op in ops:
            if op.dma and op.stream not in streams:
                streams[op.stream] = es.enter_context(nc.semaphore("d_" + op.stream))
        cnt = {k: 0 for k in sems}
        scnt = {k: 0 for k in streams}
        waited = {e: {} for e in self.engs}
        order = self.schedule()
        for oi in order:
            op = ops[oi]
            wl = {}
            for d in need[op.idx]:
                dep = ops[d]
                if dep.dma:
                    key, val = ("d", dep.stream), 16 * dep.sig_idx
                else:
                    key, val = ("e", dep.eng), dep.sig_idx
                if wl.get(key, 0) < val:
                    wl[key] = val
            eng = self.engs[op.eng]
            wd = waited[op.eng]
            pend = []
            for key, val in wl.items():
                if wd.get(key, 0) >= val:
                    continue
                wd[key] = val
                sem = streams[key[1]] if key[0] == "d" else sems[key[1]]
                pend.append((sem, val))
            attach = None
            if pend and not op.dma and not os.environ.get("KNOATTACH"):
                attach = pend.pop()
            for sem, val in pend:
                eng.wait_ge(sem, val)
            ins = op.fn(eng)
            if attach is not None:
                ins._wait_ge(attach[0], attach[1])
            self.n_wait = getattr(self, "n_wait", 0) + len(pend)
            self.n_att = getattr(self, "n_att", 0) + (1 if attach else 0)
            if op.dma:
                scnt[op.stream] += 1
                op.sig_idx = scnt[op.stream]
                ins.then_inc(streams[op.stream], 16)
            elif op.signal:
                cnt[op.eng] += 1
                op.sig_idx = cnt[op.eng]
                ins.then_inc(sems[op.eng], 1)
            else:
                op.sig_idx = cnt[op.eng] + 1
        return len(ops)


def MM(out, lhsT, rhs, start, stop):
    return _tag(lambda e: e.matmul(out, lhsT, rhs, start=start, stop=stop), out)

def TR(out, in_, ident):
    return _tag(lambda e: e.transpose(out, in_, ident), out)

def _fs(ap):
    n = 1
    for d in ap.shape[1:]:
        n *= int(d)
    return n

def _tag(fn, out):
    fn.free = _fs(out)
    return fn

def ACTF(out, in_, func, bias=None, scale=None, accum=None):
    kw = {}
    if bias is not None: kw["bias"] = bias
    if scale is not None: kw["scale"] = scale
    if accum is not None: kw["accum_out"] = accum
    fn = _tag(lambda e: e.activation(out=out, in_=in_, func=func, **kw), out)
    fn.tset = {AF.Sigmoid: 1, AF.Silu: 2, AF.Exp: 3, AF.Ln: 3}.get(func, 0)
    return fn

def TS(out, in0, s1, s2, op0, op1=None):
    if op1 is None:
        return _tag(lambda e: e.tensor_scalar(out=out, in0=in0, scalar1=s1, scalar2=None, op0=op0), out)
    return _tag(lambda e: e.tensor_scalar(out=out, in0=in0, scalar1=s1, scalar2=s2, op0=op0, op1=op1), out)

def TT(out, in0, in1, op):
    return _tag(lambda e: e.tensor_tensor(out=out, in0=in0, in1=in1, op=op), out)

def STT(out, in0, scalar, in1, op0, op1):
    fn = _tag(lambda e: e.scalar_tensor_tensor(out=out, in0=in0, scalar=scalar, in1=in1, op0=op0, op1=op1), out)
    fn.free *= 1.6
    return fn

def CP(out, in_):
    return _tag(lambda e: e.tensor_copy(out=out, in_=in_), out)

def RCP(out, in_):
    return _tag(lambda e: e.reciprocal(out=out, in_=in_), out)

def MSET(out, v):
    return lambda e: e.memset(out, v)

def CPA(out, in_):
    return _tag(lambda e: e.copy(out=out, in_=in_), out)

def MULA(out, in_, m):
    return _tag(lambda e: e.mul(out=out, in_=in_, mul=m), out)


def _tile_w(w, cols):
    out = np.zeros((128, 8, 512), np.float32)
    sub = w[:, cols]
    out[:, :, :sub.shape[1]] = sub.reshape(8, 128, -1).transpose(1, 0, 2)
    return out


def _build_wall(w_in, w_branch, w_out, w_mem_kv):
    ar = np.arange
    groups = []
    for j in range(8):
        groups.append(_tile_w(w_in, np.concatenate([o + j * 128 + ar(128) for o in (O_H, O_B, O_C, O_GC)])))
    for j in range(8):
        groups.append(_tile_w(w_in, np.concatenate([o + j * 128 + ar(128) for o in (O_R, O_K, O_V, O_GR)])))
    groups.append(_tile_w(w_in, O_WD + ar(128)))
    for h in range(4):
        groups.append(_tile_w(w_in, np.concatenate([O_Q + h * 256 + ar(256), O_GM + h * 256 + ar(256)])))
    for o in (O_MC, O_MR, O_MM):
        for hf in range(2):
            groups.append(_tile_w(w_in, o + hf * 512 + ar(512)))
    for i in range(3):
        for hf in range(2):
            groups.append(_tile_w(w_branch[i], hf * 512 + ar(512)))
    for hf in range(2):
        groups.append(_tile_w(w_out, hf * 512 + ar(512)))
    for q in range(4):
        groups.append(_tile_w(w_mem_kv, q * 512 + ar(512)))
    assert len(groups) == NGROUPS
    return np.ascontiguousarray(np.stack(groups))


def _fm(v):
    return np.ascontiguousarray(v.reshape(-1, 128).T)


def _host_inputs(inp):
    f = lambda a: np.ascontiguousarray(np.asarray(a, dtype=np.float32))
    x_prompt, x_sample, mem_prompt = f(inp["x_prompt"]), f(inp["x_sample"]), f(inp["mem_prompt"])
    ck, cv = f(inp["cache_mem_k"])[0], f(inp["cache_mem_v"])[0]
    sconv, sshift, swkv = f(inp["state_conv"])[0], f(inp["state_shift"])[0], f(inp["state_wkv"])[0]
    wall = _build_wall(f(inp["w_in"])[0], f(inp["w_branch"])[0], f(inp["w_out"])[0], f(inp["w_mem_kv"])[0])
    vecs = np.zeros((128, NV), np.float32)
    cw = f(inp["conv_w"])[0]
    for off, v in ((V_NORM_IN, f(inp["norm_in"])[0]), (V_CW0, cw[0]), (V_CW1, cw[1]), (V_CW2, cw[2]),
                   (V_CB, f(inp["conv_b"])[0]), (V_MU, f(inp["shift_mu"])[0]), (V_W0, f(inp["decay_w0"])[0]),
                   (V_A0, f(inp["icl_a0"])[0]), (V_KK, f(inp["k_k"])[0]), (V_KA, f(inp["k_a"])[0]),
                   (V_RK, f(inp["r_k"])[0].reshape(-1)), (V_LNG, f(inp["ln_x_g"])[0]), (V_LNB, f(inp["ln_x_b"])[0]),
                   (V_NMEM, f(inp["norm_mem"])[0])):
        m = _fm(v)
        vecs[:, off:off + m.shape[1]] = m
    gfin = np.ascontiguousarray(np.broadcast_to(f(inp["norm_final"])[None, :], (128, D)))
    lora = np.ascontiguousarray(np.concatenate([f(inp["decay_up"])[0], f(inp["icl_up"])[0]], axis=0))
    consts = _make_consts()
    maps = []
    for c in range(NCORES):
        xall = np.concatenate([x_prompt[c].reshape(16, 128, D),
                               x_sample[NB * c:NB * (c + 1)].reshape(1, 128, D)], axis=0)
        maps.append({
            "xall": np.ascontiguousarray(xall),
            "mem": np.ascontiguousarray(mem_prompt[c].reshape(2, 128, D)),
            "ck": np.ascontiguousarray(ck[NB * c:NB * (c + 1)].reshape(NB, 256, D)),
            "cv": np.ascontiguousarray(cv[NB * c:NB * (c + 1)].reshape(NB, 256, D)),
            "sconv": np.ascontiguousarray(sconv[NB * c:NB * (c + 1)].reshape(NB * 2, D)),
            "sshift": np.ascontiguousarray(sshift[NB * c:NB * (c + 1)]),
            "swkv": np.ascontiguousarray(swkv[NB * c:NB * (c + 1)]),
            "wall": wall, "vecs": vecs, "gfin": gfin, "lora": lora, "consts": consts,
        })
    return maps


C_IDENT, C_TRIL_S, C_TRIL_I, C_BD_S, C_BD_I, C_BLK, C_RESET, C_LT, C_BD_LT = 0, 128, 256, 384, 512, 640, 768, 896, 1024
NCONST = 1152


def _make_consts():
    c = np.zeros((128, NCONST), np.float32)
    i = np.arange(128)
    c[:, C_IDENT:C_IDENT + 128] = np.eye(128)
    c[:, C_TRIL_S:C_TRIL_S + 128] = (i[:, None] < i[None, :])
    c[:, C_TRIL_I:C_TRIL_I + 128] = (i[:, None] <= i[None, :])
    same = (i[:, None] // NT) == (i[None, :] // NT)
    c[:, C_BD_S:C_BD_S + 128] = (i[:, None] < i[None, :]) & same
    c[:, C_BD_I:C_BD_I + 128] = (i[:, None] <= i[None, :]) & same
    c[:, C_BLK:C_BLK + 128] = (i[:, None] // 64) == (i[None, :] // 64)
    c[:, C_RESET:C_RESET + 16] = (i[:, None] // NT) == np.arange(16)[None, :]
    c[:, C_LT:C_LT + 128] = (i[None, :] < i[:, None])
    c[:, C_BD_LT:C_BD_LT + 128] = (i[None, :] < i[:, None]) & same
    return c


NTP = 1152
PASS_TILES = [list(range(8)) + [16], list(range(8, 16))]
PASS_TBLK = [[(0, 512), (512, 512), (1024, 128)], [(0, 512), (512, 512)]]


def build_nc(stages=("rwkv", "conv", "mem")):
    nc = bass.Bass("TRN2", target_bir_lowering=False)
    din = lambda name, shape: nc.dram_tensor(name, list(shape), F32, kind="ExternalInput").ap()
    dout = lambda name, shape: nc.dram_tensor(name, list(shape), F32, kind="ExternalOutput").ap()
    xall_d = din("xall", (17, 128, D)); mem_d = din("mem", (2, 128, D))
    ck_d = din("ck", (NB, 256, D)); cv_d = din("cv", (NB, 256, D))
    sconv_d = din("sconv", (NB * 2, D)); sshift_d = din("sshift", (NB, 3200)); swkv_d = din("swkv", (NB, 16, 64, 64))
    wall_d = din("wall", (NGROUPS, 128, 8, 512)); vecs_d = din("vecs", (128, NV)); gfin_d = din("gfin", (128, D))
    lora_d = din("lora", (128, 1024)); consts_d = din("consts", (128, NCONST))
    y_d = dout("y", (17, 128, D)); mk_d = dout("mk", (256, D)); mv_d = dout("mv", (256, D))
    convp_d = dout("convp", (2, D)); shiftp_d = dout("shiftp", (1, 3200)); wkvp_d = dout("wkvp", (16, 64, 64))
    convs_d = dout("convs", (NB * 2, D)); shifts_d = dout("shifts", (NB, 3200)); wkvs_d = dout("wkvs", (NB, 16, 64, 64))

    with ExitStack() as es:
        sb = lambda name, shape, dt=F32: es.enter_context(nc.sbuf_tensor("s_" + name, list(shape), dt))
        P = Prog(nc)
        vecs = sb("vecs", (128, NV)); consts = sb("consts", (128, NCONST)); gfin = sb("gfin", (128, D))
        identb = sb("identb", (128, 128), BF16); onesb = sb("onesb", (128, 128), BF16)
        xt = sb("xt", (128, D)); xnb = sb("xnb", (128, D), BF16)
        xt2 = sb("xt2", (128, D)); xnb2 = sb("xnb2", (128, D), BF16)
        ssb = sb("ssb", (128, 16))
        xnT = sb("xnT", (128, 8, NTP), BF16); yT = sb("yT", (128, 8, NTP), BF16)
        acc = sb("acc", (128, 8, NTP))
        NS = 4
        ring = [sb(f"ring{s}", (128, 8, 512), BF16) for s in range(NS)]
        NTMP = 12
        tmps = [sb(f"tmp{i}", (128, 512)) for i in range(NTMP)]
        memnT = sb("memnT", (128, 8, 256), BF16); KT = sb("KT", (128, 8, 256), BF16); Vp = sb("Vp", (128, 2, D), BF16)
        U = sb("U", (128, 1026)); us = sb("us", (128, NB, 10)); ucarry = sb("ucarry", (128, 8, 2)); uls = sb("uls", (128, 8, NB, 2))
        scT = sb("scT", (128, 8, NB * 2))
        qTs = sb("qTs", (128, 8, 128), BF16); sgms = sb("sgms", (128, 8, 128))
        Kb = sb("Kb", (128, 2, D), BF16); Vb = sb("Vb", (128, 2, D), BF16); KTb = sb("KTb", (128, 8, 256), BF16)
        ps = [es.enter_context(nc.psum_tensor(f"ps{i}", [128, 512], F32)) for i in range(8)]
        psk = [f"ps{i}" for i in range(8)]
        ident = consts[:, C_IDENT:C_IDENT + 128]
        psT = ps[7][:, :].bitcast(BF16)

        tctr = [0]
        tmp_excl = set()
        def tmp():
            i = tctr[0] % NTMP; tctr[0] += 1
            while i in tmp_excl:
                i = tctr[0] % NTMP; tctr[0] += 1
            return tmps[i], f"tmp{i}"

        vcol = lambda off, j: vecs[:, off + j:off + j + 1]

        P.dma("sp", vecs[:], vecs_d[:, :], w=["vecs"], stream="c_vecs")
        P.dma("sp", consts[:], consts_d[:, :], w=["consts"], stream="c_consts")
        P.dma("sp", gfin[:], gfin_d[:, :], w=["gfin"], stream="c_gfin")
        P.dve(CP(identb[:], ident), r=["consts"], w=["identb"])
        P.dve(MSET(onesb[:], 1.0), w=["onesb"])
        P.dve(MSET(ucarry[:], 0.0), w=["ucarry"])

        sched = [G_KV + q for q in range(4)]
        for p_ in range(2):
            if "rwkv" in stages:
                sched += [G_LORA] + [G_RWKV + j for j in range(8)] + [G_GATE + 2, G_BR + 2, G_GATE + 3, G_BR + 3]
            if "conv" in stages:
                sched += [G_CONV + j for j in range(8)] + [G_GATE + 0, G_BR + 0, G_GATE + 1, G_BR + 1]
            if "mem" in stages:
                sched += [G_MEM + h for h in range(4)] + [G_GATE + 4, G_BR + 4, G_GATE + 5, G_BR + 5]
            if 'noout' not in stages:
                sched += [G_OUT, G_OUT + 1]
        wst = {"i": 0, "loaded": 0}
        def w_prefetch(upto):
            while wst["loaded"] < min(upto, len(sched)):
                k = wst["loaded"]; s = k % NS
                P.dma("pool", ring[s][:], wall_d[sched[k]], w=[f"ring{s}"], stream=f"ring{s}", lat=14.0)
                wst["loaded"] += 1
        def w_acquire(g):
            if not hasattr(P, "marks"):
                P.marks = []
            P.marks.append((f"grp{g}", len(P.ops)))
            i = wst["i"]
            if os.environ.get('RCUT'):
                sched[i] = g
            assert sched[i] == g, (i, sched[i], g)
            w_prefetch(i + NS - 1)
            wst["i"] += 1
            return i % NS

        def mm8(bank, bkey, slot, c0, rhs_of_kc, rkeys):
            for kc in range(8):
                P.add("pe", MM(bank, ring[slot][:, kc, c0:c0 + 128], rhs_of_kc(kc), kc == 0, kc == 7),
                      r=[f"ring{slot}"] + rkeys, w=[bkey], cost=0.06 + 0.00041 * int(bank.shape[-1]))

        nctr = [0]
        def norm_T(src, voff, dst3, dkeys):
            q = nctr[0] % 2; nctr[0] += 1
            xt_, kx = (xt, "xt") if q == 0 else (xt2, "xt2")
            xn_, kn = (xnb, "xnb") if q == 0 else (xnb2, "xnb2")
            pT_ = ps[7 - q][:, :].bitcast(BF16); kp = psk[7 - q]
            o = 8 * q
            P.dma("sp", xt_[:], src, w=[kx], stream=kx)
            P.act(ACTF(xn_[:], xt_[:], AF.Square, accum=ssb[:, o:o + 1]), r=[kx], w=[kn, f"ss{o}"])
            P.act(ACTF(ssb[:, o + 1:o + 2], ssb[:, o:o + 1], AF.Ln, bias=EPS, scale=1.0 / D), r=[f"ss{o}"], w=[f"ss{o + 1}"])
            P.act(ACTF(ssb[:, o + 2:o + 3], ssb[:, o + 1:o + 2], AF.Exp, scale=-0.5), r=[f"ss{o + 1}"], w=[f"ss{o + 2}"])
            P.dve(TS(xn_[:], xt_[:], ssb[:, o + 2:o + 3], None, ALU.mult), r=[kx, f"ss{o + 2}"], w=[kn])
            for c in range(8):
                P.pe(TR(pT_[:, c * 128:(c + 1) * 128], xn_[:, c * 128:(c + 1) * 128], identb[:]),
                     r=[kn, "identb"], w=[kp])
            for c in range(8):
                if c % 2 == 0:
                    P.dve(TS(dst3[:, c, :], pT_[:, c * 128:(c + 1) * 128], vcol(voff, c), None, ALU.mult),
                          r=[kp, "vecs"], w=dkeys)
                else:
                    P.act(MULA(dst3[:, c, :], pT_[:, c * 128:(c + 1) * 128], vcol(voff, c)), r=[kp, "vecs"], w=dkeys)

        LVL = int(os.environ.get('KCUT', '99'))
        for i in range(2 if LVL >= 1 else 0):
            norm_T(mem_d[i], V_NMEM, memnT[:, :, i * 128:(i + 1) * 128], ["memnT"])

        t0, k0 = tmp()
        P.dma("sp", t0[0:32, :], sconv_d[:, 0:512], w=[k0], stream="c_sc0")
        t1, k1 = tmp()
        P.dma("sp", t1[0:32, :], sconv_d[:, 512:1024], w=[k1], stream="c_sc1")
        for j in range(8 if LVL >= 2 else 0):
            src_t = t0 if j < 4 else t1
            P.pe(TR(ps[6][:, j * 32:(j + 1) * 32], src_t[0:32, (j % 4) * 128:(j % 4 + 1) * 128], ident[0:32, 0:32]),
                 r=[k0, k1, "consts"], w=[psk[6]])
        if LVL >= 2:
            P.act(CPA(scT[:].rearrange("p j s -> p (j s)"), ps[6][:, 0:256]), r=[psk[6]], w=["scT"])

        for q in range(4 if LVL >= 3 else 0):
            s = w_acquire(G_KV + q)
            KSUB = int(os.environ.get('KSUB', '9'))
            if KSUB < 1:
                P.pe(MM(ps[0][:, 0:128], ring[s][:, 0, 0:128], memnT[:, 0, 0:128], True, True), r=[f'ring{s}', 'memnT'], w=[psk[0]])
                continue
            if q < 2:
                for ct in range(4):
                    bank, bk = ps[ct % 2], psk[ct % 2]
                    mm8(bank[:, 0:256], bk, s, ct * 128, lambda kc: memnT[:, kc, :], ["memnT"])
                    P.act(CPA(KT[:, q * 4 + ct, :], bank[:, 0:256]), r=[bk], w=["KT"])
            for mt in range(2 if KSUB >= 2 else 0):
                bank, bk = ps[2 + mt], psk[2 + mt]
                for kc in range(8):
                    P.pe(MM(bank[:, :], memnT[:, kc, mt * 128:(mt + 1) * 128], ring[s][:, kc, :], kc == 0, kc == 7),
                         r=[f"ring{s}", "memnT"], w=[bk])
                t, tk = tmp()
                P.dve(CP(t[:], bank[:, :]), r=[bk], w=[tk])
                if q >= 2:
                    P.act(CPA(Vp[:, mt, (q - 2) * 512:(q - 1) * 512], bank[:, :]), r=[bk], w=["Vp"])
                dst = (mk_d if q < 2 else mv_d)[mt * 128:(mt + 1) * 128, (q % 2) * 512:(q % 2 + 1) * 512]
                P.dma("sp", dst, t[:], r=[tk], w=[f"o_mkv{q}{mt}"], stream="o_" + tk)

        def branch(i, first, tblk):
            for hf in range(2):
                sg = w_acquire(G_GATE + 2 * i + hf)
                sw = w_acquire(G_BR + 2 * i + hf)
                cnt = 0
                for ct in range(4):
                    j = hf * 4 + ct
                    for tb, (c0, n) in enumerate(tblk):
                        par = cnt % 4; cnt += 1
                        bA, kA = ps[par * 2], psk[par * 2]
                        bB, kB = ps[par * 2 + 1], psk[par * 2 + 1]
                        mm8(bA[:, 0:n], kA, sg, ct * 128, lambda kc: xnT[:, kc, c0:c0 + n], [f"xnT{tb}"])
                        mm8(bB[:, 0:n], kB, sw, ct * 128, lambda kc: yT[:, kc, c0:c0 + n], [f"yT{tb}"])
                        t, tk = tmp()
                        P.act(ACTF(t[:, 0:n], bA[:, 0:n], AF.Sigmoid), r=[kA], w=[tk])
                        ak = f"acc{j}_{tb}"
                        if first:
                            P.dve(TT(acc[:, j, c0:c0 + n], bB[:, 0:n], t[:, 0:n], ALU.mult), r=[kB, tk], w=[ak])
                        else:
                            t2, tk2 = tmp()
                            P.dve(TT(t2[:, 0:n], bB[:, 0:n], t[:, 0:n], ALU.mult), r=[kB, tk], w=[tk2])
                            P.pool(TT(acc[:, j, c0:c0 + n], acc[:, j, c0:c0 + n], t2[:, 0:n], ALU.add), r=[ak, tk2], w=[ak])

        def conv_phase(p_, tblk):
            for j in range(8):
                s = w_acquire(G_CONV + j)
                P.pool(CP(U[:, 0:2], ucarry[:, j, :]), r=["ucarry"], w=["U"])
                if p_ == 0:
                    P.pool(CP(us[:, :, 0:2], scT[:, j, :].rearrange("p (b s) -> p b s", s=2)), r=["scT"], w=["us"])
                for tb, (c0, n) in enumerate(tblk):
                    o = (tb % 2) * 4
                    bh, bB, bC, bg = ps[o:o + 4]; kh, kB, kC, kg = psk[o:o + 4]
                    for ct, (bank, bk) in enumerate(((bh, kh), (bB, kB), (bC, kC), (bg, kg))):
                        mm8(bank[:, 0:n], bk, s, ct * 128, lambda kc: xnT[:, kc, c0:c0 + n], [f"xnT{tb}"])
                    th, tkh = tmp()
                    P.act(CPA(th[:, 0:n], bh[:, 0:n]), r=[kh], w=[tkh])
                    if n == 512:
                        ucur, um1, um2 = U[:, 2 + c0:2 + c0 + n], U[:, 1 + c0:1 + c0 + n], U[:, c0:c0 + n]
                        v3 = lambda a: a
                        ukey = "U"
                    else:
                        ucur, um1, um2 = us[:, :, 2:10], us[:, :, 1:9], us[:, :, 0:8]
                        v3 = lambda a: a.rearrange("p (b t) -> p b t", t=NT)
                        ukey = "us"
                    P.dve(TT(ucur, v3(bC[:, 0:n]), v3(th[:, 0:n]), ALU.mult), r=[kC, tkh], w=[ukey])
                    a1, ka1 = tmp()
                    P.dve(TS(v3(a1[:, 0:n]), ucur, vcol(V_CW2, j), vcol(V_CB, j), ALU.mult, ALU.add), r=[ukey, "vecs"], w=[ka1])
                    a2, ka2 = tmp()
                    P.dve(STT(v3(a2[:, 0:n]), um1, vcol(V_CW1, j), v3(a1[:, 0:n]), ALU.mult, ALU.add), r=[ukey, ka1, "vecs"], w=[ka2])
                    a3, ka3 = tmp()
                    P.dve(STT(v3(a3[:, 0:n]), um2, vcol(V_CW0, j), v3(a2[:, 0:n]), ALU.mult, ALU.add), r=[ukey, ka2, "vecs"], w=[ka3])
                    sg, ksg = tmp()
                    P.act(ACTF(sg[:, 0:n], bg[:, 0:n], AF.Silu), r=[kg], w=[ksg])
                    a4, ka4 = tmp()
                    P.dve(TT(a4[:, 0:n], bB[:, 0:n], a3[:, 0:n], ALU.mult), r=[kB, ka3], w=[ka4])
                    P.pool(TT(yT[:, j, c0:c0 + n], a4[:, 0:n], sg[:, 0:n], ALU.mult), r=[ka4, ksg], w=[f"yT{tb}"])
                P.pool(CP(ucarry[:, j, :], U[:, 1024:1026]), r=["U"], w=["ucarry"])
                if p_ == 0:
                    P.pool(CP(uls[:, j, :, :], us[:, :, 8:10]), r=["us"], w=["uls"])
            for hf in range(2):
                for jj in range(4):
                    j = hf * 4 + jj
                    if p_ == 1:
                        P.pe(TR(ps[0][0:2, jj * 128:(jj + 1) * 128], ucarry[:, j, :], ident), r=["ucarry", "consts"], w=[psk[0]])
                    else:
                        P.pe(TR(ps[1][0:32, jj * 128:(jj + 1) * 128], uls[:, j, :, :].rearrange("p b s -> p (b s)"), ident),
                             r=["uls", "consts"], w=[psk[1]])
                t, tk = tmp()
                if p_ == 1:
                    P.dve(CP(t[0:2, :], ps[0][0:2, :]), r=[psk[0]], w=[tk])
                    P.dma("sp", convp_d[:, hf * 512:(hf + 1) * 512], t[0:2, :], r=[tk], w=[f"o_convp{hf}"], stream="o_" + tk)
                else:
                    P.dve(CP(t[0:32, :], ps[1][0:32, :]), r=[psk[1]], w=[tk])
                    P.dma("sp", convs_d[:, hf * 512:(hf + 1) * 512], t[0:32, :], r=[tk], w=[f"o_convs{hf}"], stream="o_" + tk)

        def mem_phase(p_, tblk):
            for h in range(4):
                s = w_acquire(G_MEM + h)
                for tb, (c0, n) in enumerate(tblk):
                    bq = [ps[0], ps[1]]; bg = [ps[2], ps[3]]
                    for ct in range(4):
                        mm8(ps[ct][:, 0:n], psk[ct], s, ct * 128, lambda kc: xnT[:, kc, c0:c0 + n], [f"xnT{tb}"])
                    if n == 128:
                        for dc in range(2):
                            P.act(CPA(qTs[:, 2 * h + dc, :], bq[dc][:, 0:128]), r=[psk[dc]], w=["qTs"])
                            P.act(ACTF(sgms[:, 2 * h + dc, :], bg[dc][:, 0:128], AF.Silu), r=[psk[2 + dc]], w=["sgms"])
                        continue
                    qt, kq = tmp()
                    qT = qt[:, :].bitcast(BF16)
                    sg0, ks0 = tmp(); sg1, ks1 = tmp()
                    for dc in range(2):
                        P.act(CPA(qT[:, dc * 512:(dc + 1) * 512], bq[dc][:, :]), r=[psk[dc]], w=[kq])
                    P.act(ACTF(sg0[:, :], bg[0][:, :], AF.Silu), r=[psk[2]], w=[ks0])
                    P.act(ACTF(sg1[:, :], bg[1][:, :], AF.Silu), r=[psk[3]], w=[ks1])
                    et, ke = tmp()
                    E = et[:, :].bitcast(BF16)
                    for mt in range(2):
                        bS, kS = ps[4 + mt], psk[4 + mt]
                        for dc in range(2):
                            P.pe(MM(bS[:, :], KT[:, 2 * h + dc, mt * 128:(mt + 1) * 128], qT[:, dc * 512:(dc + 1) * 512],
                                    dc == 0, dc == 1), r=["KT", kq], w=[kS])
                        P.act(ACTF(E[:, mt * 512:(mt + 1) * 512], bS[:, :], AF.Exp, scale=1.0 / 16.0), r=[kS], w=[ke])
                    for mt in range(2):
                        P.pe(MM(ps[6][:, :], onesb[:], E[:, mt * 512:(mt + 1) * 512], mt == 0, mt == 1),
                             r=["onesb", ke], w=[psk[6]])
                    rd, krd = tmp()
                    P.dve(RCP(rd[:, :], ps[6][:, :]), r=[psk[6]], w=[krd])
                    for dvc in range(2):
                        for mt in range(2):
                            P.pe(MM(ps[7][:, :], Vp[:, mt, h * 256 + dvc * 128:h * 256 + (dvc + 1) * 128],
                                    E[:, mt * 512:(mt + 1) * 512], mt == 0, mt == 1), r=["Vp", ke], w=[psk[7]])
                        o1, ko1 = tmp()
                        P.dve(TT(o1[:, :], ps[7][:, :], rd[:, :], ALU.mult), r=[psk[7], krd], w=[ko1])
                        sgd, ksd = (sg0, ks0) if dvc == 0 else (sg1, ks1)
                        P.pool(TT(yT[:, 2 * h + dvc, c0:c0 + n], o1[:, :], sgd[:, :], ALU.mult), r=[ko1, ksd], w=[f"yT{tb}"])
            if p_ != 0:
                return
            tmp_excl.update({4, 5, 6, 7})
            Kset = [[(Kb[:, 0, :], "Kb0"), (Kb[:, 1, :], "Kb1")],
                    [(tmps[4][:, :].bitcast(BF16), "tmp4"), (tmps[5][:, :].bitcast(BF16), "tmp5")]]
            Vset = [[(Vb[:, 0, :], "Vb0"), (Vb[:, 1, :], "Vb1")],
                    [(tmps[6][:, :].bitcast(BF16), "tmp6"), (tmps[7][:, :].bitcast(BF16), "tmp7")]]
            for b in range(NB):
                Kc, Vc = Kset[b % 2], Vset[b % 2]
                for mt in range(2):
                    P.dma("pool", Kc[mt][0], ck_d[b, mt * 128:(mt + 1) * 128, :], w=[Kc[mt][1]], stream="k_" + Kc[mt][1], lat=8.0)
                    P.dma("pool", Vc[mt][0], cv_d[b, mt * 128:(mt + 1) * 128, :], w=[Vc[mt][1]], stream="k_" + Vc[mt][1], lat=8.0)
                for mt in range(2):
                    for dch in range(8):
                        P.pe(TR(psT[:, dch * 128:(dch + 1) * 128], Kc[mt][0][:, dch * 128:(dch + 1) * 128], identb[:]),
                             r=[Kc[mt][1], "identb"], w=[psk[7]])
                    P.act(CPA(KTb[:, :, mt * 128:(mt + 1) * 128], psT.rearrange("p (c t) -> p c t", c=8)), r=[psk[7]], w=["KTb"])
                bS, kS = ps[4], psk[4]
                for h in range(4):
                    for mt in range(2):
                        for dc in range(2):
                            P.pe(MM(bS[:, (h * 2 + mt) * 8:(h * 2 + mt + 1) * 8], KTb[:, 2 * h + dc, mt * 128:(mt + 1) * 128],
                                    qTs[:, 2 * h + dc, b * 8:(b + 1) * 8], dc == 0, dc == 1), r=["KTb", "qTs"], w=[kS])
                et, ke = tmp()
                E = et[:, 0:32].bitcast(BF16)
                P.act(ACTF(E, bS[:, 0:64], AF.Exp, scale=1.0 / 16.0), r=[kS], w=[ke])
                E4 = E.rearrange("p (h m t) -> p h m t", h=4, m=2)
                for mt in range(2):
                    P.pe(MM(ps[5][:, 0:32], onesb[:], E4[:, :, mt, :], mt == 0, mt == 1), r=["onesb", ke], w=[psk[5]])
                rd, krd = tmp()
                P.dve(RCP(rd[:, 0:32], ps[5][:, 0:32]), r=[psk[5]], w=[krd])
                for h in range(4):
                    for dvc in range(2):
                        for mt in range(2):
                            P.pe(MM(ps[6][:, (h * 2 + dvc) * 8:(h * 2 + dvc + 1) * 8],
                                    Vc[mt][0][:, h * 256 + dvc * 128:h * 256 + (dvc + 1) * 128],
                                    E[:, (h * 2 + mt) * 8:(h * 2 + mt + 1) * 8], mt == 0, mt == 1), r=[Vc[mt][1], ke], w=[psk[6]])
                o1, ko1 = tmp()
                rd4 = rd[:, 0:32].rearrange("p (h t) -> p h t", h=4).unsqueeze(2).to_broadcast([128, 4, 2, 8])
                P.dve(TT(o1[:, 0:64].rearrange("p (h d t) -> p h d t", h=4, d=2),
                         ps[6][:, 0:64].rearrange("p (h d t) -> p h d t", h=4, d=2), rd4, ALU.mult), r=[psk[6], krd], w=[ko1])
                P.pool(TT(yT[:, :, 1024 + b * 8:1024 + (b + 1) * 8], o1[:, 0:64].rearrange("p (c t) -> p c t", c=8),
                          sgms[:, :, b * 8:(b + 1) * 8], ALU.mult), r=[ko1, "sgms"], w=["yT2"])
            tmp_excl.clear()

        def out_phase(p_, tblk):
            s0 = w_acquire(G_OUT); s1 = w_acquire(G_OUT + 1)
            for j in range(8):
                for tb, (c0, n) in enumerate(tblk):
                    eng = P.pool if (j + tb) % 2 else P.dve
                    eng(CP(yT[:, j, c0:c0 + n], acc[:, j, c0:c0 + n]), r=[f"acc{j}_{tb}"], w=[f"yT{tb}"])
            for li, gi in enumerate(PASS_TILES[p_]):
                tb = li // 4
                q = li % 2
                xt_, kx = (xt, "xt") if q == 0 else (xt2, "xt2")
                xn_, kn = (xnb, "xnb") if q == 0 else (xnb2, "xnb2")
                o = 8 * q
                P.dma("sp", xt_[:], xall_d[gi], w=[kx], stream=kx)
                hh = (tmp(), tmp())
                for hf, s in enumerate((s0, s1)):
                    bank, bk = ps[(li % 2) * 2 + hf], psk[(li % 2) * 2 + hf]
                    for kc in range(8):
                        P.pe(MM(bank[:, :], yT[:, kc, li * 128:(li + 1) * 128], ring[s][:, kc, :], kc == 0, kc == 7),
                             r=[f"yT{tb}", f"ring{s}"], w=[bk])
                    P.dve(TT(hh[hf][0][:, :], bank[:, :], xt_[:, hf * 512:(hf + 1) * 512], ALU.add), r=[bk, kx], w=[hh[hf][1]])
                    P.act(ACTF(xn_[:, hf * 512:(hf + 1) * 512], hh[hf][0][:, :], AF.Square, accum=ssb[:, o + 4 + hf:o + 5 + hf]),
                          r=[hh[hf][1]], w=[kn, f"ss{o + 4 + hf}"])
                P.dve(TT(ssb[:, o + 6:o + 7], ssb[:, o + 4:o + 5], ssb[:, o + 5:o + 6], ALU.add), r=[f"ss{o + 4}", f"ss{o + 5}"], w=[f"ss{o + 6}"])
                P.act(ACTF(ssb[:, o + 7:o + 8], ssb[:, o + 6:o + 7], AF.Ln, bias=EPS, scale=1.0 / D), r=[f"ss{o + 6}"], w=[f"ss{o + 7}"])
                P.act(ACTF(ssb[:, o + 3:o + 4], ssb[:, o + 7:o + 8], AF.Exp, scale=-0.5), r=[f"ss{o + 7}"], w=[f"ss{o + 3}"])
                for hf in range(2):
                    P.dve(STT(hh[hf][0][:, :], hh[hf][0][:, :], ssb[:, o + 3:o + 4], gfin[:, hf * 512:(hf + 1) * 512], ALU.mult, ALU.mult),
                          r=[hh[hf][1], f"ss{o + 3}", "gfin"], w=[hh[hf][1]])
                    P.dma("sp", y_d[gi][:, hf * 512:(hf + 1) * 512], hh[hf][0][:, :], r=[hh[hf][1]], w=[f"o_y{gi}_{hf}"],
                          stream="o_" + hh[hf][1])

        C0 = float(np.exp(-0.5))
        accf = acc[:].rearrange("p j t -> p (j t)")
        roff = [0]
        def ralloc(n32):
            a = accf[:, roff[0]:roff[0] + n32]; roff[0] += n32
            assert roff[0] <= 8 * NTP
            return a
        shb = [ralloc(514).rearrange("p (a t) -> p a t", a=2) for _ in range(4)]
        shs = [ralloc(144).rearrange("p (b t) -> p b t", t=9) for _ in range(4)]
        wdad = ralloc(576).bitcast(BF16)
        AR = ralloc(512).bitcast(BF16).rearrange("p (a t) -> p a t", a=2)
        BTt = ralloc(256).bitcast(BF16); KTt = ralloc(256).bitcast(BF16); VTt = ralloc(256).bitcast(BF16)
        Ms = ralloc(1024).rearrange("p (b v) -> p b v", v=64)
        Msb = ralloc(512).bitcast(BF16).rearrange("p (b v) -> p b v", v=64)
        Sin = ralloc(2048).rearrange("p (b c) -> p b c", c=128)
        UTs = ralloc(128)
        RKEYS = ["shb0_0", "shb1_0", "shb2_0", "shb3_0", "shb0_1", "shb1_1", "shb2_1", "shb3_1", "AR0", "AR1", "BTt0", "BTt1", "KTt0", "KTt1", "VTt0", "VTt1", "pc0", "pc1", "shs0", "shs1", "shs2", "shs3", "wdad", "AR", "BTt", "KTt", "VTt",
                 "Ms", "Msb", "Sin", "UTs"]
        loraW = sb("loraW", (128, D), BF16); blkb = sb("blkb", (128, 128), BF16)
        Mst = sb("Mst", (128, 8, 64)); M0b = sb("M0b", (128, 64), BF16)
        pcarry = sb("pcarry", (128, 25)); pls = sb("pls", (128, 25, NB)); shT = sb("shT", (128, 25, NB))
        pc = sb("pc", (128, 16)); maskP = sb("maskP", (128, 512)); maskS = sb("maskS", (128, 128))
        dummy = sb("dummy", (128, 4))
        big16 = sb("big16", (128, 2688 + 1024), BF16); Utok = sb("Utok", (128, 128), BF16)
        Mtmp = sb("Mtmp", (128, 64))
        Ubm = Kb[:].rearrange("p a n -> p (a n)").rearrange("p (b c) -> p b c", c=128)
        Mmask = consts[:, C_RESET:C_RESET + 16]

        hv = sb("hv", (128, 32))

        def rwkv_setup():
            for q_, off_ in enumerate((V_W0, V_A0, V_LNG, V_LNB)):
                P.dve(TS(hv[:, 8 * q_:8 * q_ + 8], vecs[:, off_:off_ + 8], 0.5, None, ALU.mult), r=["vecs"], w=["hv"])
            P.dma("pool", loraW[:], lora_d[:, :], w=["loraW"], stream="c_lora")
            P.dve(CP(blkb[:], consts[:, C_BLK:C_BLK + 128]), r=["consts"], w=["blkb"])
            P.dve(MSET(Mst[:], 0.0), w=["Mst"])
            P.dve(MSET(pcarry[:], 0.0), w=["pcarry"])
            P.dve(MSET(maskP[:], 1.0), w=["maskP"])
            P.dve(MSET(maskP[:].rearrange("p (c t) -> p c t", t=128)[:, :, 0:1], 0.0), w=["maskP"])
            P.dve(MSET(maskS[:], 1.0), w=["maskS"])
            P.dve(MSET(maskS[:].rearrange("p (c t) -> p c t", t=NT)[:, :, 0:1], 0.0), w=["maskS"])
            for half, (a, b_) in enumerate(((0, 13), (13, 25))):
                t, tk = tmp(); t2, tk2 = tmp(); t3, tk3 = tmp(); t4, tk4 = tmp()
                tl = [(t, tk), (t2, tk2), (t3, tk3), (t4, tk4)]
                for q in range(4):
                    lo = a * 128 + q * 512; hi = min(lo + 512, b_ * 128)
                    if hi > lo:
                        P.dma("sp", tl[q][0][0:NB, 0:hi - lo], sshift_d[:, lo:hi], w=[tl[q][1]], stream="l_" + tl[q][1])
                for ti in range(a, b_):
                    q, o = divmod((ti - a) * 128, 512)
                    P.pe(TR(ps[5][:, (ti - a) * 16:(ti - a + 1) * 16], tl[q][0][0:NB, o:o + 128], ident[0:NB, 0:NB]),
                         r=[tl[q][1], "consts"], w=[psk[5]])
                P.act(CPA(shT[:, a:b_, :].rearrange("p c b -> p (c b)"), ps[5][:, 0:(b_ - a) * 16]), r=[psk[5]], w=["shT"])

        def shift_lerp(bank, bkey, n, bi, tile_idx, out, okey, first, last, par):
            mu = vcol(V_MU, tile_idx)
            d, dk = tmps[11][:, par * 256:(par + 1) * 256], f"tmp11_{par}"
            if n == 256:
                b_, k_ = shb[bi][:, par, :], f"shb{bi}_{par}"
                o_, ok_ = shb[bi][:, 1 - par, :], f"shb{bi}_{1 - par}"
                if first:
                    P.pool(CP(b_[:, 0:1], pcarry[:, tile_idx:tile_idx + 1]), r=["pcarry"], w=[k_])
                else:
                    P.pool(CP(b_[:, 0:1], o_[:, 256:257]), r=[ok_], w=[k_])
                P.act(CPA(b_[:, 1:257], bank[:, 0:256]), r=[bkey], w=[k_])
                P.dve(TT(d[:, :], b_[:, 0:256], b_[:, 1:257], ALU.subtract), r=[k_], w=[dk])
                P.dve(STT(out[:, 0:256], d[:, :], mu, b_[:, 1:257], ALU.mult, ALU.add), r=[dk, k_, "vecs"], w=[okey])
                if last:
                    P.pool(CP(pcarry[:, tile_idx:tile_idx + 1], b_[:, 256:257]), r=[k_], w=["pcarry"])
            else:
                b_, k_ = shs[bi], f"shs{bi}"
                P.pool(CP(b_[:, :, 0:1], shT[:, tile_idx, :].unsqueeze(2)), r=["shT"], w=[k_])
                P.act(CPA(b_[:, :, 1:9], bank[:, 0:128].rearrange("p (b t) -> p b t", t=NT)), r=[bkey], w=[k_])
                d3 = d[:, 0:128].rearrange("p (b t) -> p b t", t=NT)
                P.dve(TT(d3, b_[:, :, 0:8], b_[:, :, 1:9], ALU.subtract), r=[k_], w=[dk])
                P.dve(STT(out[:, 0:128].rearrange("p (b t) -> p b t", t=NT), d3, mu, b_[:, :, 1:9], ALU.mult, ALU.add),
                      r=[dk, k_, "vecs"], w=[okey])
                P.pool(CP(pls[:, tile_idx, :].unsqueeze(2), b_[:, :, 8:9]), r=[k_], w=["pls"])

        TM = lambda i: (tmps[i], f"tmp{i}")
        TMp = lambda i, par: (tmps[i][:, par * 256:(par + 1) * 256], f"tmp{i}_{par}")
        TKEYS = [f"tmp{i}" for i in range(NTMP)] + [f"tmp{i}_{p}" for i in range(NTMP) for p in range(2)]
        pc2 = sb("pc2", (128, 4))
        psT0 = ps[0][:, :].bitcast(BF16)
        Utok2 = sb("Utok2", (128, 128), BF16)
        Vbm = Vb[:].rearrange("p a n -> p (a n)").rearrange("p (b c) -> p b c", c=128)
        Mtmp2 = sgms[:, 0:4, :].rearrange("p a (c v) -> p (a c) v", v=64)

        def rblocks(tblk):
            out = []
            for (c0, n) in tblk:
                if n == 512:
                    out += [(c0, 256), (c0 + 256, 256)]
                else:
                    out.append((c0, n))
            return out

        def lora_part(p_, tblk):
            s = w_acquire(G_LORA)
            rb = rblocks(tblk)
            npb = sum(1 for (_, n) in rb if n == 256)
            for bi_, (c0, n) in enumerate(rb):
                par = bi_ % 2; tb = c0 // 512
                bank, bk = ps[4 * par], psk[4 * par]
                mm8(bank[:, 0:n], bk, s, 0, lambda kc: xnT[:, kc, c0:c0 + n], [f"xnT{tb}"])
                o, ok = TMp(0, par)
                shift_lerp(bank, bk, n, 3, 24, o, ok, bi_ == 0, n == 256 and bi_ == npb - 1, par)
                P.act(ACTF(wdad[0:64, c0:c0 + n], o[0:64, 0:n], AF.Tanh), r=[ok], w=["wdad"])
                P.act(CPA(wdad[64:128, c0:c0 + n], o[64:128, 0:n]), r=[ok], w=["wdad"])

        def prep(j, tb, c0, n, s, first, last, par):
            P.marks.append((f"prep{j}_{tb}", len(P.ops)))
            B = 4 * par
            ARp = AR[:, :, par * 256:(par + 1) * 256]; kAR = f"AR{par}"
            BTp = BTt[:, par * 256:(par + 1) * 256]; kBT = f"BTt{par}"
            KTp = KTt[:, par * 256:(par + 1) * 256]; kKT = f"KTt{par}"
            VTp = VTt[:, par * 256:(par + 1) * 256]; kVT = f"VTt{par}"
            N = slice(0, n)
            for ct in range(4):
                mm8(ps[B + ct][:, 0:n], psk[B + ct], s, ct * 128, lambda kc: xnT[:, kc, c0:c0 + n], [f"xnT{tb}"])
            (rs, krs), (ks, kks), (vs, kvs) = TMp(0, par), TMp(1, par), TMp(2, par)
            (t3, k3), (t4, k4), (t5, k5), (t6, k6), (t7, k7), (t8, k8), (t9, k9) = [TMp(i, par) for i in range(3, 10)]
            P.act(ACTF(t9[:, N], ps[B + 3][:, N], AF.Tanh, scale=0.5), r=[psk[B + 3]], w=[k9])
            P.dve(STT(t9[:, N], t9[:, N], 1.0, ps[B + 3][:, N], ALU.add, ALU.mult), r=[k9, psk[B + 3]], w=[k9])
            shift_lerp(ps[B + 0], psk[B + 0], n, 0, j, rs, krs, first, last, par)
            shift_lerp(ps[B + 1], psk[B + 1], n, 1, 8 + j, ks, kks, first, last, par)
            shift_lerp(ps[B + 2], psk[B + 2], n, 2, 16 + j, vs, kvs, first, last, par)
            P.pe(MM(ps[B + 0][:, 0:n], loraW[0:64, j * 128:(j + 1) * 128], wdad[0:64, c0:c0 + n], True, True),
                 r=["loraW", "wdad"], w=[psk[B + 0]])
            P.pe(MM(ps[B + 1][:, 0:n], loraW[64:128, j * 128:(j + 1) * 128], wdad[64:128, c0:c0 + n], True, True),
                 r=["loraW", "wdad"], w=[psk[B + 1]])
            P.pool(CP(VTp[:, N], vs[:, N]), r=[kvs], w=[kVT])
            P.act(ACTF(t3[:, N], ps[B + 0][:, N], AF.Tanh, bias=hv[:, j:j + 1], scale=0.5), r=[psk[B + 0], "hv"], w=[k3])
            P.act(ACTF(t4[:, N], ps[B + 1][:, N], AF.Tanh, bias=hv[:, 8 + j:9 + j], scale=0.5), r=[psk[B + 1], "hv"], w=[k4])
            P.pool(TS(t3[:, N], t3[:, N], 0.5, 0.5, ALU.mult, ALU.add), r=[k3], w=[k3])
            P.pool(TS(t4[:, N], t4[:, N], 0.5, 0.5, ALU.mult, ALU.add), r=[k4], w=[k4])
            mask, mkey = (maskP, "maskP") if n == 256 else (maskS, "maskS")
            P.dve(lambda e: e.tensor_tensor_scan(out=t5[:, N], data0=mask[:, N], data1=t3[:, N], initial=0.0,
                                                 op0=ALU.mult, op1=ALU.add), r=[mkey, k3], w=[k5])
            P.dve(TT(t3[:, N], t5[:, N], t3[:, N], ALU.subtract), r=[k5, k3], w=[k3])
            P.act(ACTF(t6[:, N], t5[:, N], AF.Exp, scale=-C0), r=[k5], w=[k6])
            P.act(ACTF(t5[:, N], t5[:, N], AF.Exp, scale=C0), r=[k5], w=[k5])
            P.act(ACTF(t3[:, N], t3[:, N], AF.Exp, scale=-C0), r=[k3], w=[k3])
            if n == 256:
                P.pool(CP(pc2[:, 2 * par:2 * par + 2].unsqueeze(2), t6[:, N].rearrange("p (c t) -> p c t", t=128)[:, :, 127:128]), r=[k6], w=[f"pc{par}"])
            else:
                P.pool(CP(pc[:, 0:16].unsqueeze(2), t6[:, N].rearrange("p (c t) -> p c t", t=NT)[:, :, NT - 1:NT]), r=[k6], w=["pc"])
            P.pool(TT(ARp[:, 1, N], rs[:, N], t6[:, N], ALU.mult), r=[krs, k6], w=[kAR])
            P.dve(TS(t6[:, N], ks[:, N], vcol(V_KK, j), None, ALU.mult), r=[kks, "vecs"], w=[k6])
            t7b = t7[:, 0:128].bitcast(BF16)
            P.dve(TT(t7b[:, N], t6[:, N], t6[:, N], ALU.mult), r=[k6], w=[k7])
            P.pe(MM(ps[B + 2][:, N], blkb[:], t7b[:, N], True, True), r=["blkb", k7], w=[psk[B + 2]])
            P.act(ACTF(t7[:, N], ps[B + 2][:, N], AF.Ln, bias=1e-12), r=[psk[B + 2]], w=[k7])
            P.act(ACTF(t7[:, N], t7[:, N], AF.Exp, scale=-0.5), r=[k7], w=[k7])
            P.dve(TT(t6[:, N], t6[:, N], t7[:, N], ALU.mult), r=[k6, k7], w=[k6])
            P.dve(STT(ARp[:, 0, N], t6[:, N], -1.0, t3[:, N], ALU.mult, ALU.mult), r=[k6, k3], w=[kAR])
            P.pool(TT(t3[:, N], t6[:, N], t4[:, N], ALU.mult), r=[k6, k4], w=[k3])
            P.pool(TT(BTp[:, N], t3[:, N], t5[:, N], ALU.mult), r=[k3, k5], w=[kBT])
            P.dve(TS(t7[:, N], t4[:, N], -1.0, vcol(V_KA, j), ALU.add, ALU.mult), r=[k4, "vecs"], w=[k7])
            P.dve(STT(t7[:, N], t7[:, N], 1.0, ks[:, N], ALU.add, ALU.mult), r=[k7, kks], w=[k7])
            P.pool(TT(KTp[:, N], t7[:, N], t5[:, N], ALU.mult), r=[k7, k5], w=[kKT])
            t3b = t3[:, 0:128].bitcast(BF16)
            P.dve(STT(t3b[:, N], rs[:, N], vcol(V_RK, j), t7[:, N], ALU.mult, ALU.mult), r=[krs, k7, "vecs", k3], w=[k3])
            P.pe(MM(ps[B + 3][:, N], blkb[:], t3b[:, N], True, True), r=["blkb", k3], w=[psk[B + 3]])
            P.dve(STT(t8[:, N], ps[B + 3][:, N], 0.5, vs[:, N], ALU.mult, ALU.mult), r=[psk[B + 3], kvs], w=[k8])

        def post(j, tb, c0, n, par):
            P.marks.append((f"post{j}_{tb}", len(P.ops)))
            B = 4 * par
            N = slice(0, n)
            (ot, ko), (t8, k8), (t9, k9) = TMp(10, par), TMp(8, par), TMp(9, par)
            (a0, ka0), (a1, ka1), (a2, ka2) = TMp(0, par), TMp(1, par), TMp(2, par)
            ob = a0[:, 0:128].bitcast(BF16); o2b = a1[:, 0:128].bitcast(BF16)
            P.pool(CP(ob[:, N], ot[:, N]), r=[ko], w=[ka0])
            P.dve(TT(o2b[:, N], ot[:, N], ot[:, N], ALU.mult), r=[ko], w=[ka1])
            P.pe(MM(ps[B + 2][:, N], blkb[:], ob[:, N], True, True), r=["blkb", ka0], w=[psk[B + 2]])
            P.pe(MM(ps[B + 3][:, N], blkb[:], o2b[:, N], True, True), r=["blkb", ka1], w=[psk[B + 3]])
            P.act(MULA(a2[:, N], ps[B + 2][:, N], 1.0 / 64), r=[psk[B + 2]], w=[ka2])
            P.dve(TT(a0[:, N], a2[:, N], a2[:, N], ALU.mult), r=[ka2], w=[ka0])
            P.dve(STT(a1[:, N], ps[B + 3][:, N], 1.0 / 64, a0[:, N], ALU.mult, ALU.subtract), r=[psk[B + 3], ka0], w=[ka1])
            P.act(ACTF(a1[:, N], a1[:, N], AF.Ln, bias=GN_EPS), r=[ka1], w=[ka1])
            P.act(ACTF(a1[:, N], a1[:, N], AF.Exp, scale=-0.5), r=[ka1], w=[ka1])
            P.dve(TT(a0[:, N], ot[:, N], a2[:, N], ALU.subtract), r=[ko, ka2], w=[ka0])
            P.dve(TT(a0[:, N], a0[:, N], a1[:, N], ALU.mult), r=[ka0, ka1], w=[ka0])
            P.dve(TS(a0[:, N], a0[:, N], hv[:, 16 + j:17 + j], hv[:, 24 + j:25 + j], ALU.mult, ALU.add), r=[ka0, "hv"], w=[ka0])
            P.pool(TT(a0[:, N], a0[:, N], t8[:, N], ALU.add), r=[ka0, k8], w=[ka0])
            P.pool(TT(yT[:, j, c0:c0 + n], a0[:, N], t9[:, N], ALU.mult), r=[ka0, k9], w=[f"yT{tb}"])

        DONORS = [(memnT[:].rearrange("p a b -> p (a b)"), 2048), (KTb[:].rearrange("p a b -> p (a b)"), 2048),
                  (qTs[:].rearrange("p a b -> p (a b)"), 1024), (U[:, 0:1024].bitcast(BF16), 2048),
                  (xt[:, :].bitcast(BF16), 2048), (xnb[:, :], 1024), (big16[:, :], 2688 + 1024)]
        DONOR_KEYS = ["memnT", "KTb", "qTs", "U", "xt", "xnb", "Kb", "Vb", "Kb0", "Kb1", "Vb0", "Vb1", "sgms"]
        dst_ = {"d": 0, "o": 0}
        def dalloc(n):
            while DONORS[dst_["d"]][1] - dst_["o"] < n:
                dst_["d"] += 1; dst_["o"] = 0
            a = DONORS[dst_["d"]][0][:, dst_["o"]:dst_["o"] + n]; dst_["o"] += n
            return a
        NU = 8
        uSB = [dalloc(256) for _ in range(NU)]; uSK = [dalloc(256) for _ in range(NU)]
        uXY = [dalloc(256).rearrange("p (a t) -> p a t", a=2) for _ in range(NU)]
        uZ = [dalloc(128) for _ in range(NU)]; uLV = [dalloc(64) for _ in range(NU)]
        cTok = [dalloc(512).rearrange("p (q t) -> p q t", q=4) for _ in range(4)]
        cWT = [dalloc(128) for _ in range(4)]; cU0 = [dalloc(256).bitcast(F32) for _ in range(4)]
        UKEYS = ([f"uSB{u}" for u in range(NU)] + [f"uSK{u}" for u in range(NU)] + [f"uXY{u}" for u in range(NU)] +
                 [f"uZ{u}" for u in range(NU)] + [f"uLV{u}" for u in range(NU)] + [f"cTok{c}" for c in range(4)] +
                 [f"cWT{c}" for c in range(4)] + [f"cU0{c}" for c in range(4)])

        def scan_block(j, nch, sample, ot, ko, par):
            NL = 2 if sample else 6
            mSI = consts[:, C_BD_S:C_BD_S + 256] if sample else consts[:, C_TRIL_S:C_TRIL_S + 256]
            mLT = consts[:, C_BD_LT:C_BD_LT + 128] if sample else consts[:, C_LT:C_LT + 128]
            units = [(c, hh) for c in range(nch) for hh in range(2)]
            B = 4 * par
            UB = 4 * par
            CB = 2 * par
            AR_ = AR[:, :, par * 256:(par + 1) * 256]; kAR = f"AR{par}"
            BT_ = BTt[:, par * 256:(par + 1) * 256]; kBT = f"BTt{par}"
            KT_ = KTt[:, par * 256:(par + 1) * 256]; kKT = f"KTt{par}"
            VT_ = VTt[:, par * 256:(par + 1) * 256]; kVT = f"VTt{par}"
            CS = lambda c: slice(c * 128, (c + 1) * 128)
            SL = lambda hh: slice(64 * hh, 64 * hh + 64)
            for c in range(nch):
                pT = ps[B + c][:, :].bitcast(BF16)
                for q, (src, skey) in enumerate(((AR_[:, 0, CS(c)], kAR), (BT_[:, CS(c)], kBT), (KT_[:, CS(c)], kKT), (VT_[:, CS(c)], kVT))):
                    P.pe(TR(pT[:, q * 128:(q + 1) * 128], src, identb[:]), r=[skey, "identb"], w=[psk[B + c]])
            for c in range(nch):
                pT = ps[B + c][:, :].bitcast(BF16)
                P.act(CPA(cTok[CB + c][:].rearrange("p q t -> p (q t)"), pT[:, 0:512]), r=[psk[B + c]], w=[f"cTok{CB + c}"])
            P.marks.append((f"S2_{j}", len(P.ops)))
            for u, (c, hh) in enumerate(units):
                sl = SL(hh)
                P.pe(MM(ps[B + u][:, 0:256], BT_[sl, CS(c)], AR_[sl, :, CS(c)], True, True), r=[kBT, kAR], w=[psk[B + u]])
                P.pe(MM(ps[B + u][:, 256:512], KT_[sl, CS(c)], AR_[sl, :, CS(c)], True, True), r=[kKT, kAR], w=[psk[B + u]])
            for u, (c, hh) in enumerate(units):
                P.dve(TT(uSB[UB + u][:, :], ps[B + u][:, 0:256], mSI, ALU.mult), r=[psk[B + u], "consts"], w=[f"uSB{UB + u}"])
                P.dve(TT(uSK[UB + u][:, :], ps[B + u][:, 256:512], mSI, ALU.mult), r=[psk[B + u], "consts"], w=[f"uSK{UB + u}"])
            for u, (c, hh) in enumerate(units):
                sl = SL(hh)
                P.pe(MM(ps[B + u][:, 0:128], AR_[sl, 0, CS(c)], BT_[sl, CS(c)], True, True), r=[kBT, kAR], w=[psk[B + u]])
                P.pe(MM(ps[B + u][:, 384:448], uSK[UB + u][:, 0:128], cTok[CB + c][:, 3, sl], True, True), r=[f"uSK{UB + u}", f"cTok{CB + c}"], w=[psk[B + u]])
            for u, (c, hh) in enumerate(units):
                P.dve(TT(uXY[UB + u][:, 1, :], ps[B + u][:, 0:128], mLT, ALU.mult), r=[psk[B + u], "consts"], w=[f"uXY{UB + u}"])
                P.act(CPA(uLV[UB + u][:, :], ps[B + u][:, 384:448]), r=[psk[B + u]], w=[f"uLV{UB + u}"])
                P.pool(TT(uZ[UB + u][:, :], uSB[UB + u][:, 0:128], identb[:], ALU.add), r=[f"uSB{UB + u}", "identb"], w=[f"uZ{UB + u}"])
            P.marks.append((f"S3_{j}", len(P.ops)))
            nU = len(units)
            for n_ in range(1, NL + 1):
                last = (n_ == NL)
                for u in range(nU):
                    U_ = UB + u
                    Yp = uSB[U_][:, 0:128] if n_ == 1 else uXY[U_][:, 0, :]
                    ykey = f"uSB{U_}" if n_ == 1 else f"uXY{U_}"
                    Xp = uXY[U_][:, 1, :]
                    if not last:
                        P.pe(MM(ps[B + u][:, 0:128], Xp, Yp, True, True), r=[f"uXY{U_}", ykey], w=[psk[B + u]])
                    P.pe(MM(ps[B + u][:, 128:256], Yp, Xp, True, True), r=[f"uXY{U_}", ykey], w=[psk[B + u]])
                for u in range(nU):
                    U_ = UB + u
                    if not last:
                        P.act(CPA(uXY[U_][:].rearrange("p a t -> p (a t)"), ps[B + u][:, 0:256]), r=[psk[B + u]], w=[f"uXY{U_}"])
                    else:
                        P.act(CPA(uXY[U_][:, 1, :], ps[B + u][:, 128:256]), r=[psk[B + u]], w=[f"uXY{U_}"])
                for u in range(nU):
                    U_ = UB + u
                    P.pe(MM(ps[B + u][:, 256:384], uXY[U_][:, 1, :], uZ[U_][:, :], True, True), r=[f"uXY{U_}", f"uZ{U_}"], w=[psk[B + u]])
                for u in range(nU):
                    U_ = UB + u
                    P.dve(TT(uZ[U_][:, :], ps[B + u][:, 256:384], uZ[U_][:, :], ALU.add), r=[psk[B + u], f"uZ{U_}"], w=[f"uZ{U_}"])
            P.marks.append((f"S4_{j}", len(P.ops)))
            for u, (c, hh) in enumerate(units):
                sl = SL(hh)
                P.pe(MM(ps[B + 2 * c][sl, 0:128], cTok[CB + c][:, 0, sl], uZ[UB + u][:, :], True, True), r=[f"cTok{CB + c}", f"uZ{UB + u}"], w=[psk[B + 2 * c]])
                if not sample:
                    P.pe(MM(ps[B + 2 * c + 1][:, 64 * hh:64 * hh + 64], uZ[UB + u][:, :], uLV[UB + u][:, :], True, True), r=[f"uZ{UB + u}", f"uLV{UB + u}"], w=[psk[B + 2 * c + 1]])
                else:
                    P.pe(MM(ps[B + 2 * c + 1][sl, 0:128], uLV[UB + u][:, :], uZ[UB + u][:, :], True, True), r=[f"uZ{UB + u}", f"uLV{UB + u}"], w=[psk[B + 2 * c + 1]])
            for c in range(nch):
                P.act(CPA(cWT[CB + c][:, :], ps[B + 2 * c][:, 0:128]), r=[psk[B + 2 * c]], w=[f"cWT{CB + c}"])
                P.dve(CP(cU0[CB + c][:, :], ps[B + 2 * c + 1][:, 0:128]), r=[psk[B + 2 * c + 1]], w=[f"cU0{CB + c}"])
            P.marks.append((f"S5_{j}", len(P.ops)))
            for c in range(nch):
                tk, tkk = cTok[CB + c], f"cTok{CB + c}"
                bU, kU = ps[B + 0], psk[B + 0]
                if not sample:
                    for hh in range(2):
                        P.pe(MM(bU[:, 64 * hh:64 * hh + 64], cWT[CB + c][SL(hh), :], M0b[SL(hh), :], True, True), r=[f"cWT{CB + c}", "M0b"], w=[kU], force=(hh == 1))
                    P.dve(TT(Utok[:, :], bU[:, 0:128], cU0[CB + c][:, :], ALU.add), r=[kU, f"cU0{CB + c}"], w=["Utok"])
                    ut, utk = Utok, "Utok"
                else:
                    for hh in range(2):
                        for b in range(NB):
                            P.pe(MM(bU[SL(hh), 8 * b:8 * b + 8], Msb[SL(hh), b, :], cWT[CB + c][SL(hh), 8 * b:8 * b + 8], True, True),
                                 r=["Msb", f"cWT{CB + c}"], w=[kU], force=(hh == 1 and b == 0))
                    P.dve(TT(Utok[:, :], bU[:, 0:128], cU0[CB + c][:, :], ALU.add), r=[kU, f"cU0{CB + c}"], w=["Utok"])
                    pT = ps[B + 3][:, :].bitcast(BF16)
                    P.pe(TR(pT[:, 0:128], Utok[:, :], identb[:]), r=["Utok", "identb"], w=[psk[B + 3]])
                    P.act(CPA(Utok2[:, :], pT[:, 0:128]), r=[psk[B + 3]], w=["Utok2"])
                    ut, utk = Utok2, "Utok2"
                bO, kO = ps[B + 1], psk[B + 1]
                for hh in range(2):
                    u = UB + 2 * c + hh; sl = SL(hh)
                    out = bO[sl, 0:128]
                    if not sample:
                        P.pe(MM(out, M0b[sl, :], AR_[sl, 1, CS(c)], True, False), r=["M0b", kAR], w=[kO], force=(hh == 1))
                    else:
                        for b in range(NB):
                            P.pe(MM(bO[sl, 8 * b:8 * b + 8], Msb[sl, b, :], AR_[sl, 1, 8 * b:8 * b + 8], b == 0, False),
                                 r=["Msb", kAR], w=[kO], force=(hh == 1 and b == 0))
                    P.pe(MM(out, tk[:, 3, sl], uSK[u][:, 128:256], False, False), r=[tkk, f"uSK{u}"], w=[kO])
                    P.pe(MM(out, ut[:, sl], uSB[u][:, 128:256], False, True), r=[utk, f"uSB{u}"], w=[kO])
                if not sample:
                    bM, kM = ps[B + 2], psk[B + 2]
                    for hh in range(2):
                        sl = SL(hh)
                        out = bM[sl, 128:192]
                        P.pe(MM(out, tk[:, 2, sl], tk[:, 3, sl], True, False), r=[tkk], w=[kM])
                        P.pe(MM(out, tk[:, 1, sl], ut[:, sl], False, True), r=[tkk, utk], w=[kM])
                    P.dve(TT(Mtmp[:, :], bM[:, 128:192], Mst[:, j, :], ALU.add), r=[kM, "Mst"], w=["Mtmp"])
                    P.dve(TS(M0b[:, :], Mtmp[:, :], pc2[:, 2 * par + c:2 * par + c + 1], None, ALU.mult), r=["Mtmp", f"pc{par}"], w=["M0b"])
                    P.pool(TS(Mst[:, j, :], Mtmp[:, :], pc2[:, 2 * par + c:2 * par + c + 1], None, ALU.mult), r=["Mtmp", f"pc{par}"], w=["Mst"])
                else:
                    mm3 = Mmask.unsqueeze(2).to_broadcast([128, NB, 128])
                    P.dve(TT(Ubm[:], ut[:, :].unsqueeze(1).to_broadcast([128, NB, 128]), mm3, ALU.mult), r=[utk, "consts"], w=["Kb"])
                    P.pool(TT(Vbm[:], tk[:, 3, :].unsqueeze(1).to_broadcast([128, NB, 128]), mm3, ALU.mult), r=[tkk, "consts"], w=["Vb"])
                    for half in range(2):
                        bank, bk = ps[6 + half], psk[6 + half]
                        bs = slice(8 * half, 8 * half + 8)
                        for hh in range(2):
                            sl = SL(hh)
                            out = bank[sl, :].rearrange("p (b v) -> p b v", v=64)
                            P.pe(MM(out, tk[:, 2, sl], Vbm[:, bs, sl], True, False), r=[tkk, "Vb"], w=[bk])
                            P.pe(MM(out, tk[:, 1, sl], Ubm[:, bs, sl], False, True), r=[tkk, "Kb"], w=[bk])
                        P.dve(TT(Mtmp2[:], bank[:, :].rearrange("p (b v) -> p b v", v=64), Ms[:, bs, :], ALU.add), r=[bk, "Ms"], w=["sgms"])
                        P.dve(TT(Ms[:, bs, :], Mtmp2[:], pc[:, bs].unsqueeze(2).to_broadcast([128, 8, 64]), ALU.mult),
                              r=["sgms", "pc"], w=["Ms"])
                P.act(CPA(ot[:, CS(c)], bO[:, 0:128]), r=[kO], w=[ko])

        ACCKEYS = [f"acc{j}_{tb}" for j in range(8) for tb in range(3)]

        def load_sample_state(j):
            for h in range(2):
                P.dma("sp", Sin[0:64, :, h * 64:(h + 1) * 64], swkv_d[:, 2 * j + h].rearrange("b v k -> v b k"),
                      w=["Sin"], stream=f"Sin{h}")
            for half in range(2):
                bank, bk = ps[5 + half], psk[5 + half]
                for bb in range(8):
                    b = half * 8 + bb
                    P.pe(TR(bank[:, bb * 64:(bb + 1) * 64], Sin[0:64, b, :], ident[0:64, 0:64]), r=["Sin", "consts"], w=[bk])
                bs = slice(8 * half, 8 * half + 8)
                P.dve(CP(Ms[:, bs, :], bank[:, :].rearrange("p (b v) -> p b v", v=64)), r=[bk], w=["Ms"])
                P.act(CPA(Msb[:, bs, :], bank[:, :].rearrange("p (b v) -> p b v", v=64)), r=[bk], w=["Msb"])

        def store_sample_state(j):
            for q in range(4):
                bank, bk = ps[5 + q % 2], psk[5 + q % 2]
                for bb in range(4):
                    b = q * 4 + bb
                    P.pe(TR(bank[0:64, bb * 128:(bb + 1) * 128], Ms[:, b, :], ident), r=["Ms", "consts"], w=[bk])
                P.dve(CP(Sin[0:64, 4 * q:4 * q + 4, :], bank[0:64, :].rearrange("p (b c) -> p b c", c=128)), r=[bk], w=["Sin"])
            for h in range(2):
                P.dma("sp", wkvs_d[:, 2 * j + h].rearrange("b v k -> v b k"), Sin[0:64, :, h * 64:(h + 1) * 64],
                      r=["Sin"], w=[f"o_wkvs{j}_{h}"], stream=f"o_Sin{h}")

        def rwkv_phase(p_, tblk):
            P.dve(MSET(dummy[:, 0:1], 0.0), w=ACCKEYS + RKEYS + DONOR_KEYS + UKEYS + TKEYS)
            if p_ == 0:
                rwkv_setup()
                P.dve(MSET(dummy[:, 2:3], 0.0), w=TKEYS)
            RCUT = int(os.environ.get('RCUT', '99'))
            if RCUT >= 1:
                lora_part(p_, tblk)
            npb = sum(1 for (_, n) in tblk if n == 512)
            for j in range(8 if RCUT >= 2 else 0):
                if RCUT < 9 and j > 0:
                    break
                s = w_acquire(G_RWKV + j)
                P.act(CPA(M0b[:, :], Mst[:, j, :]), r=["Mst"], w=["M0b"])
                rb = rblocks(tblk)
                nprb = sum(1 for (_, n) in rb if n == 256)
                for bi_, (c0, n) in enumerate(rb):
                    sample = (n == 128)
                    par = bi_ % 2; tb = c0 // 512
                    if sample:
                        load_sample_state(j)
                    prep(j, tb, c0, n, s, bi_ == 0, bi_ == nprb - 1, par)
                    ot, ko = TMp(10, par)
                    if not sample:
                        scan_block(j, 2, False, ot, ko, par)
                    else:
                        scan_block(j, 1, True, ot, ko, par)
                        store_sample_state(j)
                    post(j, tb, c0, n, par)
            P.dve(MSET(dummy[:, 3:4], 0.0), w=TKEYS)
            if p_ == 0:
                for half, (a, b_) in enumerate(((0, 13), (13, 25))):
                    for ti in range(a, b_):
                        q, o = divmod((ti - a) * 128, 512)
                        P.pe(TR(ps[1 + q][0:NB, o:o + 128], pls[:, ti, :], ident), r=["pls", "consts"], w=[psk[1 + q]])
                    for q in range(4):
                        lo = a * 128 + q * 512; hi = min(lo + 512, b_ * 128)
                        if hi > lo:
                            t, tk_ = tmp()
                            P.dve(CP(t[0:NB, 0:hi - lo], ps[1 + q][0:NB, 0:hi - lo]), r=[psk[1 + q]], w=[tk_])
                            P.dma("sp", shifts_d[:, lo:hi], t[0:NB, 0:hi - lo], r=[tk_], w=[f"o_shs{lo}"], stream="o_" + tk_)
            else:
                P.pe(TR(ps[1][0:25, 0:128], pcarry[:, :], ident), r=["pcarry", "consts"], w=[psk[1]])
                t, tk_ = tmp()
                P.dve(CP(t[0:25, 0:128], ps[1][0:25, 0:128]), r=[psk[1]], w=[tk_])
                P.dma("sp", shiftp_d[0].rearrange("(c p) -> c p", p=128), t[0:25, 0:128], r=[tk_], w=["o_shp"], stream="o_" + tk_)
                for half in range(2):
                    bank, bk = ps[2 + half], psk[2 + half]
                    t, tk_ = tmp()
                    for jj in range(4):
                        P.pe(TR(bank[0:64, jj * 128:(jj + 1) * 128], Mst[:, half * 4 + jj, :], ident), r=["Mst", "consts"], w=[bk])
                    P.dve(CP(t[0:64, :], bank[0:64, :]), r=[bk], w=[tk_])
                    dst = wkvp_d[8 * half:8 * half + 8].rearrange("h v k -> v h k")
                    P.dma("sp", dst, t[0:64, :].rearrange("p (h k) -> p h k", k=64), r=[tk_], w=[f"o_wkvp{half}"], stream="o_" + tk_)
            P.dve(MSET(dummy[:, 1:2], 0.0), w=ACCKEYS + RKEYS + DONOR_KEYS + UKEYS + TKEYS)


        env = dict(locals())
        for p_ in range((2 if 'onepass' not in stages else 1) if LVL >= 4 else 0):
            tblk = PASS_TBLK[p_]
            for li, gi in enumerate(PASS_TILES[p_]):
                norm_T(xall_d[gi], V_NORM_IN, xnT[:, :, li * 128:(li + 1) * 128], [f"xnT{li // 4}"])
            nbr = 0
            if 'cutA' in stages:
                break
            if "rwkv" in stages:
                rwkv_phase(p_, tblk)
                branch(1, nbr == 0, tblk); nbr += 1
            if "conv" in stages:
                conv_phase(p_, tblk)
                branch(0, nbr == 0, tblk); nbr += 1
            if "mem" in stages:
                mem_phase(p_, tblk)
                branch(2, nbr == 0, tblk); nbr += 1
            if 'noout' not in stages:
                out_phase(p_, tblk)
        okeys = [k for k in P.lastw if k.startswith("o_")]
        P.add("sp", lambda e: e.nop(), r=okeys)
        n_ops = P.emit(es)
        print("ops", n_ops, "est_us", getattr(P, "est_us", None), "tsw", getattr(P, "n_tsw", 0), "waits", getattr(P, "n_wait", 0), "attached", getattr(P, "n_att", 0))
    return nc


def rwkv_phase(env, p_, tblk):
    raise NotImplementedError


_NC_CACHE = {}


def kernel(**inputs):
    stages = inputs.pop("_stages", ("rwkv", "conv", "mem"))
    maps = _host_inputs(inputs)
    key = tuple(stages)
    if key not in _NC_CACHE:
        _NC_CACHE[key] = build_nc(stages)
    nc = _NC_CACHE[key]
    res = run_bass_kernel_spmd(nc, maps, core_ids=list(range(NCORES)))
    R = res.results
    B = NCORES
    y_prompt = np.stack([R[c]["y"][0:16].reshape(T, D) for c in range(B)])
    y_sample = np.concatenate([R[c]["y"][16].reshape(NB, NT, D) for c in range(B)], axis=0)
    mk = np.stack([R[c]["mk"].reshape(256, 4, 256) for c in range(B)])[None]
    mv = np.stack([R[c]["mv"].reshape(256, 4, 256) for c in range(B)])[None]
    conv_p = np.stack([R[c]["convp"] for c in range(B)])[None]
    shift_p = np.stack([R[c]["shiftp"][0] for c in range(B)])[None]
    wkv_p = np.stack([R[c]["wkvp"] for c in range(B)])[None]
    conv_s = np.concatenate([R[c]["convs"].reshape(NB, 2, D) for c in range(B)], axis=0)[None]
    shift_s = np.concatenate([R[c]["shifts"] for c in range(B)], axis=0)[None]
    wkv_s = np.concatenate([R[c]["wkvs"] for c in range(B)], axis=0)[None]
    f = lambda a: np.ascontiguousarray(a, dtype=np.float32)
    return tuple(f(a) for a in (y_prompt, y_sample, mk, mv, conv_p, shift_p, wkv_p, conv_s, shift_s, wkv_s))
```

```python
import os
import numpy as np
from contextlib import ExitStack
import concourse.bass as bass
import concourse.mybir as mybir
from concourse.bass_utils import run_bass_kernel_spmd

F32 = mybir.dt.float32
BF16 = mybir.dt.bfloat16
AF = mybir.ActivationFunctionType
ALU = mybir.AluOpType
AX = mybir.AxisListType

NCORES = 8
D = 1024
T = 2048
NTOK = 2176
NB = 16
NT = 8
TBLK = [(0, 512), (512, 512), (1024, 512), (1536, 512), (2048, 128)]
EPS = 1e-6
GN_EPS = 64 * 1e-5

O_H, O_B, O_C, O_GC, O_R, O_K, O_V, O_WD, O_AD, O_GR, O_Q, O_GM, O_MC, O_MR, O_MM = (
    0, 1024, 2048, 3072, 4096, 5120, 6144, 7168, 7232, 7296, 8320, 9344, 10368, 11392, 12416)
G_CONV = 0
G_RWKV = 8
G_LORA = 16
G_MEM = 17
G_GATE = 21
G_BR = 27
G_OUT = 33
G_KV = 35
NGROUPS = 39

V_NORM_IN, V_CW0, V_CW1, V_CW2, V_CB, V_MU, V_W0, V_A0, V_KK, V_KA, V_RK, V_LNG, V_LNB, V_NMEM = (
    0, 8, 16, 24, 32, 40, 65, 73, 81, 89, 97, 105, 113, 121)
NV = 129


class _Op:
    __slots__ = ("eng", "fn", "deps", "raw", "dma", "stream", "signal", "sig_idx", "idx", "force", "cost", "lat", "tset")


class Prog:
    def __init__(self, nc):
        self.nc = nc
        self.ops = []
        self.lastw = {}
        self.readers = {}
        self.engs = {"pe": nc.tensor, "act": nc.scalar, "dve": nc.vector, "pool": nc.gpsimd, "sp": nc.sync}

    DEFC = {"pe": 0.12, "act": 0.45, "dve": 0.40, "pool": 0.60, "sp": 0.10}

    def add(self, eng, fn, r=(), w=(), dma=False, stream=None, force=False, cost=None, lat=None):
        op = _Op()
        op.force = force
        if cost is None and not dma and hasattr(fn, "free"):
            a, b = {"pe": (0.07, 0.00042), "act": (0.22, 0.00085), "dve": (0.12, 0.0011), "pool": (0.15, 0.0020), "sp": (0.1, 0.0)}[eng]
            cost = a + b * fn.free
        op.cost = cost if cost is not None else ((1.8 if eng == "pool" else 0.15) if dma else self.DEFC[eng])
        op.lat = lat if lat is not None else (4.0 if dma else 0.0)
        op.tset = getattr(fn, "tset", 0)
        op.eng, op.fn, op.dma, op.stream = eng, fn, dma, stream
        op.signal = False
        op.idx = len(self.ops)
        deps, raw = set(), set()
        psr = [k for k in r if k.startswith("ps")]
        for k in psr:
            if k in self.lastw:
                raw.add(self.lastw[k])
        r = [k for k in r if not k.startswith("ps")]
        w = list(w) + psr
        for k in r:
            if k in self.lastw:
                deps.add(self.lastw[k]); raw.add(self.lastw[k])
        for k in w:
            if k in self.lastw:
                deps.add(self.lastw[k]); raw.add(self.lastw[k])
            for x in self.readers.get(k, ()):
                deps.add(x)
        deps.discard(op.idx)
        op.deps, op.raw = deps, raw
        for k in r:
            self.readers.setdefault(k, []).append(op.idx)
        for k in w:
            self.lastw[k] = op.idx
            self.readers[k] = []
        self.ops.append(op)
        return op.idx

    def act(self, fn, r=(), w=()): return self.add("act", fn, r, w)
    def dve(self, fn, r=(), w=()): return self.add("dve", fn, r, w)
    def pool(self, fn, r=(), w=()): return self.add("pool", fn, r, w)
    def pe(self, fn, r=(), w=(), force=False): return self.add("pe", fn, r, w, force=force)

    def dma(self, eng, out, in_, r=(), w=(), stream=None, lat=None):
        assert stream is not None
        return self.add(eng, lambda e: e.dma_start(out=out, in_=in_), r, w, dma=True, stream=stream, lat=lat)

    def schedule(self, window=800):
        import heapq
        ops = self.ops; n = len(ops)
        if os.environ.get("KNOSCHED"):
            return list(range(n))
        succ = [[] for _ in ops]; indeg = [0] * n
        for op in ops:
            indeg[op.idx] = len(op.deps)
            for d in op.deps:
                succ[d].append(op.idx)
        start = [0.0] * n; finish = [0.0] * n; rtime = [0.0] * n
        blev = [0.0] * n
        for i in range(n - 1, -1, -1):
            op = ops[i]
            m = 0.0
            for sidx in succ[i]:
                v = blev[sidx] + (0.55 if ops[sidx].eng != op.eng else 0.12)
                if v > m:
                    m = v
            blev[i] = op.cost + op.lat + m
        PRI = os.environ.get("KPRI", "cp")
        ready = {e: [] for e in self.engs}
        efree = {e: 0.0 for e in self.engs}
        done = [False] * n
        for op in ops:
            if indeg[op.idx] == 0:
                ready[op.eng].append(op.idx)
        order = []; base = 0
        act_set = [0]
        TSW = 1.3
        while len(order) < n:
            while base < n and done[base]:
                base += 1
            lim = base + window
            best = None
            for e, lst in ready.items():
                cand = None; cand_key = None
                for i in lst:
                    if i >= lim:
                        continue
                    st = max(efree[e], rtime[i])
                    if e == "act" and ops[i].tset and ops[i].tset != act_set[0]:
                        st = max(efree[e] + TSW, rtime[i])
                    pr = i if PRI == "idx" else -blev[i]
                    key = (st, pr) if st > efree[e] else (efree[e], pr)
                    if cand is None or key < cand_key:
                        cand, cand_key = i, key
                if cand is not None and (best is None or cand_key < best[0]):
                    best = (cand_key, cand, e)
            assert best is not None, "scheduler stuck"
            (st, _pr), i, e = best
            op = ops[i]
            ready[e].remove(i)
            if e == "act" and op.tset:
                if op.tset != act_set[0]:
                    self.n_tsw = getattr(self, "n_tsw", 0) + 1
                act_set[0] = op.tset
            start[i] = st; efree[e] = st + op.cost; finish[i] = st + op.cost + op.lat
            done[i] = True; order.append(i)
            for sidx in succ[i]:
                sop = ops[sidx]
                if sop.eng == op.eng and not op.dma and not sop.dma and op.eng == "pe" and not sop.force:
                    t = start[i] + 0.01
                elif sop.eng == op.eng and not op.dma:
                    t = finish[i] + 0.12
                else:
                    t = finish[i] + 0.55
                if t > rtime[sidx]:
                    rtime[sidx] = t
                indeg[sidx] -= 1
                if indeg[sidx] == 0:
                    ready[sop.eng].append(sidx)
        self.est_us = max(finish)
        if os.environ.get("KDUMP"):
            a, b = [int(x) for x in os.environ["KDUMP"].split(":")]
            for i in range(a, b):
                op = ops[i]
                crit = max(op.deps, key=lambda d: finish[d]) if op.deps else -1
                print(f"   op {i} {op.eng:4s} start {start[i]:8.2f} fin {finish[i]:8.2f} rtime {rtime[i]:8.2f} critdep {crit} ({ops[crit].eng if crit >= 0 else ''} fin {finish[crit] if crit >= 0 else 0:8.2f}) dma={op.dma}")
        if os.environ.get("KSTAT"):
            tot = {e: 0.0 for e in self.engs}
            for op in ops:
                tot[op.eng] += op.cost
            print("model engine busy us:", {k: round(v) for k, v in tot.items()})
            for name, idx in getattr(self, "marks", []):
                if idx < n:
                    print(f"  mark {name:24s} op {idx:6d} start {start[idx]:9.1f}")
        return order

    def emit(self, es):
        nc = self.nc
        ops = self.ops
        need = [[] for _ in ops]
        for op in ops:
            for d in sorted(op.deps):
                dep = ops[d]
                if dep.dma:
                    need[op.idx].append(d)
                elif op.dma or dep.eng != op.eng:
                    dep.signal = True
                    need[op.idx].append(d)
                elif op.eng != "pe" or op.force:
                    dep.signal = True
                    need[op.idx].append(d)
        sems = {}
        for e in ("pe", "act", "dve", "pool", "sp"):
            sems[e] = es.enter_context(nc.semaphore("s_" + e))
        streams = {}
        for op in ops:
            if op.dma and op.stream not in streams:
                streams[op.stream] = es.enter_context(nc.semaphore("d_" + op.stream))
        cnt = {k: 0 for k in sems}
        scnt = {k: 0 for k in streams}
        waited = {e: {} for e in self.engs}
        order = self.schedule()
        for oi in order:
            op = ops[oi]
            wl = {}
            for d in need[op.idx]:
                dep = ops[d]
                if dep.dma:
                    key, val = ("d", dep.stream), 16 * dep.sig_idx
                else:
                    key, val = ("e", dep.eng), dep.sig_idx
                if wl.get(key, 0) < val:
                    wl[key] = val
            eng = self.engs[op.eng]
            wd = waited[op.eng]
            pend = []
            for key, val in wl.items():
                if wd.get(key, 0) >= val:
                    continue
                wd[key] = val
                sem = streams[key[1]] if key[0] == "d" else sems[key[1]]
                pend.append((sem, val))
            attach = None
            if pend and not op.dma and not os.environ.get("KNOATTACH"):
                attach = pend.pop()
            for sem, val in pend:
                eng.wait_ge(sem, val)
            ins = op.fn(eng)
            if attach is not None:
                ins._wait_ge(attach[0], attach[1])
            self.n_wait = getattr(self, "n_wait", 0) + len(pend)
            self.n_att = getattr(self, "n_att", 0) + (1 if attach else 0)
            if op.dma:
                scnt[op.stream] += 1
                op.sig_idx = scnt[op.stream]
                ins.then_inc(streams[op.stream], 16)
            elif op.signal:
                cnt[op.eng] += 1
                op.sig_idx = cnt[op.eng]
                ins.then_inc(sems[op.eng], 1)
            else:
                op.sig_idx = cnt[op.eng] + 1
        return len(ops)


def MM(out, lhsT, rhs, start, stop):
    return _tag(lambda e: e.matmul(out, lhsT, rhs, start=start, stop=stop), out)

def TR(out, in_, ident):
    return _tag(lambda e: e.transpose(out, in_, ident), out)

def _fs(ap):
    n = 1
    for d in ap.shape[1:]:
        n *= int(d)
    return n

def _tag(fn, out):
    fn.free = _fs(out)
    return fn

def ACTF(out, in_, func, bias=None, scale=None, accum=None):
    kw = {}
    if bias is not None: kw["bias"] = bias
    if scale is not None: kw["scale"] = scale
    if accum is not None: kw["accum_out"] = accum
    fn = _tag(lambda e: e.activation(out=out, in_=in_, func=func, **kw), out)
    fn.tset = {AF.Sigmoid: 1, AF.Silu: 2, AF.Exp: 3, AF.Ln: 3}.get(func, 0)
    return fn

def TS(out, in0, s1, s2, op0, op1=None):
    if op1 is None:
        return _tag(lambda e: e.tensor_scalar(out=out, in0=in0, scalar1=s1, scalar2=None, op0=op0), out)
    return _tag(lambda e: e.tensor_scalar(out=out, in0=in0, scalar1=s1, scalar2=s2, op0=op0, op1=op1), out)

def TT(out, in0, in1, op):
    return _tag(lambda e: e.tensor_tensor(out=out, in0=in0, in1=in1, op=op), out)

def STT(out, in0, scalar, in1, op0, op1):
    fn = _tag(lambda e: e.scalar_tensor_tensor(out=out, in0=in0, scalar=scalar, in1=in1, op0=op0, op1=op1), out)
    fn.free *= 1.6
    return fn

def CP(out, in_):
    return _tag(lambda e: e.tensor_copy(out=out, in_=in_), out)

def RCP(out, in_):
    return _tag(lambda e: e.reciprocal(out=out, in_=in_), out)

def MSET(out, v):
    return lambda e: e.memset(out, v)

def CPA(out, in_):
    return _tag(lambda e: e.copy(out=out, in_=in_), out)

def MULA(out, in_, m):
    return _tag(lambda e: e.mul(out=out, in_=in_, mul=m), out)


def _tile_w(w, cols):
    out = np.zeros((128, 8, 512), np.float32)
    sub = w[:, cols]
    out[:, :, :sub.shape[1]] = sub.reshape(8, 128, -1).transpose(1, 0, 2)
    return out


def _build_wall(w_in, w_branch, w_out, w_mem_kv):
    ar = np.arange
    groups = []
    for j in range(8):
        groups.append(_tile_w(w_in, np.concatenate([o + j * 128 + ar(128) for o in (O_H, O_B, O_C, O_GC)])))
    for j in range(8):
        groups.append(_tile_w(w_in, np.concatenate([o + j * 128 + ar(128) for o in (O_R, O_K, O_V, O_GR)])))
    groups.append(_tile_w(w_in, O_WD + ar(128)))
    for h in range(4):
        groups.append(_tile_w(w_in, np.concatenate([O_Q + h * 256 + ar(256), O_GM + h * 256 + ar(256)])))
    for o in (O_MC, O_MR, O_MM):
        for hf in range(2):
            groups.append(_tile_w(w_in, o + hf * 512 + ar(512)))
    for i in range(3):
        for hf in range(2):
            groups.append(_tile_w(w_branch[i], hf * 512 + ar(512)))
    for hf in range(2):
        groups.append(_tile_w(w_out, hf * 512 + ar(512)))
    for q in range(4):
        groups.append(_tile_w(w_mem_kv, q * 512 + ar(512)))
    assert len(groups) == NGROUPS
    return np.ascontiguousarray(np.stack(groups))


def _fm(v):
    return np.ascontiguousarray(v.reshape(-1, 128).T)


def _host_inputs(inp):
    f = lambda a: np.ascontiguousarray(np.asarray(a, dtype=np.float32))
    x_prompt, x_sample, mem_prompt = f(inp["x_prompt"]), f(inp["x_sample"]), f(inp["mem_prompt"])
    ck, cv = f(inp["cache_mem_k"])[0], f(inp["cache_mem_v"])[0]
    sconv, sshift, swkv = f(inp["state_conv"])[0], f(inp["state_shift"])[0], f(inp["state_wkv"])[0]
    wall = _build_wall(f(inp["w_in"])[0], f(inp["w_branch"])[0], f(inp["w_out"])[0], f(inp["w_mem_kv"])[0])
    vecs = np.zeros((128, NV), np.float32)
    cw = f(inp["conv_w"])[0]
    for off, v in ((V_NORM_IN, f(inp["norm_in"])[0]), (V_CW0, cw[0]), (V_CW1, cw[1]), (V_CW2, cw[2]),
                   (V_CB, f(inp["conv_b"])[0]), (V_MU, f(inp["shift_mu"])[0]), (V_W0, f(inp["decay_w0"])[0]),
                   (V_A0, f(inp["icl_a0"])[0]), (V_KK, f(inp["k_k"])[0]), (V_KA, f(inp["k_a"])[0]),
                   (V_RK, f(inp["r_k"])[0].reshape(-1)), (V_LNG, f(inp["ln_x_g"])[0]), (V_LNB, f(inp["ln_x_b"])[0]),
                   (V_NMEM, f(inp["norm_mem"])[0])):
        m = _fm(v)
        vecs[:, off:off + m.shape[1]] = m
    gfin = np.ascontiguousarray(np.broadcast_to(f(inp["norm_final"])[None, :], (128, D)))
    lora = np.ascontiguousarray(np.concatenate([f(inp["decay_up"])[0], f(inp["icl_up"])[0]], axis=0))
    consts = _make_consts()
    maps = []
    for c in range(NCORES):
        xall = np.concatenate([x_prompt[c].reshape(16, 128, D),
                               x_sample[NB * c:NB * (c + 1)].reshape(1, 128, D)], axis=0)
        maps.append({
            "xall": np.ascontiguousarray(xall),
            "mem": np.ascontiguousarray(mem_prompt[c].reshape(2, 128, D)),
            "ck": np.ascontiguousarray(ck[NB * c:NB * (c + 1)].reshape(NB, 256, D)),
            "cv": np.ascontiguousarray(cv[NB * c:NB * (c + 1)].reshape(NB, 256, D)),
            "sconv": np.ascontiguousarray(sconv[NB * c:NB * (c + 1)].reshape(NB * 2, D)),
            "sshift": np.ascontiguousarray(sshift[NB * c:NB * (c + 1)]),
            "swkv": np.ascontiguousarray(swkv[NB * c:NB * (c + 1)]),
            "wall": wall, "vecs": vecs, "gfin": gfin, "lora": lora, "consts": consts,
        })
    return maps


C_IDENT, C_TRIL_S, C_TRIL_I, C_BD_S, C_BD_I, C_BLK, C_RESET, C_LT, C_BD_LT = 0, 128, 256, 384, 512, 640, 768, 896, 1024
NCONST = 1152


def _make_consts():
    c = np.zeros((128, NCONST), np.float32)
    i = np.arange(128)
    c[:, C_IDENT:C_IDENT + 128] = np.eye(128)
    c[:, C_TRIL_S:C_TRIL_S + 128] = (i[:, None] < i[None, :])
    c[:, C_TRIL_I:C_TRIL_I + 128] = (i[:, None] <= i[None, :])
    same = (i[:, None] // NT) == (i[None, :] // NT)
    c[:, C_BD_S:C_BD_S + 128] = (i[:, None] < i[None, :]) & same
    c[:, C_BD_I:C_BD_I + 128] = (i[:, None] <= i[None, :]) & same
    c[:, C_BLK:C_BLK + 128] = (i[:, None] // 64) == (i[None, :] // 64)
    c[:, C_RESET:C_RESET + 16] = (i[:, None] // NT) == np.arange(16)[None, :]
    c[:, C_LT:C_LT + 128] = (i[None, :] < i[:, None])
    c[:, C_BD_LT:C_BD_LT + 128] = (i[None, :] < i[:, None]) & same
    return c


NTP = 1152
PASS_TILES = [list(range(8)) + [16], list(range(8, 16))]
PASS_TBLK = [[(0, 512), (512, 512), (1024, 128)], [(0, 512), (512, 512)]]


def build_nc(stages=("rwkv", "conv", "mem")):
    nc = bass.Bass("TRN2", target_bir_lowering=False)
    din = lambda name, shape: nc.dram_tensor(name, list(shape), F32, kind="ExternalInput").ap()
    dout = lambda name, shape: nc.dram_tensor(name, list(shape), F32, kind="ExternalOutput").ap()
    xall_d = din("xall", (17, 128, D)); mem_d = din("mem", (2, 128, D))
    ck_d = din("ck", (NB, 256, D)); cv_d = din("cv", (NB, 256, D))
    sconv_d = din("sconv", (NB * 2, D)); sshift_d = din("sshift", (NB, 3200)); swkv_d = din("swkv", (NB, 16, 64, 64))
    wall_d = din("wall", (NGROUPS, 128, 8, 512)); vecs_d = din("vecs", (128, NV)); gfin_d = din("gfin", (128, D))
    lora_d = din("lora", (128, 1024)); consts_d = din("consts", (128, NCONST))
    y_d = dout("y", (17, 128, D)); mk_d = dout("mk", (256, D)); mv_d = dout("mv", (256, D))
    convp_d = dout("convp", (2, D)); shiftp_d = dout("shiftp", (1, 3200)); wkvp_d = dout("wkvp", (16, 64, 64))
    convs_d = dout("convs", (NB * 2, D)); shifts_d = dout("shifts", (NB, 3200)); wkvs_d = dout("wkvs", (NB, 16, 64, 64))

    with ExitStack() as es:
        sb = lambda name, shape, dt=F32: es.enter_context(nc.sbuf_tensor("s_" + name, list(shape), dt))
        P = Prog(nc)
        vecs = sb("vecs", (128, NV)); consts = sb("consts", (128, NCONST)); gfin = sb("gfin", (128, D))
        identb = sb("identb", (128, 128), BF16); onesb = sb("onesb", (128, 128), BF16)
        xt = sb("xt", (128, D)); xnb = sb("xnb", (128, D), BF16)
        xt2 = sb("xt2", (128, D)); xnb2 = sb("xnb2", (128, D), BF16)
        ssb = sb("ssb", (128, 16))
        xnT = sb("xnT", (128, 8, NTP), BF16); yT = sb("yT", (128, 8, NTP), BF16)
        acc = sb("acc", (128, 8, NTP))
        NS = 4
        ring = [sb(f"ring{s}", (128, 8, 512), BF16) for s in range(NS)]
        NTMP = 12
        tmps = [sb(f"tmp{i}", (128, 512)) for i in range(NTMP)]
        memnT = sb("memnT", (128, 8, 256), BF16); KT = sb("KT", (128, 8, 256), BF16); Vp = sb("Vp", (128, 2, D), BF16)
        U = sb("U", (128, 1026)); us = sb("us", (128, NB, 10)); ucarry = sb("ucarry", (128, 8, 2)); uls = sb("uls", (128, 8, NB, 2))
        scT = sb("scT", (128, 8, NB * 2))
        qTs = sb("qTs", (128, 8, 128), BF16); sgms = sb("sgms", (128, 8, 128))
        Kb = sb("Kb", (128, 2, D), BF16); Vb = sb("Vb", (128, 2, D), BF16); KTb = sb("KTb", (128, 8, 256), BF16)
        ps = [es.enter_context(nc.psum_tensor(f"ps{i}", [128, 512], F32)) for i in range(8)]
        psk = [f"ps{i}" for i in range(8)]
        ident = consts[:, C_IDENT:C_IDENT + 128]
        psT = ps[7][:, :].bitcast(BF16)

        tctr = [0]
        tmp_excl = set()
        def tmp():
            i = tctr[0] % NTMP; tctr[0] += 1
            while i in tmp_excl:
                i = tctr[0] % NTMP; tctr[0] += 1
            return tmps[i], f"tmp{i}"

        vcol = lambda off, j: vecs[:, off + j:off + j + 1]

        P.dma("sp", vecs[:], vecs_d[:, :], w=["vecs"], stream="c_vecs")
        P.dma("sp", consts[:], consts_d[:, :], w=["consts"], stream="c_consts")
        P.dma("sp", gfin[:], gfin_d[:, :], w=["gfin"], stream="c_gfin")
        P.dve(CP(identb[:], ident), r=["consts"], w=["identb"])
        P.dve(MSET(onesb[:], 1.0), w=["onesb"])
        P.dve(MSET(ucarry[:], 0.0), w=["ucarry"])

        sched = [G_KV + q for q in range(4)]
        for p_ in range(2):
            if "rwkv" in stages:
                sched += [G_LORA] + [G_RWKV + j for j in range(8)] + [G_GATE + 2, G_BR + 2, G_GATE + 3, G_BR + 3]
            if "conv" in stages:
                sched += [G_CONV + j for j in range(8)] + [G_GATE + 0, G_BR + 0, G_GATE + 1, G_BR + 1]
            if "mem" in stages:
                sched += [G_MEM + h for h in range(4)] + [G_GATE + 4, G_BR + 4, G_GATE + 5, G_BR + 5]
            if 'noout' not in stages:
                sched += [G_OUT, G_OUT + 1]
        wst = {"i": 0, "loaded": 0}
        def w_prefetch(upto):
            while wst["loaded"] < min(upto, len(sched)):
                k = wst["loaded"]; s = k % NS
                P.dma("pool", ring[s][:], wall_d[sched[k]], w=[f"ring{s}"], stream=f"ring{s}", lat=14.0)
                wst["loaded"] += 1
        def w_acquire(g):
            if not hasattr(P, "marks"):
                P.marks = []
            P.marks.append((f"grp{g}", len(P.ops)))
            i = wst["i"]
            if os.environ.get('RCUT'):
                sched[i] = g
            assert sched[i] == g, (i, sched[i], g)
            w_prefetch(i + NS - 1)
            wst["i"] += 1
            return i % NS

        def mm8(bank, bkey, slot, c0, rhs_of_kc, rkeys):
            for kc in range(8):
                P.add("pe", MM(bank, ring[slot][:, kc, c0:c0 + 128], rhs_of_kc(kc), kc == 0, kc == 7),
                      r=[f"ring{slot}"] + rkeys, w=[bkey], cost=0.06 + 0.00041 * int(bank.shape[-1]))

        nctr = [0]
        def norm_T(src, voff, dst3, dkeys):
            q = nctr[0] % 2; nctr[0] += 1
            xt_, kx = (xt, "xt") if q == 0 else (xt2, "xt2")
            xn_, kn = (xnb, "xnb") if q == 0 else (xnb2, "xnb2")
            pT_ = ps[7 - q][:, :].bitcast(BF16); kp = psk[7 - q]
            o = 8 * q
            P.dma("sp", xt_[:], src, w=[kx], stream=kx)
            P.act(ACTF(xn_[:], xt_[:], AF.Square, accum=ssb[:, o:o + 1]), r=[kx], w=[kn, f"ss{o}"])
            P.act(ACTF(ssb[:, o + 1:o + 2], ssb[:, o:o + 1], AF.Ln, bias=EPS, scale=1.0 / D), r=[f"ss{o}"], w=[f"ss{o + 1}"])
            P.act(ACTF(ssb[:, o + 2:o + 3], ssb[:, o + 1:o + 2], AF.Exp, scale=-0.5), r=[f"ss{o + 1}"], w=[f"ss{o + 2}"])
            P.dve(TS(xn_[:], xt_[:], ssb[:, o + 2:o + 3], None, ALU.mult), r=[kx, f"ss{o + 2}"], w=[kn])
            for c in range(8):
                P.pe(TR(pT_[:, c * 128:(c + 1) * 128], xn_[:, c * 128:(c + 1) * 128], identb[:]),
                     r=[kn, "identb"], w=[kp])
            for c in range(8):
                if c % 2 == 0:
                    P.dve(TS(dst3[:, c, :], pT_[:, c * 128:(c + 1) * 128], vcol(voff, c), None, ALU.mult),
                          r=[kp, "vecs"], w=dkeys)
                else:
                    P.act(MULA(dst3[:, c, :], pT_[:, c * 128:(c + 1) * 128], vcol(voff, c)), r=[kp, "vecs"], w=dkeys)

        LVL = int(os.environ.get('KCUT', '99'))
        for i in range(2 if LVL >= 1 else 0):
            norm_T(mem_d[i], V_NMEM, memnT[:, :, i * 128:(i + 1) * 128], ["memnT"])

        t0, k0 = tmp()
        P.dma("sp", t0[0:32, :], sconv_d[:, 0:512], w=[k0], stream="c_sc0")
        t1, k1 = tmp()
        P.dma("sp", t1[0:32, :], sconv_d[:, 512:1024], w=[k1], stream="c_sc1")
        for j in range(8 if LVL >= 2 else 0):
            src_t = t0 if j < 4 else t1
            P.pe(TR(ps[6][:, j * 32:(j + 1) * 32], src_t[0:32, (j % 4) * 128:(j % 4 + 1) * 128], ident[0:32, 0:32]),
                 r=[k0, k1, "consts"], w=[psk[6]])
        if LVL >= 2:
            P.act(CPA(scT[:].rearrange("p j s -> p (j s)"), ps[6][:, 0:256]), r=[psk[6]], w=["scT"])

        for q in range(4 if LVL >= 3 else 0):
            s = w_acquire(G_KV + q)
            KSUB = int(os.environ.get('KSUB', '9'))
            if KSUB < 1:
                P.pe(MM(ps[0][:, 0:128], ring[s][:, 0, 0:128], memnT[:, 0, 0:128], True, True), r=[f'ring{s}', 'memnT'], w=[psk[0]])
                continue
            if q < 2:
                for ct in range(4):
                    bank, bk = ps[ct % 2], psk[ct % 2]
                    mm8(bank[:, 0:256], bk, s, ct * 128, lambda kc: memnT[:, kc, :], ["memnT"])
                    P.act(CPA(KT[:, q * 4 + ct, :], bank[:, 0:256]), r=[bk], w=["KT"])
            for mt in range(2 if KSUB >= 2 else 0):
                bank, bk = ps[2 + mt], psk[2 + mt]
                for kc in range(8):
                    P.pe(MM(bank[:, :], memnT[:, kc, mt * 128:(mt + 1) * 128], ring[s][:, kc, :], kc == 0, kc == 7),
                         r=[f"ring{s}", "memnT"], w=[bk])
                t, tk = tmp()
                P.dve(CP(t[:], bank[:, :]), r=[bk], w=[tk])
                if q >= 2:
                    P.act(CPA(Vp[:, mt, (q - 2) * 512:(q - 1) * 512], bank[:, :]), r=[bk], w=["Vp"])
                dst = (mk_d if q < 2 else mv_d)[mt * 128:(mt + 1) * 128, (q % 2) * 512:(q % 2 + 1) * 512]
                P.dma("sp", dst, t[:], r=[tk], w=[f"o_mkv{q}{mt}"], stream="o_" + tk)

        def branch(i, first, tblk):
            for hf in range(2):
                sg = w_acquire(G_GATE + 2 * i + hf)
                sw = w_acquire(G_BR + 2 * i + hf)
                cnt = 0
                for ct in range(4):
                    j = hf * 4 + ct
                    for tb, (c0, n) in enumerate(tblk):
                        par = cnt % 4; cnt += 1
                        bA, kA = ps[par * 2], psk[par * 2]
                        bB, kB = ps[par * 2 + 1], psk[par * 2 + 1]
                        mm8(bA[:, 0:n], kA, sg, ct * 128, lambda kc: xnT[:, kc, c0:c0 + n], [f"xnT{tb}"])
                        mm8(bB[:, 0:n], kB, sw, ct * 128, lambda kc: yT[:, kc, c0:c0 + n], [f"yT{tb}"])
                        t, tk = tmp()
                        P.act(ACTF(t[:, 0:n], bA[:, 0:n], AF.Sigmoid), r=[kA], w=[tk])
                        ak = f"acc{j}_{tb}"
                        if first:
                            P.dve(TT(acc[:, j, c0:c0 + n], bB[:, 0:n], t[:, 0:n], ALU.mult), r=[kB, tk], w=[ak])
                        else:
                            t2, tk2 = tmp()
                            P.dve(TT(t2[:, 0:n], bB[:, 0:n], t[:, 0:n], ALU.mult), r=[kB, tk], w=[tk2])
                            P.pool(TT(acc[:, j, c0:c0 + n], acc[:, j, c0:c0 + n], t2[:, 0:n], ALU.add), r=[ak, tk2], w=[ak])

        def conv_phase(p_, tblk):
            for j in range(8):
                s = w_acquire(G_CONV + j)
                P.pool(CP(U[:, 0:2], ucarry[:, j, :]), r=["ucarry"], w=["U"])
                if p_ == 0:
                    P.pool(CP(us[:, :, 0:2], scT[:, j, :].rearrange("p (b s) -> p b s", s=2)), r=["scT"], w=["us"])
                for tb, (c0, n) in enumerate(tblk):
                    o = (tb % 2) * 4
                    bh, bB, bC, bg = ps[o:o + 4]; kh, kB, kC, kg = psk[o:o + 4]
                    for ct, (bank, bk) in enumerate(((bh, kh), (bB, kB), (bC, kC), (bg, kg))):
                        mm8(bank[:, 0:n], bk, s, ct * 128, lambda kc: xnT[:, kc, c0:c0 + n], [f"xnT{tb}"])
                    th, tkh = tmp()
                    P.act(CPA(th[:, 0:n], bh[:, 0:n]), r=[kh], w=[tkh])
                    if n == 512:
                        ucur, um1, um2 = U[:, 2 + c0:2 + c0 + n], U[:, 1 + c0:1 + c0 + n], U[:, c0:c0 + n]
                        v3 = lambda a: a
                        ukey = "U"
                    else:
                        ucur, um1, um2 = us[:, :, 2:10], us[:, :, 1:9], us[:, :, 0:8]
                        v3 = lambda a: a.rearrange("p (b t) -> p b t", t=NT)
                        ukey = "us"
                    P.dve(TT(ucur, v3(bC[:, 0:n]), v3(th[:, 0:n]), ALU.mult), r=[kC, tkh], w=[ukey])
                    a1, ka1 = tmp()
                    P.dve(TS(v3(a1[:, 0:n]), ucur, vcol(V_CW2, j), vcol(V_CB, j), ALU.mult, ALU.add), r=[ukey, "vecs"], w=[ka1])
                    a2, ka2 = tmp()
                    P.dve(STT(v3(a2[:, 0:n]), um1, vcol(V_CW1, j), v3(a1[:, 0:n]), ALU.mult, ALU.add), r=[ukey, ka1, "vecs"], w=[ka2])
                    a3, ka3 = tmp()
                    P.dve(STT(v3(a3[:, 0:n]), um2, vcol(V_CW0, j), v3(a2[:, 0:n]), ALU.mult, ALU.add), r=[ukey, ka2, "vecs"], w=[ka3])
                    sg, ksg = tmp()
                    P.act(ACTF(sg[:, 0:n], bg[:, 0:n], AF.Silu), r=[kg], w=[ksg])
                    a4, ka4 = tmp()
                    P.dve(TT(a4[:, 0:n], bB[:, 0:n], a3[:, 0:n], ALU.mult), r=[kB, ka3], w=[ka4])
                    P.pool(TT(yT[:, j, c0:c0 + n], a4[:, 0:n], sg[:, 0:n], ALU.mult), r=[ka4, ksg], w=[f"yT{tb}"])
                P.pool(CP(ucarry[:, j, :], U[:, 1024:1026]), r=["U"], w=["ucarry"])
                if p_ == 0:
                    P.pool(CP(uls[:, j, :, :], us[:, :, 8:10]), r=["us"], w=["uls"])
            for hf in range(2):
                for jj in range(4):
                    j = hf * 4 + jj
                    if p_ == 1:
                        P.pe(TR(ps[0][0:2, jj * 128:(jj + 1) * 128], ucarry[:, j, :], ident), r=["ucarry", "consts"], w=[psk[0]])
                    else:
                        P.pe(TR(ps[1][0:32, jj * 128:(jj + 1) * 128], uls[:, j, :, :].rearrange("p b s -> p (b s)"), ident),
                             r=["uls", "consts"], w=[psk[1]])
                t, tk = tmp()
                if p_ == 1:
                    P.dve(CP(t[0:2, :], ps[0][0:2, :]), r=[psk[0]], w=[tk])
                    P.dma("sp", convp_d[:, hf * 512:(hf + 1) * 512], t[0:2, :], r=[tk], w=[f"o_convp{hf}"], stream="o_" + tk)
                else:
                    P.dve(CP(t[0:32, :], ps[1][0:32, :]), r=[psk[1]], w=[tk])
                    P.dma("sp", convs_d[:, hf * 512:(hf + 1) * 512], t[0:32, :], r=[tk], w=[f"o_convs{hf}"], stream="o_" + tk)

        def mem_phase(p_, tblk):
            for h in range(4):
                s = w_acquire(G_MEM + h)
                for tb, (c0, n) in enumerate(tblk):
                    bq = [ps[0], ps[1]]; bg = [ps[2], ps[3]]
                    for ct in range(4):
                        mm8(ps[ct][:, 0:n], psk[ct], s, ct * 128, lambda kc: xnT[:, kc, c0:c0 + n], [f"xnT{tb}"])
                    if n == 128:
                        for dc in range(2):
                            P.act(CPA(qTs[:, 2 * h + dc, :], bq[dc][:, 0:128]), r=[psk[dc]], w=["qTs"])
                            P.act(ACTF(sgms[:, 2 * h + dc, :], bg[dc][:, 0:128], AF.Silu), r=[psk[2 + dc]], w=["sgms"])
                        continue
                    qt, kq = tmp()
                    qT = qt[:, :].bitcast(BF16)
                    sg0, ks0 = tmp(); sg1, ks1 = tmp()
                    for dc in range(2):
                        P.act(CPA(qT[:, dc * 512:(dc + 1) * 512], bq[dc][:, :]), r=[psk[dc]], w=[kq])
                    P.act(ACTF(sg0[:, :], bg[0][:, :], AF.Silu), r=[psk[2]], w=[ks0])
                    P.act(ACTF(sg1[:, :], bg[1][:, :], AF.Silu), r=[psk[3]], w=[ks1])
                    et, ke = tmp()
                    E = et[:, :].bitcast(BF16)
                    for mt in range(2):
                        bS, kS = ps[4 + mt], psk[4 + mt]
                        for dc in range(2):
                            P.pe(MM(bS[:, :], KT[:, 2 * h + dc, mt * 128:(mt + 1) * 128], qT[:, dc * 512:(dc + 1) * 512],
                                    dc == 0, dc == 1), r=["KT", kq], w=[kS])
                        P.act(ACTF(E[:, mt * 512:(mt + 1) * 512], bS[:, :], AF.Exp, scale=1.0 / 16.0), r=[kS], w=[ke])
                    for mt in range(2):
                        P.pe(MM(ps[6][:, :], onesb[:], E[:, mt * 512:(mt + 1) * 512], mt == 0, mt == 1),
                             r=["onesb", ke], w=[psk[6]])
                    rd, krd = tmp()
                    P.dve(RCP(rd[:, :], ps[6][:, :]), r=[psk[6]], w=[krd])
                    for dvc in range(2):
                        for mt in range(2):
                            P.pe(MM(ps[7][:, :], Vp[:, mt, h * 256 + dvc * 128:h * 256 + (dvc + 1) * 128],
                                    E[:, mt * 512:(mt + 1) * 512], mt == 0, mt == 1), r=["Vp", ke], w=[psk[7]])
                        o1, ko1 = tmp()
                        P.dve(TT(o1[:, :], ps[7][:, :], rd[:, :], ALU.mult), r=[psk[7], krd], w=[ko1])
                        sgd, ksd = (sg0, ks0) if dvc == 0 else (sg1, ks1)
                        P.pool(TT(yT[:, 2 * h + dvc, c0:c0 + n], o1[:, :], sgd[:, :], ALU.mult), r=[ko1, ksd], w=[f"yT{tb}"])
            if p_ != 0:
                return
            tmp_excl.update({4, 5, 6, 7})
            Kset = [[(Kb[:, 0, :], "Kb0"), (Kb[:, 1, :], "Kb1")],
                    [(tmps[4][:, :].bitcast(BF16), "tmp4"), (tmps[5][:, :].bitcast(BF16), "tmp5")]]
            Vset = [[(Vb[:, 0, :], "Vb0"), (Vb[:, 1, :], "Vb1")],
                    [(tmps[6][:, :].bitcast(BF16), "tmp6"), (tmps[7][:, :].bitcast(BF16), "tmp7")]]
            for b in range(NB):
                Kc, Vc = Kset[b % 2], Vset[b % 2]
                for mt in range(2):
                    P.dma("pool", Kc[mt][0], ck_d[b, mt * 128:(mt + 1) * 128, :], w=[Kc[mt][1]], stream="k_" + Kc[mt][1], lat=8.0)
                    P.dma("pool", Vc[mt][0], cv_d[b, mt * 128:(mt + 1) * 128, :], w=[Vc[mt][1]], stream="k_" + Vc[mt][1], lat=8.0)
                for mt in range(2):
                    for dch in range(8):
                        P.pe(TR(psT[:, dch * 128:(dch + 1) * 128], Kc[mt][0][:, dch * 128:(dch + 1) * 128], identb[:]),
                             r=[Kc[mt][1], "identb"], w=[psk[7]])
                    P.act(CPA(KTb[:, :, mt * 128:(mt + 1) * 128], psT.rearrange("p (c t) -> p c t", c=8)), r=[psk[7]], w=["KTb"])
                bS, kS = ps[4], psk[4]
                for h in range(4):
                    for mt in range(2):
                        for dc in range(2):
                            P.pe(MM(bS[:, (h * 2 + mt) * 8:(h * 2 + mt + 1) * 8], KTb[:, 2 * h + dc, mt * 128:(mt + 1) * 128],
                                    qTs[:, 2 * h + dc, b * 8:(b + 1) * 8], dc == 0, dc == 1), r=["KTb", "qTs"], w=[kS])
                et, ke = tmp()
                E = et[:, 0:32].bitcast(BF16)
                P.act(ACTF(E, bS[:, 0:64], AF.Exp, scale=1.0 / 16.0), r=[kS], w=[ke])
                E4 = E.rearrange("p (h m t) -> p h m t", h=4, m=2)
                for mt in range(2):
                    P.pe(MM(ps[5][:, 0:32], onesb[:], E4[:, :, mt, :], mt == 0, mt == 1), r=["onesb", ke], w=[psk[5]])
                rd, krd = tmp()
                P.dve(RCP(rd[:, 0:32], ps[5][:, 0:32]), r=[psk[5]], w=[krd])
                for h in range(4):
                    for dvc in range(2):
                        for mt in range(2):
                            P.pe(MM(ps[6][:, (h * 2 + dvc) * 8:(h * 2 + dvc + 1) * 8],
                                    Vc[mt][0][:, h * 256 + dvc * 128:h * 256 + (dvc + 1) * 128],
                                    E[:, (h * 2 + mt) * 8:(h * 2 + mt + 1) * 8], mt == 0, mt == 1), r=[Vc[mt][1], ke], w=[psk[6]])
                o1, ko1 = tmp()
                rd4 = rd[:, 0:32].rearrange("p (h t) -> p h t", h=4).unsqueeze(2).to_broadcast([128, 4, 2, 8])
                P.dve(TT(o1[:, 0:64].rearrange("p (h d t) -> p h d t", h=4, d=2),
                         ps[6][:, 0:64].rearrange("p (h d t) -> p h d t", h=4, d=2), rd4, ALU.mult), r=[psk[6], krd], w=[ko1])
                P.pool(TT(yT[:, :, 1024 + b * 8:1024 + (b + 1) * 8], o1[:, 0:64].rearrange("p (c t) -> p c t", c=8),
                          sgms[:, :, b * 8:(b + 1) * 8], ALU.mult), r=[ko1, "sgms"], w=["yT2"])
            tmp_excl.clear()

        def out_phase(p_, tblk):
            s0 = w_acquire(G_OUT); s1 = w_acquire(G_OUT + 1)
            for j in range(8):
                for tb, (c0, n) in enumerate(tblk):
                    eng = P.pool if (j + tb) % 2 else P.dve
                    eng(CP(yT[:, j, c0:c0 + n], acc[:, j, c0:c0 + n]), r=[f"acc{j}_{tb}"], w=[f"yT{tb}"])
            for li, gi in enumerate(PASS_TILES[p_]):
                tb = li // 4
                q = li % 2
                xt_, kx = (xt, "xt") if q == 0 else (xt2, "xt2")
                xn_, kn = (xnb, "xnb") if q == 0 else (xnb2, "xnb2")
                o = 8 * q
                P.dma("sp", xt_[:], xall_d[gi], w=[kx], stream=kx)
                hh = (tmp(), tmp())
                for hf, s in enumerate((s0, s1)):
                    bank, bk = ps[(li % 2) * 2 + hf], psk[(li % 2) * 2 + hf]
                    for kc in range(8):
                        P.pe(MM(bank[:, :], yT[:, kc, li * 128:(li + 1) * 128], ring[s][:, kc, :], kc == 0, kc == 7),
                             r=[f"yT{tb}", f"ring{s}"], w=[bk])
                    P.dve(TT(hh[hf][0][:, :], bank[:, :], xt_[:, hf * 512:(hf + 1) * 512], ALU.add), r=[bk, kx], w=[hh[hf][1]])
                    P.act(ACTF(xn_[:, hf * 512:(hf + 1) * 512], hh[hf][0][:, :], AF.Square, accum=ssb[:, o + 4 + hf:o + 5 + hf]),
                          r=[hh[hf][1]], w=[kn, f"ss{o + 4 + hf}"])
                P.dve(TT(ssb[:, o + 6:o + 7], ssb[:, o + 4:o + 5], ssb[:, o + 5:o + 6], ALU.add), r=[f"ss{o + 4}", f"ss{o + 5}"], w=[f"ss{o + 6}"])
                P.act(ACTF(ssb[:, o + 7:o + 8], ssb[:, o + 6:o + 7], AF.Ln, bias=EPS, scale=1.0 / D), r=[f"ss{o + 6}"], w=[f"ss{o + 7}"])
                P.act(ACTF(ssb[:, o + 3:o + 4], ssb[:, o + 7:o + 8], AF.Exp, scale=-0.5), r=[f"ss{o + 7}"], w=[f"ss{o + 3}"])
                for hf in range(2):
                    P.dve(STT(hh[hf][0][:, :], hh[hf][0][:, :], ssb[:, o + 3:o + 4], gfin[:, hf * 512:(hf + 1) * 512], ALU.mult, ALU.mult),
                          r=[hh[hf][1], f"ss{o + 3}", "gfin"], w=[hh[hf][1]])
                    P.dma("sp", y_d[gi][:, hf * 512:(hf + 1) * 512], hh[hf][0][:, :], r=[hh[hf][1]], w=[f"o_y{gi}_{hf}"],
                          stream="o_" + hh[hf][1])

        C0 = float(np.exp(-0.5))
        accf = acc[:].rearrange("p j t -> p (j t)")
        roff = [0]
        def ralloc(n32):
            a = accf[:, roff[0]:roff[0] + n32]; roff[0] += n32
            assert roff[0] <= 8 * NTP
            return a
        shb = [ralloc(514).rearrange("p (a t) -> p a t", a=2) for _ in range(4)]
        shs = [ralloc(144).rearrange("p (b t) -> p b t", t=9) for _ in range(4)]
        wdad = ralloc(576).bitcast(BF16)
        AR = ralloc(512).bitcast(BF16).rearrange("p (a t) -> p a t", a=2)
        BTt = ralloc(256).bitcast(BF16); KTt = ralloc(256).bitcast(BF16); VTt = ralloc(256).bitcast(BF16)
        Ms = ralloc(1024).rearrange("p (b v) -> p b v", v=64)
        Msb = ralloc(512).bitcast(BF16).rearrange("p (b v) -> p b v", v=64)
        Sin = ralloc(2048).rearrange("p (b c) -> p b c", c=128)
        UTs = ralloc(128)
        RKEYS = ["shb0_0", "shb1_0", "shb2_0", "shb3_0", "shb0_1", "shb1_1", "shb2_1", "shb3_1", "AR0", "AR1", "BTt0", "BTt1", "KTt0", "KTt1", "VTt0", "VTt1", "pc0", "pc1", "shs0", "shs1", "shs2", "shs3", "wdad", "AR", "BTt", "KTt", "VTt",
                 "Ms", "Msb", "Sin", "UTs"]
        loraW = sb("loraW", (128, D), BF16); blkb = sb("blkb", (128, 128), BF16)
        Mst = sb("Mst", (128, 8, 64)); M0b = sb("M0b", (128, 64), BF16)
        pcarry = sb("pcarry", (128, 25)); pls = sb("pls", (128, 25, NB)); shT = sb("shT", (128, 25, NB))
        pc = sb("pc", (128, 16)); maskP = sb("maskP", (128, 512)); maskS = sb("maskS", (128, 128))
        dummy = sb("dummy", (128, 4))
        big16 = sb("big16", (128, 2688 + 1024), BF16); Utok = sb("Utok", (128, 128), BF16)
        Mtmp = sb("Mtmp", (128, 64))
        Ubm = Kb[:].rearrange("p a n -> p (a n)").rearrange("p (b c) -> p b c", c=128)
        Mmask = consts[:, C_RESET:C_RESET + 16]

        hv = sb("hv", (128, 32))

        def rwkv_setup():
            for hf_ in range(2):
                P.dve(CP(m512P[:, 256 * hf_:256 * hf_ + 256], consts[:, C_TRIL_S:C_TRIL_S + 256]), r=["consts"], w=["m512"])
                P.dve(CP(m512S[:, 256 * hf_:256 * hf_ + 256], consts[:, C_BD_S:C_BD_S + 256]), r=["consts"], w=["m512"])
            for q_, off_ in enumerate((V_W0, V_A0, V_LNG, V_LNB)):
                P.dve(TS(hv[:, 8 * q_:8 * q_ + 8], vecs[:, off_:off_ + 8], 0.5, None, ALU.mult), r=["vecs"], w=["hv"])
            P.dma("pool", loraW[:], lora_d[:, :], w=["loraW"], stream="c_lora")
            P.dve(CP(blkb[:], consts[:, C_BLK:C_BLK + 128]), r=["consts"], w=["blkb"])
            P.dve(MSET(Mst[:], 0.0), w=["Mst"])
            P.dve(MSET(pcarry[:], 0.0), w=["pcarry"])
            P.dve(MSET(maskP[:], 1.0), w=["maskP"])
            P.dve(MSET(maskP[:].rearrange("p (c t) -> p c t", t=128)[:, :, 0:1], 0.0), w=["maskP"])
            P.dve(MSET(maskS[:], 1.0), w=["maskS"])
            P.dve(MSET(maskS[:].rearrange("p (c t) -> p c t", t=NT)[:, :, 0:1], 0.0), w=["maskS"])
            for half, (a, b_) in enumerate(((0, 13), (13, 25))):
                t, tk = tmp(); t2, tk2 = tmp(); t3, tk3 = tmp(); t4, tk4 = tmp()
                tl = [(t, tk), (t2, tk2), (t3, tk3), (t4, tk4)]
                for q in range(4):
                    lo = a * 128 + q * 512; hi = min(lo + 512, b_ * 128)
                    if hi > lo:
                        P.dma("sp", tl[q][0][0:NB, 0:hi - lo], sshift_d[:, lo:hi], w=[tl[q][1]], stream="l_" + tl[q][1])
                for ti in range(a, b_):
                    q, o = divmod((ti - a) * 128, 512)
                    P.pe(TR(ps[5][:, (ti - a) * 16:(ti - a + 1) * 16], tl[q][0][0:NB, o:o + 128], ident[0:NB, 0:NB]),
                         r=[tl[q][1], "consts"], w=[psk[5]])
                P.act(CPA(shT[:, a:b_, :].rearrange("p c b -> p (c b)"), ps[5][:, 0:(b_ - a) * 16]), r=[psk[5]], w=["shT"])

        def shift_lerp(bank, bkey, n, bi, tile_idx, out, okey, first, last, par):
            mu = vcol(V_MU, tile_idx)
            d, dk = tmps[11][:, par * 256:(par + 1) * 256], f"tmp11_{par}"
            if n == 256:
                b_, k_ = shb[bi][:, par, :], f"shb{bi}_{par}"
                o_, ok_ = shb[bi][:, 1 - par, :], f"shb{bi}_{1 - par}"
                if first:
                    P.pool(CP(b_[:, 0:1], pcarry[:, tile_idx:tile_idx + 1]), r=["pcarry"], w=[k_])
                else:
                    P.pool(CP(b_[:, 0:1], o_[:, 256:257]), r=[ok_], w=[k_])
                P.act(CPA(b_[:, 1:257], bank[:, 0:256]), r=[bkey], w=[k_])
                P.dve(TT(d[:, :], b_[:, 0:256], b_[:, 1:257], ALU.subtract), r=[k_], w=[dk])
                P.dve(STT(out[:, 0:256], d[:, :], mu, b_[:, 1:257], ALU.mult, ALU.add), r=[dk, k_, "vecs"], w=[okey])
                if last:
                    P.pool(CP(pcarry[:, tile_idx:tile_idx + 1], b_[:, 256:257]), r=[k_], w=["pcarry"])
            else:
                b_, k_ = shs[bi], f"shs{bi}"
                P.pool(CP(b_[:, :, 0:1], shT[:, tile_idx, :].unsqueeze(2)), r=["shT"], w=[k_])
                P.act(CPA(b_[:, :, 1:9], bank[:, 0:128].rearrange("p (b t) -> p b t", t=NT)), r=[bkey], w=[k_])
                d3 = d[:, 0:128].rearrange("p (b t) -> p b t", t=NT)
                P.dve(TT(d3, b_[:, :, 0:8], b_[:, :, 1:9], ALU.subtract), r=[k_], w=[dk])
                P.dve(STT(out[:, 0:128].rearrange("p (b t) -> p b t", t=NT), d3, mu, b_[:, :, 1:9], ALU.mult, ALU.add),
                      r=[dk, k_, "vecs"], w=[okey])
                P.pool(CP(pls[:, tile_idx, :].unsqueeze(2), b_[:, :, 8:9]), r=[k_], w=["pls"])

        TM = lambda i: (tmps[i], f"tmp{i}")
        TMp = lambda i, par: (tmps[i][:, par * 256:(par + 1) * 256], f"tmp{i}_{par}")
        TKEYS = [f"tmp{i}" for i in range(NTMP)] + [f"tmp{i}_{p}" for i in range(NTMP) for p in range(2)]
        pc2 = sb("pc2", (128, 4))
        psT0 = ps[0][:, :].bitcast(BF16)
        Utok2 = sb("Utok2", (128, 128), BF16)
        Vbm = Vb[:].rearrange("p a n -> p (a n)").rearrange("p (b c) -> p b c", c=128)
        Mtmp2 = sgms[:, 0:4, :].rearrange("p a (c v) -> p (a c) v", v=64)

        def rblocks(tblk):
            out = []
            for (c0, n) in tblk:
                if n == 512:
                    out += [(c0, 256), (c0 + 256, 256)]
                else:
                    out.append((c0, n))
            return out

        def lora_part(p_, tblk):
            s = w_acquire(G_LORA)
            rb = rblocks(tblk)
            npb = sum(1 for (_, n) in rb if n == 256)
            for bi_, (c0, n) in enumerate(rb):
                par = bi_ % 2; tb = c0 // 512
                bank, bk = ps[4 * par], psk[4 * par]
                mm8(bank[:, 0:n], bk, s, 0, lambda kc: xnT[:, kc, c0:c0 + n], [f"xnT{tb}"])
                o, ok = TMp(0, par)
                shift_lerp(bank, bk, n, 3, 24, o, ok, bi_ == 0, n == 256 and bi_ == npb - 1, par)
                P.act(ACTF(wdad[0:64, c0:c0 + n], o[0:64, 0:n], AF.Tanh), r=[ok], w=["wdad"])
                P.act(CPA(wdad[64:128, c0:c0 + n], o[64:128, 0:n]), r=[ok], w=["wdad"])

        def prep(j, tb, c0, n, s, first, last, par):
            P.marks.append((f"prep{j}_{tb}", len(P.ops)))
            B = 4 * par
            ARp = AR[:, :, par * 256:(par + 1) * 256]; kAR = f"AR{par}"
            BTp = BTt[:, par * 256:(par + 1) * 256]; kBT = f"BTt{par}"
            KTp = KTt[:, par * 256:(par + 1) * 256]; kKT = f"KTt{par}"
            VTp = VTt[:, par * 256:(par + 1) * 256]; kVT = f"VTt{par}"
            N = slice(0, n)
            for ct in range(4):
                mm8(ps[B + ct][:, 0:n], psk[B + ct], s, ct * 128, lambda kc: xnT[:, kc, c0:c0 + n], [f"xnT{tb}"])
            (rs, krs), (ks, kks), (vs, kvs) = TMp(0, par), TMp(1, par), TMp(2, par)
            (t3, k3), (t4, k4), (t5, k5), (t6, k6), (t7, k7), (t8, k8), (t9, k9) = [TMp(i, par) for i in range(3, 10)]
            P.act(ACTF(t9[:, N], ps[B + 3][:, N], AF.Tanh, scale=0.5), r=[psk[B + 3]], w=[k9])
            P.dve(STT(t9[:, N], t9[:, N], 1.0, ps[B + 3][:, N], ALU.add, ALU.mult), r=[k9, psk[B + 3]], w=[k9])
            shift_lerp(ps[B + 0], psk[B + 0], n, 0, j, rs, krs, first, last, par)
            shift_lerp(ps[B + 1], psk[B + 1], n, 1, 8 + j, ks, kks, first, last, par)
            shift_lerp(ps[B + 2], psk[B + 2], n, 2, 16 + j, vs, kvs, first, last, par)
            P.pe(MM(ps[B + 0][:, 0:n], loraW[0:64, j * 128:(j + 1) * 128], wdad[0:64, c0:c0 + n], True, True),
                 r=["loraW", "wdad"], w=[psk[B + 0]])
            P.pe(MM(ps[B + 1][:, 0:n], loraW[64:128, j * 128:(j + 1) * 128], wdad[64:128, c0:c0 + n], True, True),
                 r=["loraW", "wdad"], w=[psk[B + 1]])
            P.pool(CP(VTp[:, N], vs[:, N]), r=[kvs], w=[kVT])
            P.act(ACTF(t3[:, N], ps[B + 0][:, N], AF.Tanh, bias=hv[:, j:j + 1], scale=0.5), r=[psk[B + 0], "hv"], w=[k3])
            P.act(ACTF(t4[:, N], ps[B + 1][:, N], AF.Tanh, bias=hv[:, 8 + j:9 + j], scale=0.5), r=[psk[B + 1], "hv"], w=[k4])
            P.pool(TS(t3[:, N], t3[:, N], 0.5, 0.5, ALU.mult, ALU.add), r=[k3], w=[k3])
            P.pool(TS(t4[:, N], t4[:, N], 0.5, 0.5, ALU.mult, ALU.add), r=[k4], w=[k4])
            mask, mkey = (maskP, "maskP") if n == 256 else (maskS, "maskS")
            P.dve(lambda e: e.tensor_tensor_scan(out=t5[:, N], data0=mask[:, N], data1=t3[:, N], initial=0.0,
                                                 op0=ALU.mult, op1=ALU.add), r=[mkey, k3], w=[k5])
            P.dve(TT(t3[:, N], t5[:, N], t3[:, N], ALU.subtract), r=[k5, k3], w=[k3])
            P.act(ACTF(t6[:, N], t5[:, N], AF.Exp, scale=-C0), r=[k5], w=[k6])
            P.act(ACTF(t5[:, N], t5[:, N], AF.Exp, scale=C0), r=[k5], w=[k5])
            P.act(ACTF(t3[:, N], t3[:, N], AF.Exp, scale=-C0), r=[k3], w=[k3])
            if n == 256:
                P.pool(CP(pc2[:, 2 * par:2 * par + 2].unsqueeze(2), t6[:, N].rearrange("p (c t) -> p c t", t=128)[:, :, 127:128]), r=[k6], w=[f"pc{par}"])
            else:
                P.pool(CP(pc[:, 0:16].unsqueeze(2), t6[:, N].rearrange("p (c t) -> p c t", t=NT)[:, :, NT - 1:NT]), r=[k6], w=["pc"])
            P.pool(TT(ARp[:, 1, N], rs[:, N], t6[:, N], ALU.mult), r=[krs, k6], w=[kAR])
            P.dve(TS(t6[:, N], ks[:, N], vcol(V_KK, j), None, ALU.mult), r=[kks, "vecs"], w=[k6])
            t7b = t7[:, 0:128].bitcast(BF16)
            P.dve(TT(t7b[:, N], t6[:, N], t6[:, N], ALU.mult), r=[k6], w=[k7])
            P.pe(MM(ps[B + 2][:, N], blkb[:], t7b[:, N], True, True), r=["blkb", k7], w=[psk[B + 2]])
            P.act(ACTF(t7[:, N], ps[B + 2][:, N], AF.Ln, bias=1e-12), r=[psk[B + 2]], w=[k7])
            P.act(ACTF(t7[:, N], t7[:, N], AF.Exp, scale=-0.5), r=[k7], w=[k7])
            P.dve(TT(t6[:, N], t6[:, N], t7[:, N], ALU.mult), r=[k6, k7], w=[k6])
            P.dve(STT(ARp[:, 0, N], t6[:, N], -1.0, t3[:, N], ALU.mult, ALU.mult), r=[k6, k3], w=[kAR])
            P.pool(TT(t3[:, N], t6[:, N], t4[:, N], ALU.mult), r=[k6, k4], w=[k3])
            P.pool(TT(BTp[:, N], t3[:, N], t5[:, N], ALU.mult), r=[k3, k5], w=[kBT])
            P.dve(TS(t7[:, N], t4[:, N], -1.0, vcol(V_KA, j), ALU.add, ALU.mult), r=[k4, "vecs"], w=[k7])
            P.dve(STT(t7[:, N], t7[:, N], 1.0, ks[:, N], ALU.add, ALU.mult), r=[k7, kks], w=[k7])
            P.pool(TT(KTp[:, N], t7[:, N], t5[:, N], ALU.mult), r=[k7, k5], w=[kKT])
            t3b = t3[:, 0:128].bitcast(BF16)
            P.dve(STT(t3b[:, N], rs[:, N], vcol(V_RK, j), t7[:, N], ALU.mult, ALU.mult), r=[krs, k7, "vecs", k3], w=[k3])
            P.pe(MM(ps[B + 3][:, N], blkb[:], t3b[:, N], True, True), r=["blkb", k3], w=[psk[B + 3]])
            P.dve(STT(t8[:, N], ps[B + 3][:, N], 0.5, vs[:, N], ALU.mult, ALU.mult), r=[psk[B + 3], kvs], w=[k8])

        def post(j, tb, c0, n, par):
            P.marks.append((f"post{j}_{tb}", len(P.ops)))
            B = 4 * par
            N = slice(0, n)
            (ot, ko), (t8, k8), (t9, k9) = TMp(10, par), TMp(8, par), TMp(9, par)
            (a0, ka0), (a1, ka1), (a2, ka2) = TMp(0, par), TMp(1, par), TMp(2, par)
            ob = a0[:, 0:128].bitcast(BF16); o2b = a1[:, 0:128].bitcast(BF16)
            P.pool(CP(ob[:, N], ot[:, N]), r=[ko], w=[ka0])
            P.dve(TT(o2b[:, N], ot[:, N], ot[:, N], ALU.mult), r=[ko], w=[ka1])
            P.pe(MM(ps[B + 2][:, N], blkb[:], ob[:, N], True, True), r=["blkb", ka0], w=[psk[B + 2]])
            P.pe(MM(ps[B + 3][:, N], blkb[:], o2b[:, N], True, True), r=["blkb", ka1], w=[psk[B + 3]])
            P.act(MULA(a2[:, N], ps[B + 2][:, N], 1.0 / 64), r=[psk[B + 2]], w=[ka2])
            P.dve(TT(a0[:, N], a2[:, N], a2[:, N], ALU.mult), r=[ka2], w=[ka0])
            P.dve(STT(a1[:, N], ps[B + 3][:, N], 1.0 / 64, a0[:, N], ALU.mult, ALU.subtract), r=[psk[B + 3], ka0], w=[ka1])
            P.act(ACTF(a1[:, N], a1[:, N], AF.Ln, bias=GN_EPS), r=[ka1], w=[ka1])
            P.act(ACTF(a1[:, N], a1[:, N], AF.Exp, scale=-0.5), r=[ka1], w=[ka1])
            P.dve(TT(a0[:, N], ot[:, N], a2[:, N], ALU.subtract), r=[ko, ka2], w=[ka0])
            P.dve(TT(a0[:, N], a0[:, N], a1[:, N], ALU.mult), r=[ka0, ka1], w=[ka0])
            P.dve(TS(a0[:, N], a0[:, N], hv[:, 16 + j:17 + j], hv[:, 24 + j:25 + j], ALU.mult, ALU.add), r=[ka0, "hv"], w=[ka0])
            P.pool(TT(a0[:, N], a0[:, N], t8[:, N], ALU.add), r=[ka0, k8], w=[ka0])
            P.pool(TT(yT[:, j, c0:c0 + n], a0[:, N], t9[:, N], ALU.mult), r=[ka0, k9], w=[f"yT{tb}"])

        DONORS = [(memnT[:].rearrange("p a b -> p (a b)"), 2048), (KTb[:].rearrange("p a b -> p (a b)"), 2048),
                  (qTs[:].rearrange("p a b -> p (a b)"), 1024), (U[:, 0:1024].bitcast(BF16), 2048),
                  (xt[:, :].bitcast(BF16), 2048), (xnb[:, :], 1024), (big16[:, :], 2688 + 1024)]
        DONOR_KEYS = ["memnT", "KTb", "qTs", "U", "xt", "xnb", "Kb", "Vb", "Kb0", "Kb1", "Vb0", "Vb1", "sgms"]
        dst_ = {"d": 0, "o": 0}
        def dalloc(n):
            while DONORS[dst_["d"]][1] - dst_["o"] < n:
                dst_["d"] += 1; dst_["o"] = 0
            a = DONORS[dst_["d"]][0][:, dst_["o"]:dst_["o"] + n]; dst_["o"] += n
            return a
        NU = 8
        uS = [dalloc(512) for _ in range(NU)]
        uSB = [x_[:, 0:256] for x_ in uS]; uSK = [x_[:, 256:512] for x_ in uS]
        m512P = sb("m512P", (128, 512), BF16); m512S = sb("m512S", (128, 512), BF16)
        uXY = [dalloc(256).rearrange("p (a t) -> p a t", a=2) for _ in range(NU)]
        uZ = [dalloc(128) for _ in range(NU)]; uLV = [dalloc(64) for _ in range(NU)]
        cTok = [dalloc(512).rearrange("p (q t) -> p q t", q=4) for _ in range(4)]
        cWT = [dalloc(128) for _ in range(4)]; cU0 = [dalloc(256).bitcast(F32) for _ in range(4)]
        UKEYS = ([f"uSB{u}" for u in range(NU)] + [f"uSK{u}" for u in range(NU)] + [f"uXY{u}" for u in range(NU)] +
                 [f"uZ{u}" for u in range(NU)] + [f"uLV{u}" for u in range(NU)] + [f"cTok{c}" for c in range(4)] +
                 [f"cWT{c}" for c in range(4)] + [f"cU0{c}" for c in range(4)])

        def scan_block(j, nch, sample, ot, ko, par):
            NL = 2 if sample else 6
            mSI = consts[:, C_BD_S:C_BD_S + 256] if sample else consts[:, C_TRIL_S:C_TRIL_S + 256]
            mLT = consts[:, C_BD_LT:C_BD_LT + 128] if sample else consts[:, C_LT:C_LT + 128]
            units = [(c, hh) for c in range(nch) for hh in range(2)]
            B = 4 * par
            UB = 4 * par
            CB = 2 * par
            AR_ = AR[:, :, par * 256:(par + 1) * 256]; kAR = f"AR{par}"
            BT_ = BTt[:, par * 256:(par + 1) * 256]; kBT = f"BTt{par}"
            KT_ = KTt[:, par * 256:(par + 1) * 256]; kKT = f"KTt{par}"
            VT_ = VTt[:, par * 256:(par + 1) * 256]; kVT = f"VTt{par}"
            CS = lambda c: slice(c * 128, (c + 1) * 128)
            SL = lambda hh: slice(64 * hh, 64 * hh + 64)
            for c in range(nch):
                pT = ps[B + c][:, :].bitcast(BF16)
                for q, (src, skey) in enumerate(((AR_[:, 0, CS(c)], kAR), (BT_[:, CS(c)], kBT), (KT_[:, CS(c)], kKT), (VT_[:, CS(c)], kVT))):
                    P.pe(TR(pT[:, q * 128:(q + 1) * 128], src, identb[:]), r=[skey, "identb"], w=[psk[B + c]])
            for c in range(nch):
                pT = ps[B + c][:, :].bitcast(BF16)
                P.act(CPA(cTok[CB + c][:].rearrange("p q t -> p (q t)"), pT[:, 0:512]), r=[psk[B + c]], w=[f"cTok{CB + c}"])
            P.marks.append((f"S2_{j}", len(P.ops)))
            for u, (c, hh) in enumerate(units):
                sl = SL(hh)
                P.pe(MM(ps[B + u][:, 0:256], BT_[sl, CS(c)], AR_[sl, :, CS(c)], True, True), r=[kBT, kAR], w=[psk[B + u]])
                P.pe(MM(ps[B + u][:, 256:512], KT_[sl, CS(c)], AR_[sl, :, CS(c)], True, True), r=[kKT, kAR], w=[psk[B + u]])
            for u, (c, hh) in enumerate(units):
                P.dve(TT(uS[UB + u][:, :], ps[B + u][:, 0:512], (m512S if sample else m512P)[:, :], ALU.mult),
                      r=[psk[B + u], "m512"], w=[f"uSB{UB + u}", f"uSK{UB + u}"])
            for u, (c, hh) in enumerate(units):
                sl = SL(hh)
                P.pe(MM(ps[B + u][:, 0:128], AR_[sl, 0, CS(c)], BT_[sl, CS(c)], True, True), r=[kBT, kAR], w=[psk[B + u]])
                P.pe(MM(ps[B + u][:, 384:448], uSK[UB + u][:, 0:128], cTok[CB + c][:, 3, sl], True, True), r=[f"uSK{UB + u}", f"cTok{CB + c}"], w=[psk[B + u]])
            for u, (c, hh) in enumerate(units):
                P.dve(TT(uXY[UB + u][:, 1, :], ps[B + u][:, 0:128], mLT, ALU.mult), r=[psk[B + u], "consts"], w=[f"uXY{UB + u}"])
                P.act(CPA(uLV[UB + u][:, :], ps[B + u][:, 384:448]), r=[psk[B + u]], w=[f"uLV{UB + u}"])
                P.pool(TT(uZ[UB + u][:, :], uSB[UB + u][:, 0:128], identb[:], ALU.add), r=[f"uSB{UB + u}", "identb"], w=[f"uZ{UB + u}"])
            P.marks.append((f"S3_{j}", len(P.ops)))
            nU = len(units)
            for n_ in range(1, NL + 1):
                last = (n_ == NL)
                for u in range(nU):
                    U_ = UB + u
                    Yp = uSB[U_][:, 0:128] if n_ == 1 else uXY[U_][:, 0, :]
                    ykey = f"uSB{U_}" if n_ == 1 else f"uXY{U_}"
                    Xp = uXY[U_][:, 1, :]
                    if not last:
                        P.pe(MM(ps[B + u][:, 0:128], Xp, Yp, True, True), r=[f"uXY{U_}", ykey], w=[psk[B + u]])
                    P.pe(MM(ps[B + u][:, 128:256], Yp, Xp, True, True), r=[f"uXY{U_}", ykey], w=[psk[B + u]])
                for u in range(nU):
                    U_ = UB + u
                    if not last:
                        P.act(CPA(uXY[U_][:].rearrange("p a t -> p (a t)"), ps[B + u][:, 0:256]), r=[psk[B + u]], w=[f"uXY{U_}"])
                    else:
                        P.act(CPA(uXY[U_][:, 1, :], ps[B + u][:, 128:256]), r=[psk[B + u]], w=[f"uXY{U_}"])
                for u in range(nU):
                    U_ = UB + u
                    P.pe(MM(ps[B + u][:, 256:384], uXY[U_][:, 1, :], uZ[U_][:, :], True, True), r=[f"uXY{U_}", f"uZ{U_}"], w=[psk[B + u]])
                for u in range(nU):
                    U_ = UB + u
                    P.dve(TT(uZ[U_][:, :], ps[B + u][:, 256:384], uZ[U_][:, :], ALU.add), r=[psk[B + u], f"uZ{U_}"], w=[f"uZ{U_}"])
            P.marks.append((f"S4_{j}", len(P.ops)))
            for u, (c, hh) in enumerate(units):
                sl = SL(hh)
                P.pe(MM(ps[B + 2 * c][sl, 0:128], cTok[CB + c][:, 0, sl], uZ[UB + u][:, :], True, True), r=[f"cTok{CB + c}", f"uZ{UB + u}"], w=[psk[B + 2 * c]])
                if not sample:
                    P.pe(MM(ps[B + 2 * c + 1][:, 64 * hh:64 * hh + 64], uZ[UB + u][:, :], uLV[UB + u][:, :], True, True), r=[f"uZ{UB + u}", f"uLV{UB + u}"], w=[psk[B + 2 * c + 1]])
                else:
                    P.pe(MM(ps[B + 2 * c + 1][sl, 0:128], uLV[UB + u][:, :], uZ[UB + u][:, :], True, True), r=[f"uZ{UB + u}", f"uLV{UB + u}"], w=[psk[B + 2 * c + 1]])
            for c in range(nch):
                P.act(CPA(cWT[CB + c][:, :], ps[B + 2 * c][:, 0:128]), r=[psk[B + 2 * c]], w=[f"cWT{CB + c}"])
                P.dve(CP(cU0[CB + c][:, :], ps[B + 2 * c + 1][:, 0:128]), r=[psk[B + 2 * c + 1]], w=[f"cU0{CB + c}"])
            P.marks.append((f"S5_{j}", len(P.ops)))
            for c in range(nch):
                tk, tkk = cTok[CB + c], f"cTok{CB + c}"
                bU, kU = ps[B + 0], psk[B + 0]
                if not sample:
                    for hh in range(2):
                        P.pe(MM(bU[:, 64 * hh:64 * hh + 64], cWT[CB + c][SL(hh), :], M0b[SL(hh), :], True, True), r=[f"cWT{CB + c}", "M0b"], w=[kU], force=(hh == 1))
                    P.dve(TT(Utok[:, :], bU[:, 0:128], cU0[CB + c][:, :], ALU.add), r=[kU, f"cU0{CB + c}"], w=["Utok"])
                    ut, utk = Utok, "Utok"
                else:
                    for hh in range(2):
                        for b in range(NB):
                            P.pe(MM(bU[SL(hh), 8 * b:8 * b + 8], Msb[SL(hh), b, :], cWT[CB + c][SL(hh), 8 * b:8 * b + 8], True, True),
                                 r=["Msb", f"cWT{CB + c}"], w=[kU], force=(hh == 1 and b == 0))
                    P.dve(TT(Utok[:, :], bU[:, 0:128], cU0[CB + c][:, :], ALU.add), r=[kU, f"cU0{CB + c}"], w=["Utok"])
                    pT = ps[B + 3][:, :].bitcast(BF16)
                    P.pe(TR(pT[:, 0:128], Utok[:, :], identb[:]), r=["Utok", "identb"], w=[psk[B + 3]])
                    P.act(CPA(Utok2[:, :], pT[:, 0:128]), r=[psk[B + 3]], w=["Utok2"])
                    ut, utk = Utok2, "Utok2"
                bO, kO = ps[B + 1], psk[B + 1]
                for hh in range(2):
                    u = UB + 2 * c + hh; sl = SL(hh)
                    out = bO[sl, 0:128]
                    if not sample:
                        P.pe(MM(out, M0b[sl, :], AR_[sl, 1, CS(c)], True, False), r=["M0b", kAR], w=[kO], force=(hh == 1))
                    else:
                        for b in range(NB):
                            P.pe(MM(bO[sl, 8 * b:8 * b + 8], Msb[sl, b, :], AR_[sl, 1, 8 * b:8 * b + 8], b == 0, False),
                                 r=["Msb", kAR], w=[kO], force=(hh == 1 and b == 0))
                    P.pe(MM(out, tk[:, 3, sl], uSK[u][:, 128:256], False, False), r=[tkk, f"uSK{u}"], w=[kO])
                    P.pe(MM(out, ut[:, sl], uSB[u][:, 128:256], False, True), r=[utk, f"uSB{u}"], w=[kO])
                if not sample:
                    bM, kM = ps[B + 2], psk[B + 2]
                    for hh in range(2):
                        sl = SL(hh)
                        out = bM[sl, 128:192]
                        P.pe(MM(out, tk[:, 2, sl], tk[:, 3, sl], True, False), r=[tkk], w=[kM])
                        P.pe(MM(out, tk[:, 1, sl], ut[:, sl], False, True), r=[tkk, utk], w=[kM])
                    P.dve(TT(Mtmp[:, :], bM[:, 128:192], Mst[:, j, :], ALU.add), r=[kM, "Mst"], w=["Mtmp"])
                    P.dve(TS(M0b[:, :], Mtmp[:, :], pc2[:, 2 * par + c:2 * par + c + 1], None, ALU.mult), r=["Mtmp", f"pc{par}"], w=["M0b"])
                    P.pool(TS(Mst[:, j, :], Mtmp[:, :], pc2[:, 2 * par + c:2 * par + c + 1], None, ALU.mult), r=["Mtmp", f"pc{par}"], w=["Mst"])
                else:
                    mm3 = Mmask.unsqueeze(2).to_broadcast([128, NB, 128])
                    P.dve(TT(Ubm[:], ut[:, :].unsqueeze(1).to_broadcast([128, NB, 128]), mm3, ALU.mult), r=[utk, "consts"], w=["Kb"])
                    P.pool(TT(Vbm[:], tk[:, 3, :].unsqueeze(1).to_broadcast([128, NB, 128]), mm3, ALU.mult), r=[tkk, "consts"], w=["Vb"])
                    for half in range(2):
                        bank, bk = ps[6 + half], psk[6 + half]
                        bs = slice(8 * half, 8 * half + 8)
                        for hh in range(2):
                            sl = SL(hh)
                            out = bank[sl, :].rearrange("p (b v) -> p b v", v=64)
                            P.pe(MM(out, tk[:, 2, sl], Vbm[:, bs, sl], True, False), r=[tkk, "Vb"], w=[bk])
                            P.pe(MM(out, tk[:, 1, sl], Ubm[:, bs, sl], False, True), r=[tkk, "Kb"], w=[bk])
                        P.dve(TT(Mtmp2[:], bank[:, :].rearrange("p (b v) -> p b v", v=64), Ms[:, bs, :], ALU.add), r=[bk, "Ms"], w=["sgms"])
                        P.dve(TT(Ms[:, bs, :], Mtmp2[:], pc[:, bs].unsqueeze(2).to_broadcast([128, 8, 64]), ALU.mult),
                              r=["sgms", "pc"], w=["Ms"])
                P.act(CPA(ot[:, CS(c)], bO[:, 0:128]), r=[kO], w=[ko])

        ACCKEYS = [f"acc{j}_{tb}" for j in range(8) for tb in range(3)]

        def load_sample_state(j):
            for h in range(2):
                P.dma("sp", Sin[0:64, :, h * 64:(h + 1) * 64], swkv_d[:, 2 * j + h].rearrange("b v k -> v b k"),
                      w=["Sin"], stream=f"Sin{h}")
            for half in range(2):
                bank, bk = ps[5 + half], psk[5 + half]
                for bb in range(8):
                    b = half * 8 + bb
                    P.pe(TR(bank[:, bb * 64:(bb + 1) * 64], Sin[0:64, b, :], ident[0:64, 0:64]), r=["Sin", "consts"], w=[bk])
                bs = slice(8 * half, 8 * half + 8)
                P.dve(CP(Ms[:, bs, :], bank[:, :].rearrange("p (b v) -> p b v", v=64)), r=[bk], w=["Ms"])
                P.act(CPA(Msb[:, bs, :], bank[:, :].rearrange("p (b v) -> p b v", v=64)), r=[bk], w=["Msb"])

        def store_sample_state(j):
            for q in range(4):
                bank, bk = ps[5 + q % 2], psk[5 + q % 2]
                for bb in range(4):
                    b = q * 4 + bb
                    P.pe(TR(bank[0:64, bb * 128:(bb + 1) * 128], Ms[:, b, :], ident), r=["Ms", "consts"], w=[bk])
                P.dve(CP(Sin[0:64, 4 * q:4 * q + 4, :], bank[0:64, :].rearrange("p (b c) -> p b c", c=128)), r=[bk], w=["Sin"])
            for h in range(2):
                P.dma("sp", wkvs_d[:, 2 * j + h].rearrange("b v k -> v b k"), Sin[0:64, :, h * 64:(h + 1) * 64],
                      r=["Sin"], w=[f"o_wkvs{j}_{h}"], stream=f"o_Sin{h}")

        def rwkv_phase(p_, tblk):
            P.dve(MSET(dummy[:, 0:1], 0.0), w=ACCKEYS + RKEYS + DONOR_KEYS + UKEYS + TKEYS)
            if p_ == 0:
                rwkv_setup()
                P.dve(MSET(dummy[:, 2:3], 0.0), w=TKEYS)
            RCUT = int(os.environ.get('RCUT', '99'))
            if RCUT >= 1:
                lora_part(p_, tblk)
            npb = sum(1 for (_, n) in tblk if n == 512)
            for j in range(8 if RCUT >= 2 else 0):
                if RCUT < 9 and j > 0:
                    break
                s = w_acquire(G_RWKV + j)
                P.act(CPA(M0b[:, :], Mst[:, j, :]), r=["Mst"], w=["M0b"])
                rb = rblocks(tblk)
                nprb = sum(1 for (_, n) in rb if n == 256)
                for bi_, (c0, n) in enumerate(rb):
                    sample = (n == 128)
                    par = bi_ % 2; tb = c0 // 512
                    if sample:
                        load_sample_state(j)
                    prep(j, tb, c0, n, s, bi_ == 0, bi_ == nprb - 1, par)
                    ot, ko = TMp(10, par)
                    if not sample:
                        scan_block(j, 2, False, ot, ko, par)
                    else:
                        scan_block(j, 1, True, ot, ko, par)
                        store_sample_state(j)
                    post(j, tb, c0, n, par)
            P.dve(MSET(dummy[:, 3:4], 0.0), w=TKEYS)
            if p_ == 0:
                for half, (a, b_) in enumerate(((0, 13), (13, 25))):
                    for ti in range(a, b_):
                        q, o = divmod((ti - a) * 128, 512)
                        P.pe(TR(ps[1 + q][0:NB, o:o + 128], pls[:, ti, :], ident), r=["pls", "consts"], w=[psk[1 + q]])
                    for q in range(4):
                        lo = a * 128 + q * 512; hi = min(lo + 512, b_ * 128)
                        if hi > lo:
                            t, tk_ = tmp()
                            P.dve(CP(t[0:NB, 0:hi - lo], ps[1 + q][0:NB, 0:hi - lo]), r=[psk[1 + q]], w=[tk_])
                            P.dma("sp", shifts_d[:, lo:hi], t[0:NB, 0:hi - lo], r=[tk_], w=[f"o_shs{lo}"], stream="o_" + tk_)
            else:
                P.pe(TR(ps[1][0:25, 0:128], pcarry[:, :], ident), r=["pcarry", "consts"], w=[psk[1]])
                t, tk_ = tmp()
                P.dve(CP(t[0:25, 0:128], ps[1][0:25, 0:128]), r=[psk[1]], w=[tk_])
                P.dma("sp", shiftp_d[0].rearrange("(c p) -> c p", p=128), t[0:25, 0:128], r=[tk_], w=["o_shp"], stream="o_" + tk_)
                for half in range(2):
                    bank, bk = ps[2 + half], psk[2 + half]
                    t, tk_ = tmp()
                    for jj in range(4):
                        P.pe(TR(bank[0:64, jj * 128:(jj + 1) * 128], Mst[:, half * 4 + jj, :], ident), r=["Mst", "consts"], w=[bk])
                    P.dve(CP(t[0:64, :], bank[0:64, :]), r=[bk], w=[tk_])
                    dst = wkvp_d[8 * half:8 * half + 8].rearrange("h v k -> v h k")
                    P.dma("sp", dst, t[0:64, :].rearrange("p (h k) -> p h k", k=64), r=[tk_], w=[f"o_wkvp{half}"], stream="o_" + tk_)
            P.dve(MSET(dummy[:, 1:2], 0.0), w=ACCKEYS + RKEYS + DONOR_KEYS + UKEYS + TKEYS)


        env = dict(locals())
        for p_ in range((2 if 'onepass' not in stages else 1) if LVL >= 4 else 0):
            tblk = PASS_TBLK[p_]
            for li, gi in enumerate(PASS_TILES[p_]):
                norm_T(xall_d[gi], V_NORM_IN, xnT[:, :, li * 128:(li + 1) * 128], [f"xnT{li // 4}"])
            nbr = 0
            if 'cutA' in stages:
                break
            if "rwkv" in stages:
                rwkv_phase(p_, tblk)
                branch(1, nbr == 0, tblk); nbr += 1
            if "conv" in stages:
                conv_phase(p_, tblk)
                branch(0, nbr == 0, tblk); nbr += 1
            if "mem" in stages:
                mem_phase(p_, tblk)
                branch(2, nbr == 0, tblk); nbr += 1
            if 'noout' not in stages:
                out_phase(p_, tblk)
        okeys = [k for k in P.lastw if k.startswith("o_")]
        P.add("sp", lambda e: e.nop(), r=okeys)
        n_ops = P.emit(es)
        print("ops", n_ops, "est_us", getattr(P, "est_us", None), "tsw", getattr(P, "n_tsw", 0), "waits", getattr(P, "n_wait", 0), "attached", getattr(P, "n_att", 0))
    return nc


def rwkv_phase(env, p_, tblk):
    raise NotImplementedError


_NC_CACHE = {}


def kernel(**inputs):
    stages = inputs.pop("_stages", ("rwkv", "conv", "mem"))
    maps = _host_inputs(inputs)
    key = tuple(stages)
    if key not in _NC_CACHE:
        _NC_CACHE[key] = build_nc(stages)
    nc = _NC_CACHE[key]
    res = run_bass_kernel_spmd(nc, maps, core_ids=list(range(NCORES)))
    R = res.results
    B = NCORES
    y_prompt = np.stack([R[c]["y"][0:16].reshape(T, D) for c in range(B)])
    y_sample = np.concatenate([R[c]["y"][16].reshape(NB, NT, D) for c in range(B)], axis=0)
    mk = np.stack([R[c]["mk"].reshape(256, 4, 256) for c in range(B)])[None]
    mv = np.stack([R[c]["mv"].reshape(256, 4, 256) for c in range(B)])[None]
    conv_p = np.stack([R[c]["convp"] for c in range(B)])[None]
    shift_p = np.stack([R[c]["shiftp"][0] for c in range(B)])[None]
    wkv_p = np.stack([R[c]["wkvp"] for c in range(B)])[None]
    conv_s = np.concatenate([R[c]["convs"].reshape(NB, 2, D) for c in range(B)], axis=0)[None]
    shift_s = np.concatenate([R[c]["shifts"] for c in range(B)], axis=0)[None]
    wkv_s = np.concatenate([R[c]["wkvs"] for c in range(B)], axis=0)[None]
    f = lambda a: np.ascontiguousarray(a, dtype=np.float32)
    return tuple(f(a) for a in (y_prompt, y_sample, mk, mv, conv_p, shift_p, wkv_p, conv_s, shift_s, wkv_s))
```
